# Optimizing a Trainium2 kernel written in Bass

```python
import jax, jax.numpy as jnp
from jax import lax
import numpy as np

D_MODEL = 1024
BATCH = 16
SEQ = 2048
DEPTH = 2
DEC_BATCH = 32
DEC_SEQ = 32
PAST_LEN = 2048

CHUNK = 64
MIX_WIDTH = D_MODEL
POOL_WIDTH = MIX_WIDTH // 2
CONV_WIDTH = MIX_WIDTH - POOL_WIDTH
POOL_WINDOWS = (2, 4, 8, 16)
N_POOL_GROUPS = len(POOL_WINDOWS)
POOL_GROUP = POOL_WIDTH // N_POOL_GROUPS
POOL_HIST = max(POOL_WINDOWS) - 1
CONV_K = 3
IN_WIDTH = POOL_WIDTH + 3 * CONV_WIDTH
N_MEM = 256
N_MEM_HEADS = 4
MEM_HEAD_DIM = D_MODEL // N_MEM_HEADS
D_FF = 2816
FFN_CONV_K = 3
EPS = 1e-6

kernel_name = "hybrid_pool_shortconv_stream_step"


def rms_norm(x, g):
    x32 = x.astype(jnp.float32)
    y = x32 * lax.rsqrt(jnp.mean(x32 * x32, axis=-1, keepdims=True) + EPS)
    return (y * g.astype(jnp.float32)).astype(x.dtype)


def causal_dwconv3(z, hist, w, b):
    t = z.shape[1]
    ext = jnp.concatenate([hist.astype(z.dtype), z], axis=1)
    y = ext[:, 0:t] * w[0] + ext[:, 1:t + 1] * w[1] + ext[:, 2:t + 2] * w[2] + b
    return y, ext[:, -2:]


def multiscale_pool(u, hist, start_pos, w_pool, scale):
    b, t, c = u.shape
    ext = jnp.concatenate([hist.astype(u.dtype), u], axis=1)
    cs = jnp.cumsum(ext.astype(jnp.float32), axis=1)
    cs = jnp.pad(cs, ((0, 0), (1, 0), (0, 0)))
    pos = start_pos + jnp.arange(t)
    p1 = POOL_HIST + 1
    means = []
    for g, w in enumerate(POOL_WINDOWS):
        lo, hi = g * POOL_GROUP, (g + 1) * POOL_GROUP
        s = cs[:, p1:p1 + t, lo:hi] - cs[:, p1 - w:p1 - w + t, lo:hi]
        cnt = jnp.minimum(w, pos + 1).astype(jnp.float32)[None, :, None]
        means.append(s / cnt)
    mean = jnp.concatenate(means, axis=-1)
    d = (mean - u.astype(jnp.float32)).astype(u.dtype)
    d = d.reshape(b, t, N_POOL_GROUPS, POOL_GROUP)
    y = jnp.einsum('btgc,gcd->btgd', d, w_pool).reshape(b, t, c) * scale
    return y, ext[:, -POOL_HIST:]


def memory_kv(mem, g_mem, w_k, w_v):
    m = rms_norm(mem, g_mem)
    k = jnp.einsum('bmd,dhe->bmhe', m, w_k)
    v = jnp.einsum('bmd,dhe->bmhe', m, w_v)
    return k, v


def memory_attention(h, k, v, w_q, w_o):
    q = jnp.einsum('btd,dhe->bthe', h, w_q)
    s = jnp.einsum('bthe,bmhe->bhtm', q, k).astype(jnp.float32) * (MEM_HEAD_DIM ** -0.5)
    p = jax.nn.softmax(s, axis=-1).astype(v.dtype)
    o = jnp.einsum('bhtm,bmhe->bthe', p, v)
    return jnp.einsum('bthe,hed->btd', o, w_o)


def layer_step(x, mem_k, mem_v, pool_hist, conv_hist, ffn_hist, start_pos, lw):
    h = rms_norm(x, lw['g_mix_pre'])
    proj = h @ lw['w_in']
    u_a = proj[..., :POOL_WIDTH]
    b_gate, c_gate, val = jnp.split(proj[..., POOL_WIDTH:], 3, axis=-1)
    y_a, pool_new = multiscale_pool(u_a, pool_hist, start_pos, lw['w_pool'], lw['pool_scale'])
    zc, conv_new = causal_dwconv3(c_gate * val, conv_hist, lw['conv_w'], lw['conv_b'])
    y_b = b_gate * zc
    y = jnp.concatenate([y_a, y_b], axis=-1) @ lw['w_out']
    x = x + rms_norm(y, lw['g_mix_post'])
    h = rms_norm(x, lw['g_attn_pre'])
    y = memory_attention(h, mem_k, mem_v, lw['w_q'], lw['w_o'])
    x = x + rms_norm(y, lw['g_attn_post'])
    h = rms_norm(x, lw['g_ffn_pre'])
    up = h @ lw['w_up']
    upc, ffn_new = causal_dwconv3(up, ffn_hist, lw['ffn_conv_w'], lw['ffn_conv_b'])
    gate, value = jnp.split(upc, 2, axis=-1)
    y = (jax.nn.silu(gate) * value) @ lw['w_down']
    x = x + rms_norm(y, lw['g_ffn_post'])
    return x, pool_new, conv_new, ffn_new


def setup_inputs(seed: int = 0) -> dict:
    key = jax.random.key(seed)
    ks = jax.random.split(key, 32)
    f32 = jnp.float32
    L = DEPTH

    def nrm(k, shape, scale):
        return jax.random.normal(k, shape, f32) * scale

    def gain(k, shape):
        return 1.0 + 0.05 * jax.random.normal(k, shape, f32)

    return {
        'x_prompt': nrm(ks[0], (BATCH, SEQ, D_MODEL), 1.0),
        'x_sample': nrm(ks[1], (DEC_BATCH, DEC_SEQ, D_MODEL), 1.0),
        'mem_prompt': nrm(ks[2], (BATCH, N_MEM, D_MODEL), 1.0),
        'cache_mem_k': nrm(ks[3], (L, DEC_BATCH, N_MEM, N_MEM_HEADS, MEM_HEAD_DIM), 1.0),
        'cache_mem_v': nrm(ks[4], (L, DEC_BATCH, N_MEM, N_MEM_HEADS, MEM_HEAD_DIM), 1.0),
        'state_pool': nrm(ks[5], (L, DEC_BATCH, POOL_HIST, POOL_WIDTH), 1.0),
        'state_conv': nrm(ks[6], (L, DEC_BATCH, CONV_K - 1, CONV_WIDTH), 1.0),
        'state_ffn_conv': nrm(ks[7], (L, DEC_BATCH, FFN_CONV_K - 1, 2 * D_FF), 1.0),
        'g_mix_pre': gain(ks[8], (L, D_MODEL)),
        'g_mix_post': gain(ks[9], (L, D_MODEL)),
        'w_in': nrm(ks[10], (L, D_MODEL, IN_WIDTH), D_MODEL ** -0.5),
        'w_pool': nrm(ks[11], (L, N_POOL_GROUPS, POOL_GROUP, POOL_GROUP), POOL_GROUP ** -0.5),
        'pool_scale': gain(ks[12], (L, POOL_WIDTH)),
        'conv_w': nrm(ks[13], (L, CONV_K, CONV_WIDTH), CONV_K ** -0.5),
        'conv_b': nrm(ks[14], (L, CONV_WIDTH), 0.01),
        'w_out': nrm(ks[15], (L, MIX_WIDTH, D_MODEL), MIX_WIDTH ** -0.5),
        'g_attn_pre': gain(ks[16], (L, D_MODEL)),
        'g_attn_post': gain(ks[17], (L, D_MODEL)),
        'g_mem': gain(ks[18], (L, D_MODEL)),
        'w_q': nrm(ks[19], (L, D_MODEL, N_MEM_HEADS, MEM_HEAD_DIM), D_MODEL ** -0.5),
        'w_k': nrm(ks[20], (L, D_MODEL, N_MEM_HEADS, MEM_HEAD_DIM), D_MODEL ** -0.5),
        'w_v': nrm(ks[21], (L, D_MODEL, N_MEM_HEADS, MEM_HEAD_DIM), D_MODEL ** -0.5),
        'w_o': nrm(ks[22], (L, N_MEM_HEADS, MEM_HEAD_DIM, D_MODEL), D_MODEL ** -0.5),
        'g_ffn_pre': gain(ks[23], (L, D_MODEL)),
        'g_ffn_post': gain(ks[24], (L, D_MODEL)),
        'w_up': nrm(ks[25], (L, D_MODEL, 2 * D_FF), D_MODEL ** -0.5),
        'ffn_conv_w': nrm(ks[26], (L, FFN_CONV_K, 2 * D_FF), FFN_CONV_K ** -0.5),
        'ffn_conv_b': nrm(ks[27], (L, 2 * D_FF), 0.01),
        'w_down': nrm(ks[28], (L, D_FF, D_MODEL), D_FF ** -0.5),
    }


def reference(x_prompt, x_sample, mem_prompt, cache_mem_k, cache_mem_v, state_pool, state_conv,
              state_ffn_conv, g_mix_pre, g_mix_post, w_in, w_pool, pool_scale, conv_w, conv_b, w_out,
              g_attn_pre, g_attn_post, g_mem, w_q, w_k, w_v, w_o, g_ffn_pre, g_ffn_post, w_up,
              ffn_conv_w, ffn_conv_b, w_down):
    yp, ys = x_prompt, x_sample
    bp = x_prompt.shape[0]
    mk_p, mv_p, pool_p, conv_p, ffn_p = [], [], [], [], []
    pool_s, conv_s, ffn_s = [], [], []
    for l in range(DEPTH):
        lw = {
            'g_mix_pre': g_mix_pre[l], 'g_mix_post': g_mix_post[l], 'w_in': w_in[l],
            'w_pool': w_pool[l], 'pool_scale': pool_scale[l], 'conv_w': conv_w[l], 'conv_b': conv_b[l],
            'w_out': w_out[l], 'g_attn_pre': g_attn_pre[l], 'g_attn_post': g_attn_post[l],
            'w_q': w_q[l], 'w_o': w_o[l], 'g_ffn_pre': g_ffn_pre[l], 'g_ffn_post': g_ffn_post[l],
            'w_up': w_up[l], 'ffn_conv_w': ffn_conv_w[l], 'ffn_conv_b': ffn_conv_b[l], 'w_down': w_down[l],
        }
        kp, vp = memory_kv(mem_prompt, g_mem[l], w_k[l], w_v[l])
        yp, pn, cn, fn = layer_step(
            yp, kp, vp,
            jnp.zeros((bp, POOL_HIST, POOL_WIDTH), yp.dtype),
            jnp.zeros((bp, CONV_K - 1, CONV_WIDTH), yp.dtype),
            jnp.zeros((bp, FFN_CONV_K - 1, 2 * D_FF), yp.dtype),
            0, lw)
        mk_p.append(kp)
        mv_p.append(vp)
        pool_p.append(pn)
        conv_p.append(cn)
        ffn_p.append(fn)
        ys, pn, cn, fn = layer_step(ys, cache_mem_k[l], cache_mem_v[l], state_pool[l], state_conv[l],
                                    state_ffn_conv[l], PAST_LEN, lw)
        pool_s.append(pn)
        conv_s.append(cn)
        ffn_s.append(fn)
    return (yp, ys, jnp.stack(mk_p), jnp.stack(mv_p), jnp.stack(pool_p), jnp.stack(conv_p),
            jnp.stack(ffn_p), jnp.stack(pool_s), jnp.stack(conv_s), jnp.stack(ffn_s))
```

```python
import contextlib
import numpy as np
import concourse.bass as bass
import concourse.mybir as mybir
from concourse.bass_utils import run_bass_kernel_spmd

F32 = mybir.dt.float32
BF16 = mybir.dt.bfloat16
AF = mybir.ActivationFunctionType
ALU = mybir.AluOpType

ENGINES = ("pe", "act", "dve", "pool", "sp")

L = 2
D = 1024
SEQ = 2048
NMEM = 256
DFF = 2816
NCH = 8
FCH = 44
GCH = 22
NCORES = 8
PB = 2
SB = 4
SSEQ = 32
EPS = 1e-6

CL = 252
G_MIXPRE, G_MIXPOST, G_ATTNPRE, G_ATTNPOST, G_MEM, G_FFNPRE, G_FFNPOST = 0, 8, 16, 24, 32, 40, 48
POOLSC, CONVW, CONVB, FFNW, FFNB = 56, 60, 72, 76, 208
INVCNT = L * CL
EPSC = INVCNT + 15
NCONST = EPSC + 1
SEM_EPOCH = 1500


class Op:
    __slots__ = ("id", "eng", "fn", "deps", "signal", "dsem", "ev", "pos", "vc")

    def __init__(self, id, eng, fn, dsem):
        self.id = id
        self.eng = eng
        self.fn = fn
        self.deps = set()
        self.signal = False
        self.dsem = dsem
        self.ev = None
        self.pos = None
        self.vc = None


class Prog:
    def __init__(self):
        self.ops = []
        self.last_write = {}
        self.readers = {}
        self.last_dma = {}
        self.extra = ()

    def op(self, eng, fn, reads=(), writes=(), dsem=None, chain=True, noextra=False):
        o = Op(len(self.ops), eng, fn, dsem)
        if self.extra and not noextra:
            reads = list(reads) + list(self.extra)
        deps = o.deps
        lw = self.last_write
        rd = self.readers
        for k in reads:
            w = lw.get(k)
            if w is not None:
                deps.add(w)
        for k in writes:
            w = lw.get(k)
            if w is not None:
                deps.add(w)
            r = rd.get(k)
            if r:
                deps.update(r)
        for k in reads:
            rd.setdefault(k, []).append(o.id)
        for k in writes:
            lw[k] = o.id
            rd[k] = []
        if dsem is not None:
            p = self.last_dma.get(dsem)
            if p is not None and chain:
                deps.add(p)
            self.last_dma[dsem] = o.id
        deps.discard(o.id)
        self.ops.append(o)
        return o

    def emit(self, nc, final_wait_eng="sp", same_eng_skip=10 ** 9):
        ops = self.ops
        cnt = {e: 0 for e in ENGINES}
        for o in ops:
            o.pos = cnt[o.eng]
            cnt[o.eng] += 1
        for o in ops:
            if o.dsem is not None:
                continue
            drop = []
            for d in o.deps:
                od = ops[d]
                if od.eng == o.eng and od.dsem is None:
                    if o.eng == "pe" or (o.pos - od.pos) >= same_eng_skip:
                        drop.append(d)
            for d in drop:
                o.deps.discard(d)
        for o in ops:
            for d in o.deps:
                ops[d].signal = True
        ecnt = {e: 0 for e in ENGINES}
        dcnt = {}
        sem_names = set()
        for o in ops:
            if o.dsem is not None:
                dcnt[o.dsem] = dcnt.get(o.dsem, 0) + 16
                o.ev = (("d", o.dsem), dcnt[o.dsem])
                sem_names.add(("d", o.dsem))
            elif o.signal:
                k = ("e", o.eng, ecnt[o.eng] // SEM_EPOCH)
                o.ev = (k, ecnt[o.eng] % SEM_EPOCH + 1)
                ecnt[o.eng] += 1
                sem_names.add(k)
        sems = {}
        for k in sorted(sem_names, key=str):
            sems[k] = nc.alloc_semaphore(name=("s_" + "_".join(str(x) for x in k)).replace(" ", "").replace("'", "")
                                         .replace("(", "_").replace(")", "_").replace(",", "_"))
        clock = {e: {} for e in ENGINES}
        plan = {e: [] for e in ENGINES}
        for o in ops:
            ck = clock[o.eng]
            wd = {}
            for d in sorted(o.deps):
                od = ops[d]
                s, v = od.ev
                if ck.get(s, 0) >= v:
                    continue
                wd[s] = max(wd.get(s, 0), v)
                ck[s] = v
                for s2, v2 in od.vc.items():
                    if ck.get(s2, 0) < v2:
                        ck[s2] = v2
            o.vc = dict(ck)
            if o.ev is not None:
                o.vc[o.ev[0]] = max(o.vc.get(o.ev[0], 0), o.ev[1])
            plan[o.eng].append((o, list(wd.items())))
        finals = [(("d", k), v) for k, v in dcnt.items()]
        self.n_waits = sum(len(w) for e in ENGINES for _, w in plan[e])
        self.n_sems = len(sems)

        def run_engine(eng_name, eng):
            for o, waits in plan[eng_name]:
                for s, v in waits:
                    eng.wait_ge(sems[s], v)
                ins = o.fn(eng)
                if o.ev is not None:
                    s, v = o.ev
                    ins.then_inc(sems[s], 16 if o.dsem is not None else 1)
            if eng_name == final_wait_eng:
                for s, v in finals:
                    eng.wait_ge(sems[s], v)

        with nc.Block() as block:
            @block.tensor
            def _(e):
                run_engine("pe", e)

            @block.scalar
            def _(e):
                run_engine("act", e)

            @block.vector
            def _(e):
                run_engine("dve", e)

            @block.gpsimd
            def _(e):
                run_engine("pool", e)

            @block.sync
            def _(e):
                run_engine("sp", e)


class TG:
    def __init__(self, T, NS, SL):
        self.T, self.NS, self.SL = T, NS, SL


def build_program(NT=4, do_sample=True, NL=L, NPB=PB, NBUF=4):
    nc = bass.Bass("TRN2", target_bir_lowering=False)
    P = Prog()

    def din(name, shape, dt=F32):
        return nc.dram_tensor(name, list(shape), dt, kind="ExternalInput").ap()

    def dout(name, shape, dt=F32):
        return nc.dram_tensor(name, list(shape), dt, kind="ExternalOutput").ap()

    def dint(name, shape, dt=BF16):
        return nc.dram_tensor(name, list(shape), dt, kind="Internal").ap()

    xp = din("xp", [PB, SEQ, D])
    xs = din("xs", [SB * SSEQ, D])
    memp = din("memp", [PB, NMEM, D])
    ck = din("ck", [L, SB, NMEM, D])
    cv = din("cv", [L, SB, NMEM, D])
    spool = din("spool", [L, SB, 15, 512])
    sconv = din("sconv", [L, SB, 2, 512])
    sffn = din("sffn", [L, SB, 2, 2 * DFF])
    cst_d = din("cst", [128, NCONST])
    wshape = {"w_in": (D, 2048), "w_out": (D, D), "w_q": (D, D), "w_k": (D, D), "w_v": (D, D), "w_o": (D, D),
              "w_up": (D, 2 * DFF), "w_down": (DFF, D)}
    wf = {k: din(k, [L, v[0], v[1]]) for k, v in wshape.items()}
    wb = {k: dint(k + "_b", [L, v[0], v[1]]) for k, v in wshape.items()}
    w_pool_d = din("w_pool", [L, 4, 128, 128])

    yp = dout("yp", [PB, SEQ, D])
    ys = dout("ys", [SB * SSEQ, D])
    mk = dout("mk", [L, PB, NMEM, D])
    mv = dout("mv", [L, PB, NMEM, D])
    poolp = dout("poolp", [L, PB, 15, 512])
    convp = dout("convp", [L, PB, 2, 512])
    ffnp = dout("ffnp", [L, PB, 2, 2 * DFF])
    pools = dout("pools", [L, SB, 15, 512])
    convs = dout("convs", [L, SB, 2, 512])
    ffns = dout("ffns", [L, SB, 2, 2 * DFF])

    with contextlib.ExitStack() as st:
        def sb(name, shape, dt):
            return st.enter_context(nc.sbuf_tensor(name, shape, dt))

        ident = sb("ident", [128, 128], F32)
        onesb = sb("onesb", [128, 128], BF16)
        ones1 = sb("ones1", [128, 128], BF16)
        cst = sb("cst_sb", [128, NCONST], F32)
        wpool = sb("wpool", [128, L * 4, 128], BF16)
        poolh_p = [sb("poolh_p%d" % l, [128, 128], F32) for l in range(L)]
        convh_p = [sb("convh_p%d" % l, [128, 128], F32) for l in range(L)]
        ffnh_p = [sb("ffnh_p%d" % l, [128, 128], F32) for l in range(L)]
        poolh_s = [sb("poolh_s%d" % l, [128, 2, 128], F32) for l in range(L)]
        convh_s = [sb("convh_s%d" % l, [128, 128], F32) for l in range(L)]
        ffnh_s = [sb("ffnh_s%d" % l, [128, 3, 128], F32) for l in range(L)]
        xT = sb("xT", [128, NCH, 512], F32)
        hT = sb("hT", [128, NCH, 512], BF16)
        sq = sb("sq", [128, NCH, 512], BF16)
        rstd = sb("rstd", [128, 2, 512], F32)
        yv = sb("yv", [128, NCH, 512], F32)
        catq = sb("catq", [128, NCH, 512], BF16)
        PT = sb("PT", [128, 2, 2, 512], BF16)
        rden = sb("rden", [128, 2, 512], F32)
        KT = [sb("KT%d" % i, [128, NCH, NMEM], BF16) for i in range(2)]
        VS = [sb("VS%d" % i, [128, 2, D], BF16) for i in range(2)]
        xstage = sb("xstage", [128, 4, D], F32)
        wr = [sb("wr%d" % i, [128, 8, 512], BF16) for i in range(NBUF)]
        rowt = [sb("rowt%d" % i, [128, 512], F32) for i in range(4)]
        AW = 13568
        arena = sb("arena", [128, AW], F32)
        PS = [st.enter_context(nc.psum_tensor("ps%d" % i, [128, 512], F32)) for i in range(8)]

        def AV(off, n, dt=F32):
            if dt == F32:
                return arena[:, off:off + n]
            assert n % 2 == 0
            return arena[:, off:off + n // 2].bitcast(BF16)

        bank_ctr = [0]

        def nb():
            b = bank_ctr[0]
            bank_ctr[0] = (b + 1) % 8
            return b

        wslot = [0]
        rowc = [0]

        def next_row():
            r = rowc[0]
            rowc[0] = (r + 1) % 4
            return r

        evc = [0]

        def ev_eng():
            evc[0] ^= 1
            return "act" if evc[0] else "dve"

        scratch = sb("scratch", [128, 8], F32)

        def arena_phase():
            P.extra = ()
            P.op("pool", lambda e: e.memset(scratch[:, 0:1], 0.0), writes=["AP"])
            P.extra = ("AP",)

        def cc(l, off, n=1):
            return cst[:, l * CL + off: l * CL + off + n]

        def copy_op(eng, out, in_, reads, writes):
            if eng == "act":
                P.op("act", lambda e: e.activation(out=out, in_=in_, func=AF.Copy), reads=reads, writes=writes)
            else:
                P.op(eng, lambda e: e.tensor_copy(out=out, in_=in_), reads=reads, writes=writes)

        def mm_fn(bank_ap, pairs, start=True, stop=True):
            def f(e):
                last = None
                n = len(pairs)
                for i, (a, b) in enumerate(pairs):
                    last = e.matmul(bank_ap, lhsT=a, rhs=b, start=(start and i == 0), stop=(stop and i == n - 1))
                return last
            return f

        def tr_fn(items):
            def f(e):
                last = None
                for (o, i) in items:
                    last = e.transpose(out=o, in_=i, identity=ident[:])
                return last
            return f

        def mm_block_kc_outer(slot, banks, T, rhs_of, key_of, kn=NCH, start=True, stop=True, k0=0):
            for kk in range(kn):
                def f(e, kk=kk):
                    last = None
                    for mo, b in enumerate(banks):
                        last = e.matmul(PS[b][:, 0:T], lhsT=wr[slot][:, kk, mo * 128:(mo + 1) * 128], rhs=rhs_of(k0 + kk),
                                        start=(start and kk == 0), stop=(stop and kk == kn - 1))
                    return last
                P.op("pe", f, reads=[("wr", slot), key_of(k0 + kk)], writes=[("ps", b) for b in banks])

        P.op("pool", lambda e: e.memset(ident[:], 0.0), writes=["ident"])
        P.op("pool", lambda e: e.affine_select(out=ident[:], in_=ident[:], pattern=[[-1, 128]],
                                               compare_op=ALU.not_equal, fill=1.0, base=0, channel_multiplier=1),
             reads=["ident"], writes=["ident"])
        P.op("pool", lambda e: e.memset(onesb[:], 1.0 / D), writes=["onesb"])
        P.op("pool", lambda e: e.memset(ones1[:], 1.0), writes=["ones1"])
        for i in range(4):
            P.op("pool", lambda e, i=i: e.memset(rowt[i][:], 0.0), writes=[("rowt", i)])
        P.op("pool", lambda e: e.memset(scratch[:, 2:4], 1.0), writes=["scr2", "scr3"])
        P.op("sp", lambda e: e.dma_start(out=cst[:], in_=cst_d), writes=["cst"], dsem="cst")
        P.op("pool", lambda e: e.dma_start(out=wpool[:], in_=w_pool_d.rearrange("l g c d -> c (l g) d")),
             writes=["wpool"], dsem="wpool")

        def precast(name, l):
            rows = wshape[name][0]
            for rc in range(rows // 128):
                P.op("pool", lambda e, name=name, l=l, rc=rc: e.dma_start(
                    out=wb[name][l, rc * 128:(rc + 1) * 128, :], in_=wf[name][l, rc * 128:(rc + 1) * 128, :]),
                    writes=[("wb", name, l, rc)], dsem=("pc", name, l), chain=False)

        def wb_keys(name, l, k0=0, kn=None):
            rows = wshape[name][0] // 128
            return [("wb", name, l, rc) for rc in range(rows)]

        precasted = set()

        def ensure_precast(name, l):
            if (name, l) not in precasted:
                precasted.add((name, l))
                precast(name, l)

        def layer_block_seq(l):
            seq = []
            for blk in range(4):
                seq.append(("w_in", l, 0, 8, ((blk * 512, 512),)))
            for name in ("w_out", "w_q", "w_o"):
                for blk in range(2):
                    seq.append((name, l, 0, 8, ((blk * 512, 512),)))
            for q in range(5):
                seq.append(("w_up", l, 0, 8, ((q * 512, 512),)))
                seq.append(("w_up", l, 0, 8, ((DFF + q * 512, 512),)))
            seq.append(("w_up", l, 0, 8, ((20 * 128, 256), (DFF + 20 * 128, 256))))
            for half in range(2):
                for (k0, kn) in ((0, 8), (8, 8), (16, 6)):
                    seq.append(("w_down", l, k0, kn, ((half * 512, 512),)))
            return seq

        first_pass = [bool(do_sample)]
        fp_seq = [b for l in range(NL) for b in layer_block_seq(l)]
        fp_cur = [0]
        fp_emitted = [0]
        PD = 2

        def emit_block_dma(eng, name, l, k0, kn, runs, s):
            src = wb[name]
            off = 0
            first = None
            for ri, (c0, wd) in enumerate(runs):
                o = P.op(eng, lambda e, s=s, off=off, c0=c0, wd=wd: e.dma_start(
                    out=wr[s][:, 0:kn, off:off + wd],
                    in_=src[l, k0 * 128:(k0 + kn) * 128, c0:c0 + wd].rearrange("(k p) m -> p k m", p=128)),
                    reads=wb_keys(name, l, k0, kn), writes=[("wr", s)] if ri == 0 else [],
                    dsem=(("wrp" if eng == "pool" else "wr"), s), noextra=True, chain=(ri == 0))
                if ri == 0:
                    first = o
                else:
                    o.deps |= first.deps
                    P.last_write[("wr", s)] = o.id
                off += wd

        def load_block(name, l, k0, kn, runs):
            runs = tuple(runs)
            if first_pass[0]:
                idx = fp_cur[0]
                assert fp_seq[idx] == (name, l, k0, kn, runs), (idx, fp_seq[idx], (name, l, k0, kn, runs))
                while fp_emitted[0] <= min(idx + PD, len(fp_seq) - 1):
                    j = fp_emitted[0]
                    n2, l2, k02, kn2, runs2 = fp_seq[j]
                    ensure_precast(n2, l2)
                    emit_block_dma("pool", n2, l2, k02, kn2, runs2, j % NBUF)
                    for j2 in range(j + 1, len(fp_seq)):
                        if (fp_seq[j2][0], fp_seq[j2][1]) != (n2, l2):
                            ensure_precast(fp_seq[j2][0], fp_seq[j2][1])
                            break
                    fp_emitted[0] += 1
                    if fp_emitted[0] == len(fp_seq):
                        for ll in range(NL):
                            ensure_precast("w_k", ll)
                            ensure_precast("w_v", ll)
                fp_cur[0] += 1
                wslot[0] = fp_emitted[0] % NBUF
                return idx % NBUF
            ensure_precast(name, l)
            s = wslot[0]
            wslot[0] = (s + 1) % NBUF
            emit_block_dma("sp", name, l, k0, kn, runs, s)
            return s

        def sumsq_rstd(tg, sq_keys, ri):
            T = tg.T
            b = nb()
            for c in range(NCH):
                P.op("pe", mm_fn(PS[b][:, 0:T], [(onesb[:], sq[:, c, 0:T])], start=(c == 0), stop=(c == NCH - 1)),
                     reads=[("sq", c), "onesb"], writes=[("ps", b)])
            P.op("act", lambda e: e.activation(out=rstd[:, ri, 0:T], in_=PS[b][:, 0:T], func=AF.Ln,
                                               bias=cst[:, EPSC:EPSC + 1], scale=1.0),
                 reads=["cst"], writes=[("ps", b), ("rstd", ri)])
            P.op("act", lambda e: e.activation(out=rstd[:, ri, 0:T], in_=rstd[:, ri, 0:T], func=AF.Exp, scale=-0.5),
                 reads=[("rstd", ri)], writes=[("rstd", ri)])

        def pre_norm(tg, l, goff):
            T = tg.T
            for c in range(NCH):
                P.op("act", lambda e, c=c: e.activation(out=sq[:, c, 0:T], in_=xT[:, c, 0:T], func=AF.Square),
                     reads=[("xT", c)], writes=[("sq", c)])
            sumsq_rstd(tg, None, 0)
            for c in range(NCH):
                P.op("dve", lambda e, c=c: e.scalar_tensor_tensor(out=hT[:, c, 0:T], in0=xT[:, c, 0:T],
                                                                   scalar=cc(l, goff + c), in1=rstd[:, 0, 0:T],
                                                                   op0=ALU.mult, op1=ALU.mult),
                     reads=[("xT", c), ("rstd", 0), "cst"], writes=[("hT", c)])

        def pre_norm_deferred(tg, l, goff, need_rr=False):
            T = tg.T
            for c in range(NCH):
                P.op("act", lambda e, c=c: e.activation(out=hT[:, c, 0:T], in_=xT[:, c, 0:T], func=AF.Identity,
                                                        scale=cc(l, goff + c)),
                     reads=[("xT", c), "cst"], writes=[("hT", c)])
            for c in range(NCH):
                P.op("act", lambda e, c=c: e.activation(out=sq[:, c, 0:T], in_=xT[:, c, 0:T], func=AF.Square),
                     reads=[("xT", c)], writes=[("sq", c)])

        def pre_norm_deferred2(tg, need_rr=False):
            T = tg.T
            sumsq_rstd(tg, None, 0)
            if need_rr:
                P.op("dve", lambda e: e.tensor_tensor(out=rden[:, 1, 0:T], in0=rstd[:, 0, 0:T], in1=rstd[:, 0, 0:T],
                                                      op=ALU.mult),
                     reads=[("rstd", 0)], writes=[("rden", 1)])

        def evac_y(tg, b, c, l, goff):
            T = tg.T
            P.op("act", lambda e: e.activation(out=yv[:, c, 0:T], in_=PS[b][:, 0:T], func=AF.Identity,
                                               scale=cc(l, goff + c)),
                 reads=["cst"], writes=[("ps", b), ("yv", c)])
            P.op("act", lambda e: e.activation(out=sq[:, c, 0:T], in_=PS[b][:, 0:T], func=AF.Square),
                 writes=[("ps", b), ("sq", c)])

        def post_norm(tg, l, goff):
            T = tg.T
            sumsq_rstd(tg, None, 1)
            for c in range(NCH):
                P.op("dve", lambda e, c=c: e.tensor_tensor(out=yv[:, c, 0:T], in0=yv[:, c, 0:T], in1=rstd[:, 1, 0:T],
                                                           op=ALU.mult),
                     reads=[("yv", c), ("rstd", 1)], writes=[("yv", c)])
                P.op("dve", lambda e, c=c: e.tensor_tensor(out=xT[:, c, 0:T], in0=yv[:, c, 0:T], in1=xT[:, c, 0:T],
                                                           op=ALU.add),
                     reads=[("yv", c), ("xT", c)], writes=[("xT", c)])

        def dense_1024(tg, name, l, rhs_buf, rhs_key, evac, hook=None):
            T = tg.T
            for blk in range(2):
                s = load_block(name, l, 0, 8, [(blk * 512, 512)])
                banks = []
                if blk == 0:
                    banks = [nb() for _ in range(4)]
                    mm_block_kc_outer(s, banks, T, lambda kc: rhs_buf[:, kc, 0:T], lambda kc: (rhs_key, kc))
                else:
                    for mo in range(4):
                        b = nb()
                        banks.append(b)
                        P.op("pe", mm_fn(PS[b][:, 0:T], [(wr[s][:, kc, mo * 128:(mo + 1) * 128], rhs_buf[:, kc, 0:T])
                                                           for kc in range(NCH)]),
                             reads=[("wr", s)] + [(rhs_key, kc) for kc in range(NCH)], writes=[("ps", b)])
                if blk == 0 and hook is not None:
                    hook()
                for mo in range(4):
                    evac(banks[mo], blk * 4 + mo)

        def mix(tg, l, kind, first):
            T, NS, SL = tg.T, tg.NS, tg.SL
            EL = 15 + SL
            uA = AV(0, 4 * NS * EL).rearrange("p (g s e) -> p g s e", g=4, s=NS)
            Bg = AV(2112, 4 * T).rearrange("p (j t) -> p j t", j=4)
            CVb = AV(4160, 4 * NS * (2 + SL)).rearrange("p (j s e) -> p j s e", j=4, s=NS)
            pa = [AV(6224 + g * 528, NS * EL).rearrange("p (s e) -> p s e", s=NS) for g in range(4)]
            pb = [AV(6224 + (4 + g) * 528, NS * EL).rearrange("p (s e) -> p s e", s=NS) for g in range(4)]
            dd = AV(10448, 4 * T, BF16).rearrange("p (g t) -> p g t", g=4)
            ta = AV(11472, 4 * T).rearrange("p (j t) -> p j t", j=4)

            def v3(ap2d):
                return ap2d.rearrange("p (s l) -> p s l", s=NS)

            pre_norm_deferred(tg, l, G_MIXPRE)
            r3 = v3(rstd[:, 0, 0:T])
            rr3 = v3(rden[:, 1, 0:T])
            s0 = load_block("w_in", l, 0, 8, [(0, 512)])
            banks0 = [nb() for _ in range(4)]
            mm_block_kc_outer(s0, banks0, T, lambda kc: hT[:, kc, 0:T], lambda kc: ("hT", kc))
            pre_norm_deferred2(tg, need_rr=True)
            arena_phase()
            if kind == "p":
                P.op("pool", lambda e: e.tensor_copy(out=uA[:, :, 0, 0:15],
                                                     in_=poolh_p[l][:, 0:60].rearrange("p (g r) -> p g r", g=4)),
                     reads=[("poolh", l)], writes=[("uA", g) for g in range(4)])
                P.op("pool", lambda e: e.tensor_copy(out=CVb[:, :, 0, 0:2],
                                                     in_=convh_p[l][:, 0:8].rearrange("p (j r) -> p j r", j=4)),
                     reads=[("convh", l)], writes=[("CV", j) for j in range(4)])
            else:
                r = next_row()
                P.op("sp", lambda e, r=r: e.dma_start(out=rowt[r][0:60, :], in_=spool[l].rearrange("s r d -> (s r) d")),
                     reads=[("rowt", r)], writes=[("rowt", r)], dsem=("rowt", r))
                b = nb()
                P.op("pe", tr_fn([(PS[b][:, g * 128:(g + 1) * 128], rowt[r][:, g * 128:(g + 1) * 128]) for g in range(4)]),
                     reads=[("rowt", r), "ident"], writes=[("ps", b)])
                P.op("dve", lambda e, b=b: e.tensor_copy(
                    out=uA[:, :, :, 0:15],
                    in_=PS[b][:, :].rearrange("p (g q) -> p g q", q=128)[:, :, 0:60].rearrange("p g (s r) -> p g s r", s=4)),
                    writes=[("ps", b)] + [("uA", g) for g in range(4)])
                r2 = next_row()
                P.op("sp", lambda e, r2=r2: e.dma_start(out=rowt[r2][0:8, :], in_=sconv[l].rearrange("s r d -> (s r) d")),
                     reads=[("rowt", r2)], writes=[("rowt", r2)], dsem=("rowt", r2))
                b2 = nb()
                P.op("pe", tr_fn([(PS[b2][:, j * 128:(j + 1) * 128], rowt[r2][:, j * 128:(j + 1) * 128]) for j in range(4)]),
                     reads=[("rowt", r2), "ident"], writes=[("ps", b2)])
                P.op("dve", lambda e, b2=b2: e.tensor_copy(
                    out=CVb[:, :, :, 0:2],
                    in_=PS[b2][:, :].rearrange("p (j q) -> p j q", q=128)[:, :, 0:8].rearrange("p j (s r) -> p j s r", s=4)),
                    writes=[("ps", b2)] + [("CV", j) for j in range(4)])
            def pool_section():
                lo1 = [15, 13, 9, 1]
                for g in range(4):
                    lo = lo1[g]
                    P.op("pool", lambda e, g=g, lo=lo: e.tensor_tensor(out=pa[g][:, :, lo:EL], in0=uA[:, g, :, lo:EL],
                                                                       in1=uA[:, g, :, lo - 1:EL - 1], op=ALU.add),
                         reads=[("uA", g)], writes=[("pa", g)])
                lo2 = [None, 15, 11, 3]
                for g in range(1, 4):
                    lo = lo2[g]
                    P.op("pool", lambda e, g=g, lo=lo: e.tensor_tensor(out=pb[g][:, :, lo:EL], in0=pa[g][:, :, lo:EL],
                                                                       in1=pa[g][:, :, lo - 2:EL - 2], op=ALU.add),
                         reads=[("pa", g)], writes=[("pb", g)])
                lo3 = [None, None, 15, 7]
                for g in range(2, 4):
                    lo = lo3[g]
                    P.op("pool", lambda e, g=g, lo=lo: e.tensor_tensor(out=pa[g][:, :, lo:EL], in0=pb[g][:, :, lo:EL],
                                                                       in1=pb[g][:, :, lo - 4:EL - 4], op=ALU.add),
                         reads=[("pb", g), ("pa", g)], writes=[("pa", g)])
                P.op("pool", lambda e: e.tensor_tensor(out=pb[3][:, :, 15:EL], in0=pa[3][:, :, 15:EL],
                                                       in1=pa[3][:, :, 7:EL - 8], op=ALU.add),
                     reads=[("pa", 3), ("pb", 3)], writes=[("pb", 3)])
                fin = [pa[0], pb[1], pa[2], pb[3]]
                fkey = [("pa", 0), ("pb", 1), ("pa", 2), ("pb", 3)]
                for g in range(4):
                    w = 2 << g
                    P.op("dve", lambda e, g=g, w=w: e.scalar_tensor_tensor(
                        out=v3(dd[:, g, 0:T]), in0=fin[g][:, :, 15:EL], scalar=1.0 / w, in1=uA[:, g, :, 15:EL],
                        op0=ALU.mult, op1=ALU.subtract),
                        reads=[fkey[g], ("uA", g)], writes=[("dd", g)])
                    if first:
                        n = w - 1
                        P.op("dve", lambda e, g=g, n=n: e.tensor_tensor(out=fin[g][:, 0, 15:15 + n], in0=fin[g][:, 0, 15:15 + n],
                                                                        in1=cst[:, INVCNT:INVCNT + n], op=ALU.mult),
                             reads=[fkey[g], "cst", ("dd", g)], writes=[fkey[g]])
                        P.op("dve", lambda e, g=g, n=n: e.tensor_tensor(out=dd[:, g, 0:n], in0=fin[g][:, 0, 15:15 + n],
                                                                        in1=uA[:, g, 0, 15:15 + n], op=ALU.subtract),
                             reads=[fkey[g], ("uA", g)], writes=[("dd", g)])
                if kind == "p":
                    P.op("pool", lambda e: e.tensor_copy(out=poolh_p[l][:, 0:60].rearrange("p (g r) -> p g r", g=4),
                                                         in_=uA[:, :, 0, SL:SL + 15]),
                         reads=[("uA", g) for g in range(4)], writes=[("poolh", l)])
                else:
                    for w in range(2):
                        P.op("pool", lambda e, w=w: e.tensor_copy(
                            out=poolh_s[l][:, w, 0:120].rearrange("p (gl s r) -> p gl s r", gl=2, s=4),
                            in_=uA[:, 2 * w:2 * w + 2, :, SL:SL + 15]),
                            reads=[("uA", 2 * w), ("uA", 2 * w + 1)], writes=[("poolhs", l, w)])

            for blk in range(4):
                if blk > 0:
                    s = load_block("w_in", l, 0, 8, [(blk * 512, 512)])
                for mo in range(4):
                    if blk == 0:
                        b = banks0[mo]
                    else:
                        b = nb()
                        P.op("pe", mm_fn(PS[b][:, 0:T], [(wr[s][:, kc, mo * 128:(mo + 1) * 128], hT[:, kc, 0:T])
                                                           for kc in range(NCH)]),
                             reads=[("wr", s)] + [("hT", kc) for kc in range(NCH)], writes=[("ps", b)])
                    if blk == 0:
                        P.op("dve", lambda e, b=b, mo=mo: e.tensor_tensor(out=uA[:, mo, :, 15:15 + SL], in0=v3(PS[b][:, 0:T]),
                                                                          in1=r3, op=ALU.mult),
                             reads=[("rstd", 0)], writes=[("ps", b), ("uA", mo)])
                    elif blk == 1:
                        P.op("dve", lambda e, b=b, mo=mo: e.tensor_tensor(out=Bg[:, mo, :], in0=PS[b][:, 0:T],
                                                                          in1=rstd[:, 0, 0:T], op=ALU.mult),
                             reads=[("rstd", 0)], writes=[("ps", b), ("Bg", mo)])
                    elif blk == 2:
                        P.op("dve", lambda e, b=b, mo=mo: e.tensor_tensor(out=CVb[:, mo, :, 2:2 + SL], in0=v3(PS[b][:, 0:T]),
                                                                          in1=rr3, op=ALU.mult),
                             reads=[("rden", 1)], writes=[("ps", b), ("CV", mo)])
                    else:
                        P.op("dve", lambda e, b=b, mo=mo: e.tensor_tensor(out=CVb[:, mo, :, 2:2 + SL], in0=v3(PS[b][:, 0:T]),
                                                                          in1=CVb[:, mo, :, 2:2 + SL], op=ALU.mult),
                             writes=[("ps", b), ("CV", mo)])
                if blk == 0:
                    pool_section()
            for g in range(4):
                b = nb()
                P.op("pe", mm_fn(PS[b][:, 0:T], [(wpool[:, l * 4 + g, :], dd[:, g, 0:T])]),
                     reads=["wpool", ("dd", g)], writes=[("ps", b)])
                P.op("act", lambda e, b=b, g=g: e.activation(out=catq[:, g, 0:T], in_=PS[b][:, 0:T], func=AF.Identity,
                                                             scale=cc(l, POOLSC + g)),
                     reads=["cst"], writes=[("ps", b), ("catq", g)])
            for j in range(4):
                P.op("pool", lambda e, j=j: e.tensor_scalar(out=v3(ta[:, j, 0:T]), in0=CVb[:, j, :, 2:2 + SL],
                                                            scalar1=cc(l, CONVW + 2 * 4 + j), scalar2=cc(l, CONVB + j),
                                                            op0=ALU.mult, op1=ALU.add),
                     reads=[("CV", j), "cst"], writes=[("ta", j)])
            for j in range(4):
                P.op("dve", lambda e, j=j: e.scalar_tensor_tensor(out=v3(ta[:, j, 0:T]), in0=CVb[:, j, :, 1:1 + SL],
                                                                   scalar=cc(l, CONVW + 1 * 4 + j), in1=v3(ta[:, j, 0:T]),
                                                                   op0=ALU.mult, op1=ALU.add),
                     reads=[("CV", j), ("ta", j), "cst"], writes=[("ta", j)])
            for j in range(4):
                P.op("dve", lambda e, j=j: e.scalar_tensor_tensor(out=v3(ta[:, j, 0:T]), in0=CVb[:, j, :, 0:SL],
                                                                   scalar=cc(l, CONVW + 0 * 4 + j), in1=v3(ta[:, j, 0:T]),
                                                                   op0=ALU.mult, op1=ALU.add),
                     reads=[("CV", j), ("ta", j), "cst"], writes=[("ta", j)])
            for j in range(4):
                P.op("pool", lambda e, j=j: e.tensor_tensor(out=catq[:, 4 + j, 0:T], in0=ta[:, j, 0:T], in1=Bg[:, j, :],
                                                            op=ALU.mult),
                     reads=[("ta", j), ("Bg", j)], writes=[("catq", 4 + j)])
            if kind == "p":
                P.op("pool", lambda e: e.tensor_copy(out=convh_p[l][:, 0:8].rearrange("p (j r) -> p j r", j=4),
                                                     in_=CVb[:, :, 0, SL:SL + 2]),
                     reads=[("CV", j) for j in range(4)], writes=[("convh", l)])
            else:
                P.op("pool", lambda e: e.tensor_copy(out=convh_s[l][:, 0:32].rearrange("p (j s r) -> p j s r", j=4, s=4),
                                                     in_=CVb[:, :, :, SL:SL + 2]),
                     reads=[("CV", j) for j in range(4)], writes=[("convhs", l)])
            dense_1024(tg, "w_out", l, catq, "catq", lambda b, c: evac_y(tg, b, c, l, G_MIXPOST))
            post_norm(tg, l, G_MIXPOST)

        def load_sample_kv(l, s):
            slot = s % 2
            P.op("sp", lambda e: e.dma_start(out=xstage[:, 0:2, :], in_=ck[l, s].rearrange("(m p) d -> p m d", p=128)),
                 writes=["xstage"], dsem="xst")
            P.op("sp", lambda e: e.dma_start(out=xstage[:, 2:4, :], in_=cv[l, s].rearrange("(m p) d -> p m d", p=128)),
                 writes=["xstageV"], dsem="xstv")
            P.op("pool", lambda e: e.tensor_copy(out=VS[slot][:], in_=xstage[:, 2:4, :]),
                 reads=["xstageV"], writes=[("VS", slot)])
            for hc2 in range(4):
                b = nb()
                items = []
                for i in range(2):
                    hc = hc2 * 2 + i
                    for mc in range(2):
                        items.append((PS[b][:, i * 256 + mc * 128: i * 256 + (mc + 1) * 128],
                                      xstage[:, mc, hc * 128:(hc + 1) * 128]))
                P.op("pe", tr_fn(items), reads=["xstage", "ident"], writes=[("ps", b)])
                copy_op(ev_eng(), KT[slot][:, hc2 * 2:hc2 * 2 + 2, :], PS[b][:, :].rearrange("p (i m) -> p i m", i=2),
                        [], [("ps", b), ("KT", slot)])

        def attn(tg, l, kind):
            T, NS, SL = tg.T, tg.NS, tg.SL
            pre_norm_deferred(tg, l, G_ATTNPRE)

            def evq(b, c):
                P.op("dve", lambda e: e.tensor_tensor(out=catq[:, c, 0:T], in0=PS[b][:, 0:T], in1=rstd[:, 0, 0:T],
                                                      op=ALU.mult),
                     reads=[("rstd", 0)], writes=[("ps", b), ("catq", c)])
            dense_1024(tg, "w_q", l, hT, "hT", evq, hook=lambda: pre_norm_deferred2(tg))
            units = []
            for s in range(NS):
                for hd in range(4):
                    units.append((s, hd))

            def geom(s):
                if kind == "p":
                    return l, 0, T
                return s % 2, s * SL, SL

            def stage_a(u, s, hd):
                slot, c0, cn = geom(s)
                if kind != "p" and hd == 0:
                    load_sample_kv(l, s)
                pti = u % 2
                pt = PT[:, pti]
                ptk = ("PT", pti)
                for mc in range(2):
                    b = nb()
                    P.op("pe", mm_fn(PS[b][:, 0:cn], [(KT[slot][:, 2 * hd + ec, mc * 128:(mc + 1) * 128],
                                                        catq[:, 2 * hd + ec, c0:c0 + cn]) for ec in range(2)]),
                         reads=[("KT", slot), ("catq", 2 * hd), ("catq", 2 * hd + 1)], writes=[("ps", b)])
                    P.op("act", lambda e, b=b, mc=mc, pt=pt, cn=cn: e.activation(out=pt[:, mc, 0:cn], in_=PS[b][:, 0:cn],
                                                                                 func=AF.Exp, scale=1.0 / 16.0),
                         writes=[("ps", b), ptk])

            def stage_b(u, s, hd):
                slot, c0, cn = geom(s)
                pti = u % 2
                pt = PT[:, pti]
                ptk = ("PT", pti)
                rk = ("rden", pti)
                rd = rden[:, pti, 0:cn]
                b = nb()
                P.op("pe", mm_fn(PS[b][:, 0:cn], [(ones1[:], pt[:, mc, 0:cn]) for mc in range(2)]),
                     reads=[ptk, "ones1"], writes=[("ps", b)])
                P.op("act", lambda e, b=b, rd=rd, cn=cn: e.activation(out=rd, in_=PS[b][:, 0:cn], func=AF.Ln),
                     writes=[("ps", b), rk])
                P.op("act", lambda e, rd=rd: e.activation(out=rd, in_=rd, func=AF.Exp, scale=-1.0),
                     reads=[rk], writes=[rk])
                for ec in range(2):
                    b = nb()
                    P.op("pe", mm_fn(PS[b][:, 0:cn], [(VS[slot][:, mc, hd * 256 + ec * 128: hd * 256 + (ec + 1) * 128],
                                                        pt[:, mc, 0:cn]) for mc in range(2)]),
                         reads=[("VS", slot), ptk], writes=[("ps", b)])
                    P.op("dve", lambda e, b=b, ec=ec, rd=rd, hd=hd, c0=c0, cn=cn: e.tensor_tensor(
                        out=hT[:, 2 * hd + ec, c0:c0 + cn], in0=PS[b][:, 0:cn], in1=rd, op=ALU.mult),
                        reads=[rk], writes=[("ps", b), ("hT", 2 * hd + ec)])

            for u, (s, hd) in enumerate(units):
                stage_a(u, s, hd)
                if u >= 1:
                    stage_b(u - 1, *units[u - 1])
            stage_b(len(units) - 1, *units[-1])
            dense_1024(tg, "w_o", l, hT, "hT", lambda b, c: evac_y(tg, b, c, l, G_ATTNPOST))
            post_norm(tg, l, G_ATTNPOST)

        def ffn(tg, l, kind, first):
            T, NS, SL = tg.T, tg.NS, tg.SL
            act = AV(0, GCH * T, BF16).rearrange("p (j t) -> p j t", j=GCH)
            U = [AV(5632 + i * 516, NS * (2 + SL)).rearrange("p (s e) -> p s e", s=NS) for i in range(4)]
            tb = [AV(7700 + i * 512, T) for i in range(4)]
            sg = [AV(9760 + i * 512, T) for i in range(4)]
            uc = [0]

            def v3(ap2d):
                return ap2d.rearrange("p (s l) -> p s l", s=NS)

            pre_norm(tg, l, G_FFNPRE)
            arena_phase()

            def conv_chunk(b, c):
                i = uc[0]
                uc[0] = (i + 1) % 4
                if kind == "p":
                    hin = ffnh_p[l][:, c * 2:c * 2 + 2]
                    hk = ("ffnh", l, c)
                    P.op("pool", lambda e: e.tensor_copy(out=U[i][:, 0, 0:2], in_=hin), reads=[hk, ("U", i)], writes=[("Uh", i)])
                else:
                    hin = ffnh_s[l][:, c // 16, (c % 16) * 8:(c % 16) * 8 + 8].rearrange("p (s r) -> p s r", s=4)
                    hk = ("ffnhs", l, c // 16)
                    P.op("pool", lambda e: e.tensor_copy(out=U[i][:, :, 0:2], in_=hin), reads=[hk, ("U", i)], writes=[("Uh", i)])
                P.op("act", lambda e: e.activation(out=U[i][:, :, 2:2 + SL], in_=v3(PS[b][:, 0:T]), func=AF.Copy),
                     reads=[("Uh", i)], writes=[("ps", b), ("U", i)])
                if c % 2 == 0:
                    P.op("act", lambda e: e.activation(out=tb[i], in_=PS[b][:, 0:T], func=AF.Identity,
                                                       scale=cc(l, FFNW + 2 * FCH + c), bias=cc(l, FFNB + c)),
                         reads=["cst"], writes=[("ps", b), ("tb", i)])
                else:
                    P.op("dve", lambda e: e.tensor_scalar(out=v3(tb[i]), in0=U[i][:, :, 2:2 + SL],
                                                          scalar1=cc(l, FFNW + 2 * FCH + c), scalar2=cc(l, FFNB + c),
                                                          op0=ALU.mult, op1=ALU.add),
                         reads=["cst", ("U", i)], writes=[("tb", i)])
                P.op("dve", lambda e: e.scalar_tensor_tensor(out=v3(tb[i]), in0=U[i][:, :, 1:1 + SL],
                                                              scalar=cc(l, FFNW + 1 * FCH + c), in1=v3(tb[i]),
                                                              op0=ALU.mult, op1=ALU.add),
                     reads=[("U", i), ("Uh", i), ("tb", i), "cst"], writes=[("tb", i)])
                P.op("dve", lambda e: e.scalar_tensor_tensor(out=v3(tb[i]), in0=U[i][:, :, 0:SL],
                                                              scalar=cc(l, FFNW + 0 * FCH + c), in1=v3(tb[i]),
                                                              op0=ALU.mult, op1=ALU.add),
                     reads=[("U", i), ("Uh", i), ("tb", i), "cst"], writes=[("tb", i)])
                if kind == "p":
                    P.op("pool", lambda e: e.tensor_copy(out=ffnh_p[l][:, c * 2:c * 2 + 2], in_=U[i][:, 0, SL:SL + 2]),
                         reads=[("U", i)], writes=[hk])
                else:
                    P.op("pool", lambda e: e.tensor_copy(
                        out=ffnh_s[l][:, c // 16, (c % 16) * 8:(c % 16) * 8 + 8].rearrange("p (s r) -> p s r", s=4),
                        in_=U[i][:, :, SL:SL + 2]),
                        reads=[("U", i)], writes=[hk])
                return i

            pending = []
            DELAY = 2

            def flush(n):
                while len(pending) > n:
                    pending.pop(0)()

            def gate_chunk(b, j, gi):
                i = conv_chunk(b, j)
                pending.append(lambda: P.op("act", lambda e: e.activation(out=sg[gi], in_=tb[i], func=AF.Silu),
                                            reads=[("tb", i)], writes=[("sg", gi)]))
                flush(DELAY)

            def val_chunk(b, j, gi):
                i = conv_chunk(b, GCH + j)
                pending.append(lambda: P.op("pool", lambda e: e.tensor_tensor(out=act[:, j, 0:T], in0=tb[i], in1=sg[gi],
                                                                              op=ALU.mult),
                                            reads=[("tb", i), ("sg", gi)], writes=[("act", j)]))
                flush(DELAY)

            def up_block(runs, chunks, first_blk=False):
                s = load_block("w_up", l, 0, 8, runs)
                if first_blk:
                    banks = [nb() for _ in range(4)]
                    mm_block_kc_outer(s, banks, T, lambda kc: hT[:, kc, 0:T], lambda kc: ("hT", kc))
                for mo, (kd, j, gi) in enumerate(chunks):
                    if first_blk:
                        b = banks[mo]
                    else:
                        b = nb()
                        P.op("pe", mm_fn(PS[b][:, 0:T], [(wr[s][:, kc, mo * 128:(mo + 1) * 128], hT[:, kc, 0:T])
                                                           for kc in range(NCH)]),
                             reads=[("wr", s)] + [("hT", kc) for kc in range(NCH)], writes=[("ps", b)])
                    if kd == "g":
                        gate_chunk(b, j, gi)
                    else:
                        val_chunk(b, j, gi)

            for q in range(5):
                up_block([(q * 512, 512)], [("g", q * 4 + i, i) for i in range(4)], first_blk=(q == 0))
                up_block([(DFF + q * 512, 512)], [("v", q * 4 + i, i) for i in range(4)])
            up_block([(20 * 128, 256), (DFF + 20 * 128, 256)], [("g", 20, 0), ("g", 21, 1), ("v", 20, 0), ("v", 21, 1)])
            flush(0)
            P.op("act", lambda e: e.activation(out=scratch[:, 3:4], in_=scratch[:, 2:3], func=AF.Ln),
                 reads=["scr2"], writes=["scr3"])
            for half in range(2):
                banks = [nb() for _ in range(4)]
                kgroups = [(0, 8), (8, 8), (16, 6)]
                for gi, (k0, kn) in enumerate(kgroups):
                    s = load_block("w_down", l, k0, kn, [(half * 512, 512)])
                    if half == 0:
                        mm_block_kc_outer(s, banks, T, lambda kc: act[:, kc, 0:T], lambda kc: ("act", kc), kn=kn,
                                          start=(gi == 0), stop=(gi == 2), k0=k0)
                        continue
                    for mo in range(4):
                        b = banks[mo]
                        P.op("pe", mm_fn(PS[b][:, 0:T], [(wr[s][:, kk, mo * 128:(mo + 1) * 128], act[:, k0 + kk, 0:T])
                                                           for kk in range(kn)], start=(gi == 0), stop=(gi == 2)),
                             reads=[("wr", s)] + [("act", k0 + kk) for kk in range(kn)], writes=[("ps", b)])
                for mo in range(4):
                    evac_y(tg, banks[mo], half * 4 + mo, l, G_FFNPOST)
            post_norm(tg, l, G_FFNPOST)

        def store_states(l, kind, b_idx):
            def tr_store(src_ap, key, nvalid, dst):
                b = nb()
                P.op("pe", tr_fn([(PS[b][:, 0:128], src_ap)]), reads=[key, "ident"], writes=[("ps", b)])
                r = next_row()
                P.op("act", lambda e: e.activation(out=rowt[r][:, 0:128], in_=PS[b][:, 0:128], func=AF.Copy),
                     writes=[("ps", b), ("rowt", r)])
                P.op("act", lambda e: e.dma_start(out=dst, in_=rowt[r][0:nvalid, 0:128]), reads=[("rowt", r)],
                     dsem=("rowst", r))
                P.op("pool", lambda e: e.memset(rowt[r][:, 0:128], 0.0), reads=[], writes=[("rowt", r)])
            if kind == "p":
                tr_store(poolh_p[l][:, :], ("poolh", l), 60,
                         poolp[l, b_idx].rearrange("r (g p) -> g r p", p=128))
                tr_store(convh_p[l][:, :], ("convh", l), 8,
                         convp[l, b_idx].rearrange("r (j p) -> j r p", p=128))
                b = nb()
                P.op("pe", tr_fn([(PS[b][:, 0:128], ffnh_p[l][:, :])]),
                     reads=[("ffnh", l, c) for c in range(FCH)] + ["ident"], writes=[("ps", b)])
                r = next_row()
                P.op("act", lambda e: e.activation(out=rowt[r][:, 0:128], in_=PS[b][:, 0:128], func=AF.Copy),
                     writes=[("ps", b), ("rowt", r)])
                P.op("act", lambda e: e.dma_start(out=ffnp[l, b_idx].rearrange("r (c p) -> c r p", p=128),
                                                  in_=rowt[r][0:88, 0:128]), reads=[("rowt", r)], dsem=("rowst", r))
                P.op("pool", lambda e: e.memset(rowt[r][:, 0:128], 0.0), reads=[], writes=[("rowt", r)])
            else:
                for w in range(2):
                    tr_store(poolh_s[l][:, w, :], ("poolhs", l, w), 120,
                             pools[l].rearrange("s r (g p) -> g s r p", p=128)[2 * w:2 * w + 2])
                tr_store(convh_s[l][:, :], ("convhs", l), 32,
                         convs[l].rearrange("s r (j p) -> j s r p", p=128))
                for w in range(3):
                    ncl = 16 if w < 2 else 12
                    tr_store(ffnh_s[l][:, w, :], ("ffnhs", l, w), ncl * 8,
                             ffns[l].rearrange("s r (c p) -> c s r p", p=128)[16 * w:16 * w + ncl])

        def load_sample_ffn_state(l):
            for grp in range(11):
                r = next_row()
                P.op("sp", lambda e, r=r, grp=grp: e.dma_start(
                    out=rowt[r][0:8, :], in_=sffn[l].rearrange("s r d -> (s r) d")[:, grp * 512:(grp + 1) * 512]),
                    reads=[("rowt", r)], writes=[("rowt", r)], dsem=("rowt", r))
                b = nb()
                P.op("pe", tr_fn([(PS[b][:, i * 128:(i + 1) * 128], rowt[r][:, i * 128:(i + 1) * 128]) for i in range(4)]),
                     reads=[("rowt", r), "ident"], writes=[("ps", b)])
                c0 = grp * 4
                w = c0 // 16
                cl = c0 % 16
                P.op("dve", lambda e, b=b, w=w, cl=cl: e.tensor_copy(
                    out=ffnh_s[l][:, w, cl * 8:(cl + 4) * 8].rearrange("p (c q) -> p c q", c=4),
                    in_=PS[b][:, :].rearrange("p (c q) -> p c q", q=128)[:, :, 0:8]),
                    writes=[("ps", b), ("ffnhs", l, w)])

        def kv_prep(bi):
            mstage = AV(0, 2 * D).rearrange("p (m d) -> p m d", m=2)
            mT = AV(2048, NCH * NMEM).rearrange("p (c m) -> p c m", c=NCH)
            mh = AV(5120, L * NCH * NMEM, BF16).rearrange("p (l c m) -> p l c m", l=L, c=NCH)
            rm = AV(7168, NMEM)
            stg = [AV(7424 + i * 512, 512) for i in range(4)]
            sc = [0]
            arena_phase()
            P.op("sp", lambda e: e.dma_start(out=mstage, in_=memp[bi].rearrange("(m p) d -> p m d", p=128)),
                 writes=["mstage"], dsem="mst")
            for c2 in range(4):
                b = nb()
                items = []
                for i in range(2):
                    c = c2 * 2 + i
                    for mc in range(2):
                        items.append((PS[b][:, i * 256 + mc * 128: i * 256 + (mc + 1) * 128],
                                      mstage[:, mc, c * 128:(c + 1) * 128]))
                P.op("pe", tr_fn(items), reads=["mstage", "ident"], writes=[("ps", b)])
                copy_op(ev_eng(), mT[:, c2 * 2:c2 * 2 + 2, :], PS[b][:, :].rearrange("p (i m) -> p i m", i=2),
                        [], [("ps", b), ("mT", c2)])
            P.op("act", lambda e: e.activation(out=sq[:, :, 0:NMEM], in_=mT, func=AF.Square),
                 reads=[("mT", i) for i in range(4)], writes=[("sq", c) for c in range(NCH)])
            b = nb()
            P.op("pe", mm_fn(PS[b][:, 0:NMEM], [(onesb[:], sq[:, c, 0:NMEM]) for c in range(NCH)]),
                 reads=[("sq", c) for c in range(NCH)] + ["onesb"], writes=[("ps", b)])
            P.op("act", lambda e, b=b: e.activation(out=rm, in_=PS[b][:, 0:NMEM], func=AF.Ln, bias=cst[:, EPSC:EPSC + 1], scale=1.0),
                 reads=["cst"], writes=[("ps", b), "rm"])
            P.op("act", lambda e: e.activation(out=rm, in_=rm, func=AF.Exp, scale=-0.5), reads=["rm"], writes=["rm"])
            for l in range(NL):
                for c in range(NCH):
                    P.op("dve", lambda e, l=l, c=c: e.scalar_tensor_tensor(out=mh[:, l, c, :], in0=mT[:, c, :],
                                                                           scalar=cc(l, G_MEM + c), in1=rm,
                                                                           op0=ALU.mult, op1=ALU.mult),
                         reads=[("mT", c // 2), "rm", "cst"], writes=[("mh", l, c)])
            for l in range(NL):
                mhk = [("mh", l, c) for c in range(NCH)]
                for name, dst in (("w_k", mk), ("w_v", mv)):
                    for blk in range(2):
                        s = load_block(name, l, 0, 8, [(blk * 512, 512)])
                        if name == "w_k":
                            for mo in range(4):
                                b = nb()
                                P.op("pe", mm_fn(PS[b][:, 0:NMEM], [(wr[s][:, kc, mo * 128:(mo + 1) * 128], mh[:, l, kc, :])
                                                                     for kc in range(NCH)]),
                                     reads=[("wr", s)] + mhk, writes=[("ps", b)])
                                copy_op(ev_eng(), KT[l][:, blk * 4 + mo, :], PS[b][:, 0:NMEM], [], [("ps", b), ("KT", l)])
                        for mc in range(2):
                            b = nb()
                            P.op("pe", mm_fn(PS[b][:, :], [(mh[:, l, kc, mc * 128:(mc + 1) * 128], wr[s][:, kc, :])
                                                             for kc in range(NCH)]),
                                 reads=[("wr", s)] + mhk, writes=[("ps", b)])
                            si = sc[0]
                            sc[0] = (si + 1) % 4
                            P.op("act", lambda e, b=b, si=si: e.activation(out=stg[si], in_=PS[b][:, :], func=AF.Copy),
                                 writes=[("ps", b), ("stg", si)])
                            if name == "w_v":
                                P.op("pool", lambda e, si=si, l=l, mc=mc, blk=blk: e.tensor_copy(
                                    out=VS[l][:, mc, blk * 512:(blk + 1) * 512], in_=stg[si]),
                                    reads=[("stg", si)], writes=[("VS", l)])
                            P.op("act", lambda e, si=si, l=l, mc=mc, blk=blk, dst=dst: e.dma_start(
                                out=dst[l, bi, mc * 128:(mc + 1) * 128, blk * 512:(blk + 1) * 512], in_=stg[si]),
                                reads=[("stg", si)], dsem=("stg", si))

        def run_tile(kind, bi, ti, last):
            if kind == "p":
                tg = TG(512, 1, 512)
                src = xp[bi, ti * 512:(ti + 1) * 512, :].rearrange("(tb p) d -> p tb d", p=128)
                ntb = 4
            else:
                tg = TG(128, 4, 32)
                src = xs.rearrange("(tb p) d -> p tb d", p=128)
                ntb = 1
            T = tg.T
            P.op("sp", lambda e: e.dma_start(out=xstage[:, 0:ntb, :], in_=src), writes=["xstage", "xstageV"], dsem="xst")
            for c in range(NCH):
                b = nb()
                P.op("pe", tr_fn([(PS[b][:, tb * 128:(tb + 1) * 128], xstage[:, tb, c * 128:(c + 1) * 128])
                                  for tb in range(ntb)]),
                     reads=["xstage", "ident"], writes=[("ps", b)])
                copy_op(ev_eng(), xT[:, c, 0:T], PS[b][:, 0:T], [], [("ps", b), ("xT", c)])
            for l in range(NL):
                first = (kind == "p" and ti == 0)
                if kind == "s":
                    load_sample_ffn_state(l)
                mix(tg, l, kind, first)
                attn(tg, l, kind)
                ffn(tg, l, kind, first)
                if last:
                    store_states(l, kind, bi)
            ystage = yv[:, :, :].rearrange("p c t -> p (c t)").rearrange("p (tb d) -> p tb d", d=D)
            for tb in range(ntb):
                for half in range(2):
                    b = nb()
                    P.op("pe", tr_fn([(PS[b][:, k * 128:(k + 1) * 128], xT[:, half * 4 + k, tb * 128:(tb + 1) * 128])
                                      for k in range(4)]),
                         reads=[("xT", half * 4 + k) for k in range(4)] + ["ident"], writes=[("ps", b)])
                    copy_op(ev_eng(), ystage[:, tb, half * 512:(half + 1) * 512], PS[b][:, :], [],
                            [("ps", b), ("ystg", tb, half)] + ([("yv", c) for c in range(NCH)] if (tb == 0 and half == 0) else []))
                if kind == "p":
                    dst = yp[bi, ti * 512 + tb * 128: ti * 512 + (tb + 1) * 128, :]
                else:
                    dst = ys[:, :]
                P.op("act", lambda e, tb=tb, dst=dst: e.dma_start(out=dst, in_=ystage[:, tb, :]),
                     reads=[("ystg", tb, 0), ("ystg", tb, 1)] + [("yv", c) for c in range(NCH)], dsem=("yst", tb))

        if do_sample:
            for l in range(NL):
                P.op("pool", lambda e, l=l: e.memset(poolh_s[l][:], 0.0), writes=[("poolhs", l, 0), ("poolhs", l, 1)])
                P.op("pool", lambda e, l=l: e.memset(convh_s[l][:], 0.0), writes=[("convhs", l)])
                P.op("pool", lambda e, l=l: e.memset(ffnh_s[l][:], 0.0), writes=[("ffnhs", l, w) for w in range(3)])
            run_tile("s", 0, 0, True)
            assert fp_cur[0] == len(fp_seq) and fp_emitted[0] == len(fp_seq)
            first_pass[0] = False
        for bi in range(NPB):
            kv_prep(bi)
            for l in range(NL):
                P.op("pool", lambda e, l=l: e.memset(poolh_p[l][:], 0.0), writes=[("poolh", l)])
                P.op("pool", lambda e, l=l: e.memset(convh_p[l][:], 0.0), writes=[("convh", l)])
                P.op("pool", lambda e, l=l: e.memset(ffnh_p[l][:], 0.0), writes=[("ffnh", l, c) for c in range(FCH)])
            for ti in range(NT):
                run_tile("p", bi, ti, ti == NT - 1)
        P.emit(nc)
    return nc, P


def _pack_consts(inp):
    cst = np.zeros((128, NCONST), np.float32)

    def put(col, vec):
        n = vec.shape[0] // 128
        cst[:, col:col + n] = vec.reshape(n, 128).T

    for l in range(L):
        base = l * CL
        put(base + G_MIXPRE, inp["g_mix_pre"][l])
        put(base + G_MIXPOST, inp["g_mix_post"][l])
        put(base + G_ATTNPRE, inp["g_attn_pre"][l])
        put(base + G_ATTNPOST, inp["g_attn_post"][l])
        put(base + G_MEM, inp["g_mem"][l])
        put(base + G_FFNPRE, inp["g_ffn_pre"][l])
        put(base + G_FFNPOST, inp["g_ffn_post"][l])
        put(base + POOLSC, inp["pool_scale"][l])
        for k in range(3):
            put(base + CONVW + k * 4, inp["conv_w"][l, k])
            put(base + FFNW + k * FCH, inp["ffn_conv_w"][l, k])
        put(base + CONVB, inp["conv_b"][l])
        put(base + FFNB, inp["ffn_conv_b"][l])
    cst[:, INVCNT:INVCNT + 15] = (1.0 / np.arange(1, 16, dtype=np.float64)).astype(np.float32)[None, :]
    cst[:, EPSC] = EPS
    return cst


_CACHE = {}


def kernel(**inputs):
    inp = {k: np.asarray(v) for k, v in inputs.items()}
    if "nc" not in _CACHE:
        _CACHE["nc"] = build_program()[0]
    nc = _CACHE["nc"]
    cst = _pack_consts(inp)
    shared = {
        "cst": cst,
        "w_in": np.ascontiguousarray(inp["w_in"]),
        "w_pool": np.ascontiguousarray(inp["w_pool"]),
        "w_out": np.ascontiguousarray(inp["w_out"]),
        "w_q": np.ascontiguousarray(inp["w_q"].reshape(L, D, D)),
        "w_k": np.ascontiguousarray(inp["w_k"].reshape(L, D, D)),
        "w_v": np.ascontiguousarray(inp["w_v"].reshape(L, D, D)),
        "w_o": np.ascontiguousarray(inp["w_o"].reshape(L, D, D)),
        "w_up": np.ascontiguousarray(inp["w_up"]),
        "w_down": np.ascontiguousarray(inp["w_down"]),
    }
    in_maps = []
    for i in range(NCORES):
        m = dict(shared)
        m["xp"] = np.ascontiguousarray(inp["x_prompt"][PB * i:PB * (i + 1)])
        m["xs"] = np.ascontiguousarray(inp["x_sample"][SB * i:SB * (i + 1)].reshape(SB * SSEQ, D))
        m["memp"] = np.ascontiguousarray(inp["mem_prompt"][PB * i:PB * (i + 1)])
        m["ck"] = np.ascontiguousarray(inp["cache_mem_k"][:, SB * i:SB * (i + 1)].reshape(L, SB, NMEM, D))
        m["cv"] = np.ascontiguousarray(inp["cache_mem_v"][:, SB * i:SB * (i + 1)].reshape(L, SB, NMEM, D))
        m["spool"] = np.ascontiguousarray(inp["state_pool"][:, SB * i:SB * (i + 1)])
        m["sconv"] = np.ascontiguousarray(inp["state_conv"][:, SB * i:SB * (i + 1)])
        m["sffn"] = np.ascontiguousarray(inp["state_ffn_conv"][:, SB * i:SB * (i + 1)])
        in_maps.append(m)
    res = run_bass_kernel_spmd(nc, in_maps, core_ids=list(range(NCORES)))
    R = res.results

    def cat(name, axis):
        return np.concatenate([np.asarray(r[name]) for r in R], axis=axis)

    y_prompt = cat("yp", 0)
    y_sample = cat("ys", 0).reshape(NCORES * SB, SSEQ, D)
    mem_k = cat("mk", 1).reshape(L, NCORES * PB, NMEM, 4, 256)
    mem_v = cat("mv", 1).reshape(L, NCORES * PB, NMEM, 4, 256)
    pool_p = cat("poolp", 1)
    conv_p = cat("convp", 1)
    ffn_p = cat("ffnp", 1)
    pool_s = cat("pools", 1)
    conv_s = cat("convs", 1)
    ffn_s = cat("ffns", 1)
    return (y_prompt, y_sample, mem_k, mem_v, pool_p, conv_p, ffn_p, pool_s, conv_s, ffn_s)
```

```python
import contextlib
import numpy as np
import concourse.bass as bass
import concourse.mybir as mybir
from concourse.bass_utils import run_bass_kernel_spmd

F32 = mybir.dt.float32
BF16 = mybir.dt.bfloat16
AF = mybir.ActivationFunctionType
ALU = mybir.AluOpType

ENGINES = ("pe", "act", "dve", "pool", "sp")

L = 2
D = 1024
SEQ = 2048
NMEM = 256
DFF = 2816
NCH = 8
FCH = 44
GCH = 22
NCORES = 8
PB = 2
SB = 4
SSEQ = 32
EPS = 1e-6

CL = 252
G_MIXPRE, G_MIXPOST, G_ATTNPRE, G_ATTNPOST, G_MEM, G_FFNPRE, G_FFNPOST = 0, 8, 16, 24, 32, 40, 48
POOLSC, CONVW, CONVB, FFNW, FFNB = 56, 60, 72, 76, 208
INVCNT = L * CL
EPSC = INVCNT + 15
NCONST = EPSC + 1
SEM_EPOCH = 1500


class Op:
    __slots__ = ("id", "eng", "fn", "deps", "signal", "dsem", "ev", "pos", "vc")

    def __init__(self, id, eng, fn, dsem):
        self.id = id
        self.eng = eng
        self.fn = fn
        self.deps = set()
        self.signal = False
        self.dsem = dsem
        self.ev = None
        self.pos = None
        self.vc = None


class Prog:
    def __init__(self):
        self.ops = []
        self.last_write = {}
        self.readers = {}
        self.last_dma = {}
        self.extra = ()

    def op(self, eng, fn, reads=(), writes=(), dsem=None, chain=True, noextra=False):
        o = Op(len(self.ops), eng, fn, dsem)
        if self.extra and not noextra:
            reads = list(reads) + list(self.extra)
        deps = o.deps
        lw = self.last_write
        rd = self.readers
        for k in reads:
            w = lw.get(k)
            if w is not None:
                deps.add(w)
        for k in writes:
            w = lw.get(k)
            if w is not None:
                deps.add(w)
            r = rd.get(k)
            if r:
                deps.update(r)
        for k in reads:
            rd.setdefault(k, []).append(o.id)
        for k in writes:
            lw[k] = o.id
            rd[k] = []
        if dsem is not None:
            p = self.last_dma.get(dsem)
            if p is not None and chain:
                deps.add(p)
            self.last_dma[dsem] = o.id
        deps.discard(o.id)
        self.ops.append(o)
        return o

    def emit(self, nc, final_wait_eng="sp", same_eng_skip=10 ** 9):
        ops = self.ops
        cnt = {e: 0 for e in ENGINES}
        for o in ops:
            o.pos = cnt[o.eng]
            cnt[o.eng] += 1
        for o in ops:
            if o.dsem is not None:
                continue
            drop = []
            for d in o.deps:
                od = ops[d]
                if od.eng == o.eng and od.dsem is None:
                    if o.eng == "pe" or (o.pos - od.pos) >= same_eng_skip:
                        drop.append(d)
            for d in drop:
                o.deps.discard(d)
        for o in ops:
            for d in o.deps:
                ops[d].signal = True
        ecnt = {e: 0 for e in ENGINES}
        dcnt = {}
        sem_names = set()
        for o in ops:
            if o.dsem is not None:
                dcnt[o.dsem] = dcnt.get(o.dsem, 0) + 16
                o.ev = (("d", o.dsem), dcnt[o.dsem])
                sem_names.add(("d", o.dsem))
            elif o.signal:
                k = ("e", o.eng, ecnt[o.eng] // SEM_EPOCH)
                o.ev = (k, ecnt[o.eng] % SEM_EPOCH + 1)
                ecnt[o.eng] += 1
                sem_names.add(k)
        sems = {}
        for k in sorted(sem_names, key=str):
            sems[k] = nc.alloc_semaphore(name=("s_" + "_".join(str(x) for x in k)).replace(" ", "").replace("'", "")
                                         .replace("(", "_").replace(")", "_").replace(",", "_"))
        clock = {e: {} for e in ENGINES}
        plan = {e: [] for e in ENGINES}
        for o in ops:
            ck = clock[o.eng]
            wd = {}
            for d in sorted(o.deps):
                od = ops[d]
                s, v = od.ev
                if ck.get(s, 0) >= v:
                    continue
                wd[s] = max(wd.get(s, 0), v)
                ck[s] = v
                for s2, v2 in od.vc.items():
                    if ck.get(s2, 0) < v2:
                        ck[s2] = v2
            o.vc = dict(ck)
            if o.ev is not None:
                o.vc[o.ev[0]] = max(o.vc.get(o.ev[0], 0), o.ev[1])
            plan[o.eng].append((o, list(wd.items())))
        finals = [(("d", k), v) for k, v in dcnt.items()]
        self.n_waits = sum(len(w) for e in ENGINES for _, w in plan[e])
        self.n_sems = len(sems)

        def run_engine(eng_name, eng):
            for o, waits in plan[eng_name]:
                for s, v in waits:
                    eng.wait_ge(sems[s], v)
                ins = o.fn(eng)
                if o.ev is not None:
                    s, v = o.ev
                    ins.then_inc(sems[s], 16 if o.dsem is not None else 1)
            if eng_name == final_wait_eng:
                for s, v in finals:
                    eng.wait_ge(sems[s], v)

        with nc.Block() as block:
            @block.tensor
            def _(e):
                run_engine("pe", e)

            @block.scalar
            def _(e):
                run_engine("act", e)

            @block.vector
            def _(e):
                run_engine("dve", e)

            @block.gpsimd
            def _(e):
                run_engine("pool", e)

            @block.sync
            def _(e):
                run_engine("sp", e)


class TG:
    def __init__(self, T, NS, SL):
        self.T, self.NS, self.SL = T, NS, SL


def build_program(NT=4, do_sample=True, NL=L, NPB=PB, NBUF=4):
    nc = bass.Bass("TRN2", target_bir_lowering=False)
    P = Prog()

    def din(name, shape, dt=F32):
        return nc.dram_tensor(name, list(shape), dt, kind="ExternalInput").ap()

    def dout(name, shape, dt=F32):
        return nc.dram_tensor(name, list(shape), dt, kind="ExternalOutput").ap()

    def dint(name, shape, dt=BF16):
        return nc.dram_tensor(name, list(shape), dt, kind="Internal").ap()

    xp = din("xp", [PB, SEQ, D])
    xs = din("xs", [SB * SSEQ, D])
    memp = din("memp", [PB, NMEM, D])
    ck = din("ck", [L, SB, NMEM, D])
    cv = din("cv", [L, SB, NMEM, D])
    spool = din("spool", [L, SB, 15, 512])
    sconv = din("sconv", [L, SB, 2, 512])
    sffn = din("sffn", [L, SB, 2, 2 * DFF])
    cst_d = din("cst", [128, NCONST])
    wshape = {"w_in": (D, 2048), "w_out": (D, D), "w_q": (D, D), "w_k": (D, D), "w_v": (D, D), "w_o": (D, D),
              "w_up": (D, 2 * DFF), "w_down": (DFF, D)}
    wf = {k: din(k, [L, v[0], v[1]]) for k, v in wshape.items()}
    wb = {k: dint(k + "_b", [L, v[0], v[1]]) for k, v in wshape.items()}
    w_pool_d = din("w_pool", [L, 4, 128, 128])

    yp = dout("yp", [PB, SEQ, D])
    ys = dout("ys", [SB * SSEQ, D])
    mk = dout("mk", [L, PB, NMEM, D])
    mv = dout("mv", [L, PB, NMEM, D])
    poolp = dout("poolp", [L, PB, 15, 512])
    convp = dout("convp", [L, PB, 2, 512])
    ffnp = dout("ffnp", [L, PB, 2, 2 * DFF])
    pools = dout("pools", [L, SB, 15, 512])
    convs = dout("convs", [L, SB, 2, 512])
    ffns = dout("ffns", [L, SB, 2, 2 * DFF])

    with contextlib.ExitStack() as st:
        def sb(name, shape, dt):
            return st.enter_context(nc.sbuf_tensor(name, shape, dt))

        ident = sb("ident", [128, 128], F32)
        onesb = sb("onesb", [128, 128], BF16)
        ones1 = sb("ones1", [128, 128], BF16)
        cst = sb("cst_sb", [128, NCONST], F32)
        wpool = sb("wpool", [128, L * 4, 128], BF16)
        poolh_p = [sb("poolh_p%d" % l, [128, 128], F32) for l in range(L)]
        convh_p = [sb("convh_p%d" % l, [128, 128], F32) for l in range(L)]
        ffnh_p = [sb("ffnh_p%d" % l, [128, 128], F32) for l in range(L)]
        poolh_s = [sb("poolh_s%d" % l, [128, 2, 128], F32) for l in range(L)]
        convh_s = [sb("convh_s%d" % l, [128, 128], F32) for l in range(L)]
        ffnh_s = [sb("ffnh_s%d" % l, [128, 3, 128], F32) for l in range(L)]
        xT = sb("xT", [128, NCH, 512], F32)
        hT = sb("hT", [128, NCH, 512], BF16)
        sq = sb("sq", [128, NCH, 512], BF16)
        rstd = sb("rstd", [128, 2, 512], F32)
        yv = sb("yv", [128, NCH, 512], F32)
        catq = sb("catq", [128, NCH, 512], BF16)
        PT = sb("PT", [128, 2, 2, 512], BF16)
        rden = sb("rden", [128, 2, 512], F32)
        KT = [sb("KT%d" % i, [128, NCH, NMEM], BF16) for i in range(2)]
        VS = [sb("VS%d" % i, [128, 2, D], BF16) for i in range(2)]
        xstage = sb("xstage", [128, 4, D], F32)
        wr = [sb("wr%d" % i, [128, 8, 512], BF16) for i in range(NBUF)]
        rowt = [sb("rowt%d" % i, [128, 512], F32) for i in range(4)]
        AW = 13568
        arena = sb("arena", [128, AW], F32)
        PS = [st.enter_context(nc.psum_tensor("ps%d" % i, [128, 512], F32)) for i in range(8)]

        def AV(off, n, dt=F32):
            if dt == F32:
                return arena[:, off:off + n]
            assert n % 2 == 0
            return arena[:, off:off + n // 2].bitcast(BF16)

        bank_ctr = [0]

        def nb():
            b = bank_ctr[0]
            bank_ctr[0] = (b + 1) % 8
            return b

        wslot = [0]
        rowc = [0]

        def next_row():
            r = rowc[0]
            rowc[0] = (r + 1) % 4
            return r

        evc = [0]

        def ev_eng():
            evc[0] ^= 1
            return "act" if evc[0] else "dve"

        scratch = sb("scratch", [128, 8], F32)

        def arena_phase():
            P.extra = ()
            P.op("pool", lambda e: e.memset(scratch[:, 0:1], 0.0), writes=["AP"])
            P.extra = ("AP",)

        def cc(l, off, n=1):
            return cst[:, l * CL + off: l * CL + off + n]

        def copy_op(eng, out, in_, reads, writes):
            if eng == "act":
                P.op("act", lambda e: e.activation(out=out, in_=in_, func=AF.Copy), reads=reads, writes=writes)
            else:
                P.op(eng, lambda e: e.tensor_copy(out=out, in_=in_), reads=reads, writes=writes)

        def mm_fn(bank_ap, pairs, start=True, stop=True):
            def f(e):
                last = None
                n = len(pairs)
                for i, (a, b) in enumerate(pairs):
                    last = e.matmul(bank_ap, lhsT=a, rhs=b, start=(start and i == 0), stop=(stop and i == n - 1))
                return last
            return f

        def tr_fn(items):
            def f(e):
                last = None
                for (o, i) in items:
                    last = e.transpose(out=o, in_=i, identity=ident[:])
                return last
            return f

        def mm_block_kc_outer(slot, banks, T, rhs_of, key_of, kn=NCH, start=True, stop=True, k0=0):
            for kk in range(kn):
                def f(e, kk=kk):
                    last = None
                    for mo, b in enumerate(banks):
                        last = e.matmul(PS[b][:, 0:T], lhsT=wr[slot][:, kk, mo * 128:(mo + 1) * 128], rhs=rhs_of(k0 + kk),
                                        start=(start and kk == 0), stop=(stop and kk == kn - 1))
                    return last
                P.op("pe", f, reads=[("wr", slot), key_of(k0 + kk)], writes=[("ps", b) for b in banks])

        P.op("pool", lambda e: e.memset(ident[:], 0.0), writes=["ident"])
        P.op("pool", lambda e: e.affine_select(out=ident[:], in_=ident[:], pattern=[[-1, 128]],
                                               compare_op=ALU.not_equal, fill=1.0, base=0, channel_multiplier=1),
             reads=["ident"], writes=["ident"])
        P.op("pool", lambda e: e.memset(onesb[:], 1.0 / D), writes=["onesb"])
        P.op("pool", lambda e: e.memset(ones1[:], 1.0), writes=["ones1"])
        for i in range(4):
            P.op("pool", lambda e, i=i: e.memset(rowt[i][:], 0.0), writes=[("rowt", i)])
        P.op("pool", lambda e: e.memset(scratch[:, 2:4], 1.0), writes=["scr2", "scr3"])
        P.op("sp", lambda e: e.dma_start(out=cst[:], in_=cst_d), writes=["cst"], dsem="cst")
        P.op("pool", lambda e: e.dma_start(out=wpool[:], in_=w_pool_d.rearrange("l g c d -> c (l g) d")),
             writes=["wpool"], dsem="wpool")

        def precast(name, l):
            rows = wshape[name][0]
            for rc in range(rows // 128):
                P.op("pool", lambda e, name=name, l=l, rc=rc: e.dma_start(
                    out=wb[name][l, rc * 128:(rc + 1) * 128, :], in_=wf[name][l, rc * 128:(rc + 1) * 128, :]),
                    writes=[("wb", name, l, rc)], dsem=("pc", name, l, rc // 8), chain=False)

        def wb_keys(name, l, k0=0, kn=None):
            rows = wshape[name][0] // 128
            if kn is None:
                kn = rows
            g0, g1 = k0 // 8, (k0 + kn - 1) // 8
            return [("wb", name, l, rc) for rc in range(g0 * 8, min(rows, (g1 + 1) * 8))]

        precasted = set()

        def ensure_precast(name, l):
            if (name, l) not in precasted:
                precasted.add((name, l))
                precast(name, l)

        def layer_block_seq(l):
            seq = []
            for blk in range(4):
                seq.append(("w_in", l, 0, 8, ((blk * 512, 512),)))
            for name in ("w_out", "w_q", "w_o"):
                for blk in range(2):
                    seq.append((name, l, 0, 8, ((blk * 512, 512),)))
            for q in range(5):
                seq.append(("w_up", l, 0, 8, ((q * 512, 512),)))
                seq.append(("w_up", l, 0, 8, ((DFF + q * 512, 512),)))
            seq.append(("w_up", l, 0, 8, ((20 * 128, 256), (DFF + 20 * 128, 256))))
            for half in range(2):
                for (k0, kn) in ((0, 8), (8, 8), (16, 6)):
                    seq.append(("w_down", l, k0, kn, ((half * 512, 512),)))
            return seq

        first_pass = [bool(do_sample)]
        fp_seq = [b for l in range(NL) for b in layer_block_seq(l)]
        fp_cur = [0]
        fp_emitted = [0]
        PD = 2

        def emit_block_dma(eng, name, l, k0, kn, runs, s):
            src = wb[name]
            off = 0
            first = None
            for ri, (c0, wd) in enumerate(runs):
                o = P.op(eng, lambda e, s=s, off=off, c0=c0, wd=wd: e.dma_start(
                    out=wr[s][:, 0:kn, off:off + wd],
                    in_=src[l, k0 * 128:(k0 + kn) * 128, c0:c0 + wd].rearrange("(k p) m -> p k m", p=128)),
                    reads=wb_keys(name, l, k0, kn), writes=[("wr", s)] if ri == 0 else [],
                    dsem=(("wrp" if eng == "pool" else "wr"), s), noextra=True, chain=(ri == 0))
                if ri == 0:
                    first = o
                else:
                    o.deps |= first.deps
                    P.last_write[("wr", s)] = o.id
                off += wd

        def load_block(name, l, k0, kn, runs):
            runs = tuple(runs)
            if first_pass[0]:
                idx = fp_cur[0]
                assert fp_seq[idx] == (name, l, k0, kn, runs), (idx, fp_seq[idx], (name, l, k0, kn, runs))
                while fp_emitted[0] <= min(idx + PD, len(fp_seq) - 1):
                    j = fp_emitted[0]
                    n2, l2, k02, kn2, runs2 = fp_seq[j]
                    ensure_precast(n2, l2)
                    emit_block_dma("pool", n2, l2, k02, kn2, runs2, j % NBUF)
                    fp_emitted[0] += 1
                    if fp_emitted[0] == len(fp_seq):
                        for ll in range(NL):
                            ensure_precast("w_k", ll)
                            ensure_precast("w_v", ll)
                fp_cur[0] += 1
                wslot[0] = fp_emitted[0] % NBUF
                return idx % NBUF
            ensure_precast(name, l)
            s = wslot[0]
            wslot[0] = (s + 1) % NBUF
            emit_block_dma("sp", name, l, k0, kn, runs, s)
            return s

        def sumsq_rstd(tg, sq_keys, ri):
            T = tg.T
            b = nb()
            for c in range(NCH):
                P.op("pe", mm_fn(PS[b][:, 0:T], [(onesb[:], sq[:, c, 0:T])], start=(c == 0), stop=(c == NCH - 1)),
                     reads=[("sq", c), "onesb"], writes=[("ps", b)])
            P.op("act", lambda e: e.activation(out=rstd[:, ri, 0:T], in_=PS[b][:, 0:T], func=AF.Ln,
                                               bias=cst[:, EPSC:EPSC + 1], scale=1.0),
                 reads=["cst"], writes=[("ps", b), ("rstd", ri)])
            P.op("act", lambda e: e.activation(out=rstd[:, ri, 0:T], in_=rstd[:, ri, 0:T], func=AF.Exp, scale=-0.5),
                 reads=[("rstd", ri)], writes=[("rstd", ri)])

        def pre_norm(tg, l, goff):
            T = tg.T
            for c in range(NCH):
                P.op("act", lambda e, c=c: e.activation(out=sq[:, c, 0:T], in_=xT[:, c, 0:T], func=AF.Square),
                     reads=[("xT", c)], writes=[("sq", c)])
            sumsq_rstd(tg, None, 0)
            for c in range(NCH):
                P.op("dve", lambda e, c=c: e.scalar_tensor_tensor(out=hT[:, c, 0:T], in0=xT[:, c, 0:T],
                                                                   scalar=cc(l, goff + c), in1=rstd[:, 0, 0:T],
                                                                   op0=ALU.mult, op1=ALU.mult),
                     reads=[("xT", c), ("rstd", 0), "cst"], writes=[("hT", c)])

        def pre_norm_deferred(tg, l, goff, need_rr=False):
            T = tg.T
            for c in range(NCH):
                P.op("act", lambda e, c=c: e.activation(out=hT[:, c, 0:T], in_=xT[:, c, 0:T], func=AF.Identity,
                                                        scale=cc(l, goff + c)),
                     reads=[("xT", c), "cst"], writes=[("hT", c)])
            for c in range(NCH):
                P.op("act", lambda e, c=c: e.activation(out=sq[:, c, 0:T], in_=xT[:, c, 0:T], func=AF.Square),
                     reads=[("xT", c)], writes=[("sq", c)])

        def pre_norm_deferred2(tg, need_rr=False):
            T = tg.T
            sumsq_rstd(tg, None, 0)
            if need_rr:
                P.op("dve", lambda e: e.tensor_tensor(out=rden[:, 1, 0:T], in0=rstd[:, 0, 0:T], in1=rstd[:, 0, 0:T],
                                                      op=ALU.mult),
                     reads=[("rstd", 0)], writes=[("rden", 1)])

        def evac_y(tg, b, c, l, goff):
            T = tg.T
            P.op("act", lambda e: e.activation(out=yv[:, c, 0:T], in_=PS[b][:, 0:T], func=AF.Identity,
                                               scale=cc(l, goff + c)),
                 reads=["cst"], writes=[("ps", b), ("yv", c)])
            P.op("act", lambda e: e.activation(out=sq[:, c, 0:T], in_=PS[b][:, 0:T], func=AF.Square),
                 writes=[("ps", b), ("sq", c)])

        def post_norm(tg, l, goff):
            T = tg.T
            sumsq_rstd(tg, None, 1)
            for c in range(NCH):
                P.op("dve", lambda e, c=c: e.tensor_tensor(out=yv[:, c, 0:T], in0=yv[:, c, 0:T], in1=rstd[:, 1, 0:T],
                                                           op=ALU.mult),
                     reads=[("yv", c), ("rstd", 1)], writes=[("yv", c)])
                P.op("dve", lambda e, c=c: e.tensor_tensor(out=xT[:, c, 0:T], in0=yv[:, c, 0:T], in1=xT[:, c, 0:T],
                                                           op=ALU.add),
                     reads=[("yv", c), ("xT", c)], writes=[("xT", c)])

        def dense_1024(tg, name, l, rhs_buf, rhs_key, evac, hook=None):
            T = tg.T
            for blk in range(2):
                s = load_block(name, l, 0, 8, [(blk * 512, 512)])
                banks = []
                if blk == 0:
                    banks = [nb() for _ in range(4)]
                    mm_block_kc_outer(s, banks, T, lambda kc: rhs_buf[:, kc, 0:T], lambda kc: (rhs_key, kc))
                else:
                    for mo in range(4):
                        b = nb()
                        banks.append(b)
                        P.op("pe", mm_fn(PS[b][:, 0:T], [(wr[s][:, kc, mo * 128:(mo + 1) * 128], rhs_buf[:, kc, 0:T])
                                                           for kc in range(NCH)]),
                             reads=[("wr", s)] + [(rhs_key, kc) for kc in range(NCH)], writes=[("ps", b)])
                if blk == 0 and hook is not None:
                    hook()
                for mo in range(4):
                    evac(banks[mo], blk * 4 + mo)

        def mix(tg, l, kind, first):
            T, NS, SL = tg.T, tg.NS, tg.SL
            EL = 15 + SL
            uA = AV(0, 4 * NS * EL).rearrange("p (g s e) -> p g s e", g=4, s=NS)
            Bg = AV(2112, 4 * T).rearrange("p (j t) -> p j t", j=4)
            CVb = AV(4160, 4 * NS * (2 + SL)).rearrange("p (j s e) -> p j s e", j=4, s=NS)
            pa = [AV(6224 + g * 528, NS * EL).rearrange("p (s e) -> p s e", s=NS) for g in range(4)]
            pb = [AV(6224 + (4 + g) * 528, NS * EL).rearrange("p (s e) -> p s e", s=NS) for g in range(4)]
            dd = AV(10448, 4 * T, BF16).rearrange("p (g t) -> p g t", g=4)
            ta = AV(11472, 4 * T).rearrange("p (j t) -> p j t", j=4)

            def v3(ap2d):
                return ap2d.rearrange("p (s l) -> p s l", s=NS)

            pre_norm_deferred(tg, l, G_MIXPRE)
            r3 = v3(rstd[:, 0, 0:T])
            rr3 = v3(rden[:, 1, 0:T])
            s0 = load_block("w_in", l, 0, 8, [(0, 512)])
            banks0 = [nb() for _ in range(4)]
            mm_block_kc_outer(s0, banks0, T, lambda kc: hT[:, kc, 0:T], lambda kc: ("hT", kc))
            pre_norm_deferred2(tg, need_rr=True)
            arena_phase()
            if kind == "p":
                P.op("pool", lambda e: e.tensor_copy(out=uA[:, :, 0, 0:15],
                                                     in_=poolh_p[l][:, 0:60].rearrange("p (g r) -> p g r", g=4)),
                     reads=[("poolh", l)], writes=[("uA", g) for g in range(4)])
                P.op("pool", lambda e: e.tensor_copy(out=CVb[:, :, 0, 0:2],
                                                     in_=convh_p[l][:, 0:8].rearrange("p (j r) -> p j r", j=4)),
                     reads=[("convh", l)], writes=[("CV", j) for j in range(4)])
            else:
                r = next_row()
                P.op("sp", lambda e, r=r: e.dma_start(out=rowt[r][0:60, :], in_=spool[l].rearrange("s r d -> (s r) d")),
                     reads=[("rowt", r)], writes=[("rowt", r)], dsem=("rowt", r))
                b = nb()
                P.op("pe", tr_fn([(PS[b][:, g * 128:(g + 1) * 128], rowt[r][:, g * 128:(g + 1) * 128]) for g in range(4)]),
                     reads=[("rowt", r), "ident"], writes=[("ps", b)])
                P.op("dve", lambda e, b=b: e.tensor_copy(
                    out=uA[:, :, :, 0:15],
                    in_=PS[b][:, :].rearrange("p (g q) -> p g q", q=128)[:, :, 0:60].rearrange("p g (s r) -> p g s r", s=4)),
                    writes=[("ps", b)] + [("uA", g) for g in range(4)])
                r2 = next_row()
                P.op("sp", lambda e, r2=r2: e.dma_start(out=rowt[r2][0:8, :], in_=sconv[l].rearrange("s r d -> (s r) d")),
                     reads=[("rowt", r2)], writes=[("rowt", r2)], dsem=("rowt", r2))
                b2 = nb()
                P.op("pe", tr_fn([(PS[b2][:, j * 128:(j + 1) * 128], rowt[r2][:, j * 128:(j + 1) * 128]) for j in range(4)]),
                     reads=[("rowt", r2), "ident"], writes=[("ps", b2)])
                P.op("dve", lambda e, b2=b2: e.tensor_copy(
                    out=CVb[:, :, :, 0:2],
                    in_=PS[b2][:, :].rearrange("p (j q) -> p j q", q=128)[:, :, 0:8].rearrange("p j (s r) -> p j s r", s=4)),
                    writes=[("ps", b2)] + [("CV", j) for j in range(4)])
            def pool_section():
                lo1 = [15, 13, 9, 1]
                for g in range(4):
                    lo = lo1[g]
                    P.op("pool", lambda e, g=g, lo=lo: e.tensor_tensor(out=pa[g][:, :, lo:EL], in0=uA[:, g, :, lo:EL],
                                                                       in1=uA[:, g, :, lo - 1:EL - 1], op=ALU.add),
                         reads=[("uA", g)], writes=[("pa", g)])
                lo2 = [None, 15, 11, 3]
                for g in range(1, 4):
                    lo = lo2[g]
                    P.op("pool", lambda e, g=g, lo=lo: e.tensor_tensor(out=pb[g][:, :, lo:EL], in0=pa[g][:, :, lo:EL],
                                                                       in1=pa[g][:, :, lo - 2:EL - 2], op=ALU.add),
                         reads=[("pa", g)], writes=[("pb", g)])
                lo3 = [None, None, 15, 7]
                for g in range(2, 4):
                    lo = lo3[g]
                    P.op("pool", lambda e, g=g, lo=lo: e.tensor_tensor(out=pa[g][:, :, lo:EL], in0=pb[g][:, :, lo:EL],
                                                                       in1=pb[g][:, :, lo - 4:EL - 4], op=ALU.add),
                         reads=[("pb", g), ("pa", g)], writes=[("pa", g)])
                P.op("pool", lambda e: e.tensor_tensor(out=pb[3][:, :, 15:EL], in0=pa[3][:, :, 15:EL],
                                                       in1=pa[3][:, :, 7:EL - 8], op=ALU.add),
                     reads=[("pa", 3), ("pb", 3)], writes=[("pb", 3)])
                fin = [pa[0], pb[1], pa[2], pb[3]]
                fkey = [("pa", 0), ("pb", 1), ("pa", 2), ("pb", 3)]
                for g in range(4):
                    w = 2 << g
                    P.op("dve", lambda e, g=g, w=w: e.scalar_tensor_tensor(
                        out=v3(dd[:, g, 0:T]), in0=fin[g][:, :, 15:EL], scalar=1.0 / w, in1=uA[:, g, :, 15:EL],
                        op0=ALU.mult, op1=ALU.subtract),
                        reads=[fkey[g], ("uA", g)], writes=[("dd", g)])
                    if first:
                        n = w - 1
                        P.op("dve", lambda e, g=g, n=n: e.tensor_tensor(out=fin[g][:, 0, 15:15 + n], in0=fin[g][:, 0, 15:15 + n],
                                                                        in1=cst[:, INVCNT:INVCNT + n], op=ALU.mult),
                             reads=[fkey[g], "cst", ("dd", g)], writes=[fkey[g]])
                        P.op("dve", lambda e, g=g, n=n: e.tensor_tensor(out=dd[:, g, 0:n], in0=fin[g][:, 0, 15:15 + n],
                                                                        in1=uA[:, g, 0, 15:15 + n], op=ALU.subtract),
                             reads=[fkey[g], ("uA", g)], writes=[("dd", g)])
                if kind == "p":
                    P.op("pool", lambda e: e.tensor_copy(out=poolh_p[l][:, 0:60].rearrange("p (g r) -> p g r", g=4),
                                                         in_=uA[:, :, 0, SL:SL + 15]),
                         reads=[("uA", g) for g in range(4)], writes=[("poolh", l)])
                else:
                    for w in range(2):
                        P.op("pool", lambda e, w=w: e.tensor_copy(
                            out=poolh_s[l][:, w, 0:120].rearrange("p (gl s r) -> p gl s r", gl=2, s=4),
                            in_=uA[:, 2 * w:2 * w + 2, :, SL:SL + 15]),
                            reads=[("uA", 2 * w), ("uA", 2 * w + 1)], writes=[("poolhs", l, w)])

            for blk in range(4):
                if blk > 0:
                    s = load_block("w_in", l, 0, 8, [(blk * 512, 512)])
                for mo in range(4):
                    if blk == 0:
                        b = banks0[mo]
                    else:
                        b = nb()
                        P.op("pe", mm_fn(PS[b][:, 0:T], [(wr[s][:, kc, mo * 128:(mo + 1) * 128], hT[:, kc, 0:T])
                                                           for kc in range(NCH)]),
                             reads=[("wr", s)] + [("hT", kc) for kc in range(NCH)], writes=[("ps", b)])
                    if blk == 0:
                        P.op("dve", lambda e, b=b, mo=mo: e.tensor_tensor(out=uA[:, mo, :, 15:15 + SL], in0=v3(PS[b][:, 0:T]),
                                                                          in1=r3, op=ALU.mult),
                             reads=[("rstd", 0)], writes=[("ps", b), ("uA", mo)])
                    elif blk == 1:
                        P.op("dve", lambda e, b=b, mo=mo: e.tensor_tensor(out=Bg[:, mo, :], in0=PS[b][:, 0:T],
                                                                          in1=rstd[:, 0, 0:T], op=ALU.mult),
                             reads=[("rstd", 0)], writes=[("ps", b), ("Bg", mo)])
                    elif blk == 2:
                        P.op("dve", lambda e, b=b, mo=mo: e.tensor_tensor(out=CVb[:, mo, :, 2:2 + SL], in0=v3(PS[b][:, 0:T]),
                                                                          in1=rr3, op=ALU.mult),
                             reads=[("rden", 1)], writes=[("ps", b), ("CV", mo)])
                    else:
                        P.op("dve", lambda e, b=b, mo=mo: e.tensor_tensor(out=CVb[:, mo, :, 2:2 + SL], in0=v3(PS[b][:, 0:T]),
                                                                          in1=CVb[:, mo, :, 2:2 + SL], op=ALU.mult),
                             writes=[("ps", b), ("CV", mo)])
                if blk == 0:
                    pool_section()
            for g in range(4):
                b = nb()
                P.op("pe", mm_fn(PS[b][:, 0:T], [(wpool[:, l * 4 + g, :], dd[:, g, 0:T])]),
                     reads=["wpool", ("dd", g)], writes=[("ps", b)])
                P.op("act", lambda e, b=b, g=g: e.activation(out=catq[:, g, 0:T], in_=PS[b][:, 0:T], func=AF.Identity,
                                                             scale=cc(l, POOLSC + g)),
                     reads=["cst"], writes=[("ps", b), ("catq", g)])
            for j in range(4):
                P.op("pool", lambda e, j=j: e.tensor_scalar(out=v3(ta[:, j, 0:T]), in0=CVb[:, j, :, 2:2 + SL],
                                                            scalar1=cc(l, CONVW + 2 * 4 + j), scalar2=cc(l, CONVB + j),
                                                            op0=ALU.mult, op1=ALU.add),
                     reads=[("CV", j), "cst"], writes=[("ta", j)])
            for j in range(4):
                P.op("dve", lambda e, j=j: e.scalar_tensor_tensor(out=v3(ta[:, j, 0:T]), in0=CVb[:, j, :, 1:1 + SL],
                                                                   scalar=cc(l, CONVW + 1 * 4 + j), in1=v3(ta[:, j, 0:T]),
                                                                   op0=ALU.mult, op1=ALU.add),
                     reads=[("CV", j), ("ta", j), "cst"], writes=[("ta", j)])
            for j in range(4):
                P.op("dve", lambda e, j=j: e.scalar_tensor_tensor(out=v3(ta[:, j, 0:T]), in0=CVb[:, j, :, 0:SL],
                                                                   scalar=cc(l, CONVW + 0 * 4 + j), in1=v3(ta[:, j, 0:T]),
                                                                   op0=ALU.mult, op1=ALU.add),
                     reads=[("CV", j), ("ta", j), "cst"], writes=[("ta", j)])
            for j in range(4):
                P.op("pool", lambda e, j=j: e.tensor_tensor(out=catq[:, 4 + j, 0:T], in0=ta[:, j, 0:T], in1=Bg[:, j, :],
                                                            op=ALU.mult),
                     reads=[("ta", j), ("Bg", j)], writes=[("catq", 4 + j)])
            if kind == "p":
                P.op("pool", lambda e: e.tensor_copy(out=convh_p[l][:, 0:8].rearrange("p (j r) -> p j r", j=4),
                                                     in_=CVb[:, :, 0, SL:SL + 2]),
                     reads=[("CV", j) for j in range(4)], writes=[("convh", l)])
            else:
                P.op("pool", lambda e: e.tensor_copy(out=convh_s[l][:, 0:32].rearrange("p (j s r) -> p j s r", j=4, s=4),
                                                     in_=CVb[:, :, :, SL:SL + 2]),
                     reads=[("CV", j) for j in range(4)], writes=[("convhs", l)])
            dense_1024(tg, "w_out", l, catq, "catq", lambda b, c: evac_y(tg, b, c, l, G_MIXPOST))
            post_norm(tg, l, G_MIXPOST)

        def load_sample_kv(l, s):
            slot = s % 2
            P.op("sp", lambda e: e.dma_start(out=xstage[:, 0:2, :], in_=ck[l, s].rearrange("(m p) d -> p m d", p=128)),
                 writes=["xstage"], dsem="xst")
            P.op("sp", lambda e: e.dma_start(out=xstage[:, 2:4, :], in_=cv[l, s].rearrange("(m p) d -> p m d", p=128)),
                 writes=["xstageV"], dsem="xstv")
            P.op("pool", lambda e: e.tensor_copy(out=VS[slot][:], in_=xstage[:, 2:4, :]),
                 reads=["xstageV"], writes=[("VS", slot)])
            for hc2 in range(4):
                b = nb()
                items = []
                for i in range(2):
                    hc = hc2 * 2 + i
                    for mc in range(2):
                        items.append((PS[b][:, i * 256 + mc * 128: i * 256 + (mc + 1) * 128],
                                      xstage[:, mc, hc * 128:(hc + 1) * 128]))
                P.op("pe", tr_fn(items), reads=["xstage", "ident"], writes=[("ps", b)])
                copy_op(ev_eng(), KT[slot][:, hc2 * 2:hc2 * 2 + 2, :], PS[b][:, :].rearrange("p (i m) -> p i m", i=2),
                        [], [("ps", b), ("KT", slot)])

        def attn(tg, l, kind):
            T, NS, SL = tg.T, tg.NS, tg.SL
            pre_norm_deferred(tg, l, G_ATTNPRE)

            def evq(b, c):
                P.op("dve", lambda e: e.tensor_tensor(out=catq[:, c, 0:T], in0=PS[b][:, 0:T], in1=rstd[:, 0, 0:T],
                                                      op=ALU.mult),
                     reads=[("rstd", 0)], writes=[("ps", b), ("catq", c)])
            dense_1024(tg, "w_q", l, hT, "hT", evq, hook=lambda: pre_norm_deferred2(tg))
            units = []
            for s in range(NS):
                for hd in range(4):
                    units.append((s, hd))

            def geom(s):
                if kind == "p":
                    return l, 0, T
                return s % 2, s * SL, SL

            def stage_a(u, s, hd):
                slot, c0, cn = geom(s)
                if kind != "p" and hd == 0:
                    load_sample_kv(l, s)
                pti = u % 2
                pt = PT[:, pti]
                ptk = ("PT", pti)
                for mc in range(2):
                    b = nb()
                    P.op("pe", mm_fn(PS[b][:, 0:cn], [(KT[slot][:, 2 * hd + ec, mc * 128:(mc + 1) * 128],
                                                        catq[:, 2 * hd + ec, c0:c0 + cn]) for ec in range(2)]),
                         reads=[("KT", slot), ("catq", 2 * hd), ("catq", 2 * hd + 1)], writes=[("ps", b)])
                    P.op("act", lambda e, b=b, mc=mc, pt=pt, cn=cn: e.activation(out=pt[:, mc, 0:cn], in_=PS[b][:, 0:cn],
                                                                                 func=AF.Exp, scale=1.0 / 16.0),
                         writes=[("ps", b), ptk])

            def stage_b(u, s, hd):
                slot, c0, cn = geom(s)
                pti = u % 2
                pt = PT[:, pti]
                ptk = ("PT", pti)
                rk = ("rden", pti)
                rd = rden[:, pti, 0:cn]
                b = nb()
                P.op("pe", mm_fn(PS[b][:, 0:cn], [(ones1[:], pt[:, mc, 0:cn]) for mc in range(2)]),
                     reads=[ptk, "ones1"], writes=[("ps", b)])
                P.op("act", lambda e, b=b, rd=rd, cn=cn: e.activation(out=rd, in_=PS[b][:, 0:cn], func=AF.Ln),
                     writes=[("ps", b), rk])
                P.op("act", lambda e, rd=rd: e.activation(out=rd, in_=rd, func=AF.Exp, scale=-1.0),
                     reads=[rk], writes=[rk])
                for ec in range(2):
                    b = nb()
                    P.op("pe", mm_fn(PS[b][:, 0:cn], [(VS[slot][:, mc, hd * 256 + ec * 128: hd * 256 + (ec + 1) * 128],
                                                        pt[:, mc, 0:cn]) for mc in range(2)]),
                         reads=[("VS", slot), ptk], writes=[("ps", b)])
                    P.op("dve", lambda e, b=b, ec=ec, rd=rd, hd=hd, c0=c0, cn=cn: e.tensor_tensor(
                        out=hT[:, 2 * hd + ec, c0:c0 + cn], in0=PS[b][:, 0:cn], in1=rd, op=ALU.mult),
                        reads=[rk], writes=[("ps", b), ("hT", 2 * hd + ec)])

            for u, (s, hd) in enumerate(units):
                stage_a(u, s, hd)
                if u >= 1:
                    stage_b(u - 1, *units[u - 1])
            stage_b(len(units) - 1, *units[-1])
            dense_1024(tg, "w_o", l, hT, "hT", lambda b, c: evac_y(tg, b, c, l, G_ATTNPOST))
            post_norm(tg, l, G_ATTNPOST)

        def ffn(tg, l, kind, first):
            T, NS, SL = tg.T, tg.NS, tg.SL
            act = AV(0, GCH * T, BF16).rearrange("p (j t) -> p j t", j=GCH)
            U = [AV(5632 + i * 516, NS * (2 + SL)).rearrange("p (s e) -> p s e", s=NS) for i in range(4)]
            tb = [AV(7700 + i * 512, T) for i in range(4)]
            sg = [AV(9760 + i * 512, T) for i in range(4)]
            uc = [0]

            def v3(ap2d):
                return ap2d.rearrange("p (s l) -> p s l", s=NS)

            pre_norm(tg, l, G_FFNPRE)
            arena_phase()

            def conv_chunk(b, c):
                i = uc[0]
                uc[0] = (i + 1) % 4
                if kind == "p":
                    hin = ffnh_p[l][:, c * 2:c * 2 + 2]
                    hk = ("ffnh", l, c)
                    P.op("pool", lambda e: e.tensor_copy(out=U[i][:, 0, 0:2], in_=hin), reads=[hk, ("U", i)], writes=[("Uh", i)])
                else:
                    hin = ffnh_s[l][:, c // 16, (c % 16) * 8:(c % 16) * 8 + 8].rearrange("p (s r) -> p s r", s=4)
                    hk = ("ffnhs", l, c // 16)
                    P.op("pool", lambda e: e.tensor_copy(out=U[i][:, :, 0:2], in_=hin), reads=[hk, ("U", i)], writes=[("Uh", i)])
                P.op("act", lambda e: e.activation(out=U[i][:, :, 2:2 + SL], in_=v3(PS[b][:, 0:T]), func=AF.Copy),
                     reads=[("Uh", i)], writes=[("ps", b), ("U", i)])
                if c % 2 == 0:
                    P.op("act", lambda e: e.activation(out=tb[i], in_=PS[b][:, 0:T], func=AF.Identity,
                                                       scale=cc(l, FFNW + 2 * FCH + c), bias=cc(l, FFNB + c)),
                         reads=["cst"], writes=[("ps", b), ("tb", i)])
                else:
                    P.op("dve", lambda e: e.tensor_scalar(out=v3(tb[i]), in0=U[i][:, :, 2:2 + SL],
                                                          scalar1=cc(l, FFNW + 2 * FCH + c), scalar2=cc(l, FFNB + c),
                                                          op0=ALU.mult, op1=ALU.add),
                         reads=["cst", ("U", i)], writes=[("tb", i)])
                P.op("dve", lambda e: e.scalar_tensor_tensor(out=v3(tb[i]), in0=U[i][:, :, 1:1 + SL],
                                                              scalar=cc(l, FFNW + 1 * FCH + c), in1=v3(tb[i]),
                                                              op0=ALU.mult, op1=ALU.add),
                     reads=[("U", i), ("Uh", i), ("tb", i), "cst"], writes=[("tb", i)])
                P.op("dve", lambda e: e.scalar_tensor_tensor(out=v3(tb[i]), in0=U[i][:, :, 0:SL],
                                                              scalar=cc(l, FFNW + 0 * FCH + c), in1=v3(tb[i]),
                                                              op0=ALU.mult, op1=ALU.add),
                     reads=[("U", i), ("Uh", i), ("tb", i), "cst"], writes=[("tb", i)])
                if kind == "p":
                    P.op("pool", lambda e: e.tensor_copy(out=ffnh_p[l][:, c * 2:c * 2 + 2], in_=U[i][:, 0, SL:SL + 2]),
                         reads=[("U", i)], writes=[hk])
                else:
                    P.op("pool", lambda e: e.tensor_copy(
                        out=ffnh_s[l][:, c // 16, (c % 16) * 8:(c % 16) * 8 + 8].rearrange("p (s r) -> p s r", s=4),
                        in_=U[i][:, :, SL:SL + 2]),
                        reads=[("U", i)], writes=[hk])
                return i

            pending = []
            DELAY = 2

            def flush(n):
                while len(pending) > n:
                    pending.pop(0)()

            def gate_chunk(b, j, gi):
                i = conv_chunk(b, j)
                pending.append(lambda: P.op("act", lambda e: e.activation(out=sg[gi], in_=tb[i], func=AF.Silu),
                                            reads=[("tb", i)], writes=[("sg", gi)]))
                flush(DELAY)

            def val_chunk(b, j, gi):
                i = conv_chunk(b, GCH + j)
                pending.append(lambda: P.op("pool", lambda e: e.tensor_tensor(out=act[:, j, 0:T], in0=tb[i], in1=sg[gi],
                                                                              op=ALU.mult),
                                            reads=[("tb", i), ("sg", gi)], writes=[("act", j)]))
                flush(DELAY)

            def up_block(runs, chunks, first_blk=False):
                s = load_block("w_up", l, 0, 8, runs)
                if first_blk:
                    banks = [nb() for _ in range(4)]
                    mm_block_kc_outer(s, banks, T, lambda kc: hT[:, kc, 0:T], lambda kc: ("hT", kc))
                for mo, (kd, j, gi) in enumerate(chunks):
                    if first_blk:
                        b = banks[mo]
                    else:
                        b = nb()
                        P.op("pe", mm_fn(PS[b][:, 0:T], [(wr[s][:, kc, mo * 128:(mo + 1) * 128], hT[:, kc, 0:T])
                                                           for kc in range(NCH)]),
                             reads=[("wr", s)] + [("hT", kc) for kc in range(NCH)], writes=[("ps", b)])
                    if kd == "g":
                        gate_chunk(b, j, gi)
                    else:
                        val_chunk(b, j, gi)

            for q in range(5):
                up_block([(q * 512, 512)], [("g", q * 4 + i, i) for i in range(4)], first_blk=(q == 0))
                up_block([(DFF + q * 512, 512)], [("v", q * 4 + i, i) for i in range(4)])
            up_block([(20 * 128, 256), (DFF + 20 * 128, 256)], [("g", 20, 0), ("g", 21, 1), ("v", 20, 0), ("v", 21, 1)])
            flush(0)
            P.op("act", lambda e: e.activation(out=scratch[:, 3:4], in_=scratch[:, 2:3], func=AF.Ln),
                 reads=["scr2"], writes=["scr3"])
            for half in range(2):
                banks = [nb() for _ in range(4)]
                kgroups = [(0, 8), (8, 8), (16, 6)]
                for gi, (k0, kn) in enumerate(kgroups):
                    s = load_block("w_down", l, k0, kn, [(half * 512, 512)])
                    if half == 0:
                        mm_block_kc_outer(s, banks, T, lambda kc: act[:, kc, 0:T], lambda kc: ("act", kc), kn=kn,
                                          start=(gi == 0), stop=(gi == 2), k0=k0)
                        continue
                    for mo in range(4):
                        b = banks[mo]
                        P.op("pe", mm_fn(PS[b][:, 0:T], [(wr[s][:, kk, mo * 128:(mo + 1) * 128], act[:, k0 + kk, 0:T])
                                                           for kk in range(kn)], start=(gi == 0), stop=(gi == 2)),
                             reads=[("wr", s)] + [("act", k0 + kk) for kk in range(kn)], writes=[("ps", b)])
                for mo in range(4):
                    evac_y(tg, banks[mo], half * 4 + mo, l, G_FFNPOST)
            post_norm(tg, l, G_FFNPOST)

        def store_states(l, kind, b_idx):
            def tr_store(src_ap, key, nvalid, dst):
                b = nb()
                P.op("pe", tr_fn([(PS[b][:, 0:128], src_ap)]), reads=[key, "ident"], writes=[("ps", b)])
                r = next_row()
                P.op("act", lambda e: e.activation(out=rowt[r][:, 0:128], in_=PS[b][:, 0:128], func=AF.Copy),
                     writes=[("ps", b), ("rowt", r)])
                P.op("act", lambda e: e.dma_start(out=dst, in_=rowt[r][0:nvalid, 0:128]), reads=[("rowt", r)],
                     dsem=("rowst", r))
                P.op("pool", lambda e: e.memset(rowt[r][:, 0:128], 0.0), reads=[], writes=[("rowt", r)])
            if kind == "p":
                tr_store(poolh_p[l][:, :], ("poolh", l), 60,
                         poolp[l, b_idx].rearrange("r (g p) -> g r p", p=128))
                tr_store(convh_p[l][:, :], ("convh", l), 8,
                         convp[l, b_idx].rearrange("r (j p) -> j r p", p=128))
                b = nb()
                P.op("pe", tr_fn([(PS[b][:, 0:128], ffnh_p[l][:, :])]),
                     reads=[("ffnh", l, c) for c in range(FCH)] + ["ident"], writes=[("ps", b)])
                r = next_row()
                P.op("act", lambda e: e.activation(out=rowt[r][:, 0:128], in_=PS[b][:, 0:128], func=AF.Copy),
                     writes=[("ps", b), ("rowt", r)])
                P.op("act", lambda e: e.dma_start(out=ffnp[l, b_idx].rearrange("r (c p) -> c r p", p=128),
                                                  in_=rowt[r][0:88, 0:128]), reads=[("rowt", r)], dsem=("rowst", r))
                P.op("pool", lambda e: e.memset(rowt[r][:, 0:128], 0.0), reads=[], writes=[("rowt", r)])
            else:
                for w in range(2):
                    tr_store(poolh_s[l][:, w, :], ("poolhs", l, w), 120,
                             pools[l].rearrange("s r (g p) -> g s r p", p=128)[2 * w:2 * w + 2])
                tr_store(convh_s[l][:, :], ("convhs", l), 32,
                         convs[l].rearrange("s r (j p) -> j s r p", p=128))
                for w in range(3):
                    ncl = 16 if w < 2 else 12
                    tr_store(ffnh_s[l][:, w, :], ("ffnhs", l, w), ncl * 8,
                             ffns[l].rearrange("s r (c p) -> c s r p", p=128)[16 * w:16 * w + ncl])

        def load_sample_ffn_state(l):
            for grp in range(11):
                r = next_row()
                P.op("sp", lambda e, r=r, grp=grp: e.dma_start(
                    out=rowt[r][0:8, :], in_=sffn[l].rearrange("s r d -> (s r) d")[:, grp * 512:(grp + 1) * 512]),
                    reads=[("rowt", r)], writes=[("rowt", r)], dsem=("rowt", r))
                b = nb()
                P.op("pe", tr_fn([(PS[b][:, i * 128:(i + 1) * 128], rowt[r][:, i * 128:(i + 1) * 128]) for i in range(4)]),
                     reads=[("rowt", r), "ident"], writes=[("ps", b)])
                c0 = grp * 4
                w = c0 // 16
                cl = c0 % 16
                P.op("dve", lambda e, b=b, w=w, cl=cl: e.tensor_copy(
                    out=ffnh_s[l][:, w, cl * 8:(cl + 4) * 8].rearrange("p (c q) -> p c q", c=4),
                    in_=PS[b][:, :].rearrange("p (c q) -> p c q", q=128)[:, :, 0:8]),
                    writes=[("ps", b), ("ffnhs", l, w)])

        def kv_prep(bi):
            mstage = AV(0, 2 * D).rearrange("p (m d) -> p m d", m=2)
            mT = AV(2048, NCH * NMEM).rearrange("p (c m) -> p c m", c=NCH)
            mh = AV(5120, L * NCH * NMEM, BF16).rearrange("p (l c m) -> p l c m", l=L, c=NCH)
            rm = AV(7168, NMEM)
            stg = [AV(7424 + i * 512, 512) for i in range(4)]
            sc = [0]
            arena_phase()
            P.op("sp", lambda e: e.dma_start(out=mstage, in_=memp[bi].rearrange("(m p) d -> p m d", p=128)),
                 writes=["mstage"], dsem="mst")
            for c2 in range(4):
                b = nb()
                items = []
                for i in range(2):
                    c = c2 * 2 + i
                    for mc in range(2):
                        items.append((PS[b][:, i * 256 + mc * 128: i * 256 + (mc + 1) * 128],
                                      mstage[:, mc, c * 128:(c + 1) * 128]))
                P.op("pe", tr_fn(items), reads=["mstage", "ident"], writes=[("ps", b)])
                copy_op(ev_eng(), mT[:, c2 * 2:c2 * 2 + 2, :], PS[b][:, :].rearrange("p (i m) -> p i m", i=2),
                        [], [("ps", b), ("mT", c2)])
            P.op("act", lambda e: e.activation(out=sq[:, :, 0:NMEM], in_=mT, func=AF.Square),
                 reads=[("mT", i) for i in range(4)], writes=[("sq", c) for c in range(NCH)])
            b = nb()
            P.op("pe", mm_fn(PS[b][:, 0:NMEM], [(onesb[:], sq[:, c, 0:NMEM]) for c in range(NCH)]),
                 reads=[("sq", c) for c in range(NCH)] + ["onesb"], writes=[("ps", b)])
            P.op("act", lambda e, b=b: e.activation(out=rm, in_=PS[b][:, 0:NMEM], func=AF.Ln, bias=cst[:, EPSC:EPSC + 1], scale=1.0),
                 reads=["cst"], writes=[("ps", b), "rm"])
            P.op("act", lambda e: e.activation(out=rm, in_=rm, func=AF.Exp, scale=-0.5), reads=["rm"], writes=["rm"])
            for l in range(NL):
                for c in range(NCH):
                    P.op("dve", lambda e, l=l, c=c: e.scalar_tensor_tensor(out=mh[:, l, c, :], in0=mT[:, c, :],
                                                                           scalar=cc(l, G_MEM + c), in1=rm,
                                                                           op0=ALU.mult, op1=ALU.mult),
                         reads=[("mT", c // 2), "rm", "cst"], writes=[("mh", l, c)])
            for l in range(NL):
                mhk = [("mh", l, c) for c in range(NCH)]
                for name, dst in (("w_k", mk), ("w_v", mv)):
                    for blk in range(2):
                        s = load_block(name, l, 0, 8, [(blk * 512, 512)])
                        if name == "w_k":
                            for mo in range(4):
                                b = nb()
                                P.op("pe", mm_fn(PS[b][:, 0:NMEM], [(wr[s][:, kc, mo * 128:(mo + 1) * 128], mh[:, l, kc, :])
                                                                     for kc in range(NCH)]),
                                     reads=[("wr", s)] + mhk, writes=[("ps", b)])
                                copy_op(ev_eng(), KT[l][:, blk * 4 + mo, :], PS[b][:, 0:NMEM], [], [("ps", b), ("KT", l)])
                        for mc in range(2):
                            b = nb()
                            P.op("pe", mm_fn(PS[b][:, :], [(mh[:, l, kc, mc * 128:(mc + 1) * 128], wr[s][:, kc, :])
                                                             for kc in range(NCH)]),
                                 reads=[("wr", s)] + mhk, writes=[("ps", b)])
                            si = sc[0]
                            sc[0] = (si + 1) % 4
                            P.op("act", lambda e, b=b, si=si: e.activation(out=stg[si], in_=PS[b][:, :], func=AF.Copy),
                                 writes=[("ps", b), ("stg", si)])
                            if name == "w_v":
                                P.op("pool", lambda e, si=si, l=l, mc=mc, blk=blk: e.tensor_copy(
                                    out=VS[l][:, mc, blk * 512:(blk + 1) * 512], in_=stg[si]),
                                    reads=[("stg", si)], writes=[("VS", l)])
                            P.op("act", lambda e, si=si, l=l, mc=mc, blk=blk, dst=dst: e.dma_start(
                                out=dst[l, bi, mc * 128:(mc + 1) * 128, blk * 512:(blk + 1) * 512], in_=stg[si]),
                                reads=[("stg", si)], dsem=("stg", si))

        def run_tile(kind, bi, ti, last):
            if kind == "p":
                tg = TG(512, 1, 512)
                src = xp[bi, ti * 512:(ti + 1) * 512, :].rearrange("(tb p) d -> p tb d", p=128)
                ntb = 4
            else:
                tg = TG(128, 4, 32)
                src = xs.rearrange("(tb p) d -> p tb d", p=128)
                ntb = 1
            T = tg.T
            P.op("sp", lambda e: e.dma_start(out=xstage[:, 0:ntb, :], in_=src), writes=["xstage", "xstageV"], dsem="xst")
            for c in range(NCH):
                b = nb()
                P.op("pe", tr_fn([(PS[b][:, tb * 128:(tb + 1) * 128], xstage[:, tb, c * 128:(c + 1) * 128])
                                  for tb in range(ntb)]),
                     reads=["xstage", "ident"], writes=[("ps", b)])
                copy_op(ev_eng(), xT[:, c, 0:T], PS[b][:, 0:T], [], [("ps", b), ("xT", c)])
            for l in range(NL):
                first = (kind == "p" and ti == 0)
                if kind == "s":
                    load_sample_ffn_state(l)
                mix(tg, l, kind, first)
                attn(tg, l, kind)
                ffn(tg, l, kind, first)
                if last:
                    store_states(l, kind, bi)
            ystage = yv[:, :, :].rearrange("p c t -> p (c t)").rearrange("p (tb d) -> p tb d", d=D)
            for tb in range(ntb):
                for half in range(2):
                    b = nb()
                    P.op("pe", tr_fn([(PS[b][:, k * 128:(k + 1) * 128], xT[:, half * 4 + k, tb * 128:(tb + 1) * 128])
                                      for k in range(4)]),
                         reads=[("xT", half * 4 + k) for k in range(4)] + ["ident"], writes=[("ps", b)])
                    copy_op(ev_eng(), ystage[:, tb, half * 512:(half + 1) * 512], PS[b][:, :], [],
                            [("ps", b), ("ystg", tb, half)] + ([("yv", c) for c in range(NCH)] if (tb == 0 and half == 0) else []))
                if kind == "p":
                    dst = yp[bi, ti * 512 + tb * 128: ti * 512 + (tb + 1) * 128, :]
                else:
                    dst = ys[:, :]
                P.op("act", lambda e, tb=tb, dst=dst: e.dma_start(out=dst, in_=ystage[:, tb, :]),
                     reads=[("ystg", tb, 0), ("ystg", tb, 1)] + [("yv", c) for c in range(NCH)], dsem=("yst", tb))

        if do_sample:
            for l in range(NL):
                P.op("pool", lambda e, l=l: e.memset(poolh_s[l][:], 0.0), writes=[("poolhs", l, 0), ("poolhs", l, 1)])
                P.op("pool", lambda e, l=l: e.memset(convh_s[l][:], 0.0), writes=[("convhs", l)])
                P.op("pool", lambda e, l=l: e.memset(ffnh_s[l][:], 0.0), writes=[("ffnhs", l, w) for w in range(3)])
            run_tile("s", 0, 0, True)
            assert fp_cur[0] == len(fp_seq) and fp_emitted[0] == len(fp_seq)
            first_pass[0] = False
        for bi in range(NPB):
            kv_prep(bi)
            for l in range(NL):
                P.op("pool", lambda e, l=l: e.memset(poolh_p[l][:], 0.0), writes=[("poolh", l)])
                P.op("pool", lambda e, l=l: e.memset(convh_p[l][:], 0.0), writes=[("convh", l)])
                P.op("pool", lambda e, l=l: e.memset(ffnh_p[l][:], 0.0), writes=[("ffnh", l, c) for c in range(FCH)])
            for ti in range(NT):
                run_tile("p", bi, ti, ti == NT - 1)
        P.emit(nc)
    return nc, P


def _pack_consts(inp):
    cst = np.zeros((128, NCONST), np.float32)

    def put(col, vec):
        n = vec.shape[0] // 128
        cst[:, col:col + n] = vec.reshape(n, 128).T

    for l in range(L):
        base = l * CL
        put(base + G_MIXPRE, inp["g_mix_pre"][l])
        put(base + G_MIXPOST, inp["g_mix_post"][l])
        put(base + G_ATTNPRE, inp["g_attn_pre"][l])
        put(base + G_ATTNPOST, inp["g_attn_post"][l])
        put(base + G_MEM, inp["g_mem"][l])
        put(base + G_FFNPRE, inp["g_ffn_pre"][l])
        put(base + G_FFNPOST, inp["g_ffn_post"][l])
        put(base + POOLSC, inp["pool_scale"][l])
        for k in range(3):
            put(base + CONVW + k * 4, inp["conv_w"][l, k])
            put(base + FFNW + k * FCH, inp["ffn_conv_w"][l, k])
        put(base + CONVB, inp["conv_b"][l])
        put(base + FFNB, inp["ffn_conv_b"][l])
    cst[:, INVCNT:INVCNT + 15] = (1.0 / np.arange(1, 16, dtype=np.float64)).astype(np.float32)[None, :]
    cst[:, EPSC] = EPS
    return cst


_CACHE = {}


def kernel(**inputs):
    inp = {k: np.asarray(v) for k, v in inputs.items()}
    if "nc" not in _CACHE:
        _CACHE["nc"] = build_program()[0]
    nc = _CACHE["nc"]
    cst = _pack_consts(inp)
    shared = {
        "cst": cst,
        "w_in": np.ascontiguousarray(inp["w_in"]),
        "w_pool": np.ascontiguousarray(inp["w_pool"]),
        "w_out": np.ascontiguousarray(inp["w_out"]),
        "w_q": np.ascontiguousarray(inp["w_q"].reshape(L, D, D)),
        "w_k": np.ascontiguousarray(inp["w_k"].reshape(L, D, D)),
        "w_v": np.ascontiguousarray(inp["w_v"].reshape(L, D, D)),
        "w_o": np.ascontiguousarray(inp["w_o"].reshape(L, D, D)),
        "w_up": np.ascontiguousarray(inp["w_up"]),
        "w_down": np.ascontiguousarray(inp["w_down"]),
    }
    in_maps = []
    for i in range(NCORES):
        m = dict(shared)
        m["xp"] = np.ascontiguousarray(inp["x_prompt"][PB * i:PB * (i + 1)])
        m["xs"] = np.ascontiguousarray(inp["x_sample"][SB * i:SB * (i + 1)].reshape(SB * SSEQ, D))
        m["memp"] = np.ascontiguousarray(inp["mem_prompt"][PB * i:PB * (i + 1)])
        m["ck"] = np.ascontiguousarray(inp["cache_mem_k"][:, SB * i:SB * (i + 1)].reshape(L, SB, NMEM, D))
        m["cv"] = np.ascontiguousarray(inp["cache_mem_v"][:, SB * i:SB * (i + 1)].reshape(L, SB, NMEM, D))
        m["spool"] = np.ascontiguousarray(inp["state_pool"][:, SB * i:SB * (i + 1)])
        m["sconv"] = np.ascontiguousarray(inp["state_conv"][:, SB * i:SB * (i + 1)])
        m["sffn"] = np.ascontiguousarray(inp["state_ffn_conv"][:, SB * i:SB * (i + 1)])
        in_maps.append(m)
    res = run_bass_kernel_spmd(nc, in_maps, core_ids=list(range(NCORES)))
    R = res.results

    def cat(name, axis):
        return np.concatenate([np.asarray(r[name]) for r in R], axis=axis)

    y_prompt = cat("yp", 0)
    y_sample = cat("ys", 0).reshape(NCORES * SB, SSEQ, D)
    mem_k = cat("mk", 1).reshape(L, NCORES * PB, NMEM, 4, 256)
    mem_v = cat("mv", 1).reshape(L, NCORES * PB, NMEM, 4, 256)
    pool_p = cat("poolp", 1)
    conv_p = cat("convp", 1)
    ffn_p = cat("ffnp", 1)
    pool_s = cat("pools", 1)
    conv_s = cat("convs", 1)
    ffn_s = cat("ffns", 1)
    return (y_prompt, y_sample, mem_k, mem_v, pool_p, conv_p, ffn_p, pool_s, conv_s, ffn_s)
```

```python
import contextlib
import numpy as np
import concourse.bass as bass
import concourse.mybir as mybir
from concourse.bass_utils import run_bass_kernel_spmd

F32 = mybir.dt.float32
BF16 = mybir.dt.bfloat16
AF = mybir.ActivationFunctionType
ALU = mybir.AluOpType

ENGINES = ("pe", "act", "dve", "pool", "sp")

L = 2
D = 1024
SEQ = 2048
NMEM = 256
DFF = 2816
NCH = 8
FCH = 44
GCH = 22
NCORES = 8
PB = 2
SB = 4
SSEQ = 32
EPS = 1e-6

CL = 252
G_MIXPRE, G_MIXPOST, G_ATTNPRE, G_ATTNPOST, G_MEM, G_FFNPRE, G_FFNPOST = 0, 8, 16, 24, 32, 40, 48
POOLSC, CONVW, CONVB, FFNW, FFNB = 56, 60, 72, 76, 208
INVCNT = L * CL
EPSC = INVCNT + 15
NCONST = EPSC + 1
SEM_EPOCH = 1500


class Op:
    __slots__ = ("id", "eng", "fn", "deps", "signal", "dsem", "ev", "pos", "vc")

    def __init__(self, id, eng, fn, dsem):
        self.id = id
        self.eng = eng
        self.fn = fn
        self.deps = set()
        self.signal = False
        self.dsem = dsem
        self.ev = None
        self.pos = None
        self.vc = None


class Prog:
    def __init__(self):
        self.ops = []
        self.last_write = {}
        self.readers = {}
        self.last_dma = {}
        self.extra = ()

    def op(self, eng, fn, reads=(), writes=(), dsem=None, chain=True, noextra=False):
        o = Op(len(self.ops), eng, fn, dsem)
        if self.extra and not noextra:
            reads = list(reads) + list(self.extra)
        deps = o.deps
        lw = self.last_write
        rd = self.readers
        for k in reads:
            w = lw.get(k)
            if w is not None:
                deps.add(w)
        for k in writes:
            w = lw.get(k)
            if w is not None:
                deps.add(w)
            r = rd.get(k)
            if r:
                deps.update(r)
        for k in reads:
            rd.setdefault(k, []).append(o.id)
        for k in writes:
            lw[k] = o.id
            rd[k] = []
        if dsem is not None:
            p = self.last_dma.get(dsem)
            if p is not None and chain:
                deps.add(p)
            self.last_dma[dsem] = o.id
        deps.discard(o.id)
        self.ops.append(o)
        return o

    def emit(self, nc, final_wait_eng="sp", same_eng_skip=10 ** 9):
        ops = self.ops
        cnt = {e: 0 for e in ENGINES}
        for o in ops:
            o.pos = cnt[o.eng]
            cnt[o.eng] += 1
        for o in ops:
            if o.dsem is not None:
                continue
            drop = []
            for d in o.deps:
                od = ops[d]
                if od.eng == o.eng and od.dsem is None:
                    if o.eng == "pe" or (o.pos - od.pos) >= same_eng_skip:
                        drop.append(d)
            for d in drop:
                o.deps.discard(d)
        for o in ops:
            for d in o.deps:
                ops[d].signal = True
        ecnt = {e: 0 for e in ENGINES}
        dcnt = {}
        sem_names = set()
        for o in ops:
            if o.dsem is not None:
                dcnt[o.dsem] = dcnt.get(o.dsem, 0) + 16
                o.ev = (("d", o.dsem), dcnt[o.dsem])
                sem_names.add(("d", o.dsem))
            elif o.signal:
                k = ("e", o.eng, ecnt[o.eng] // SEM_EPOCH)
                o.ev = (k, ecnt[o.eng] % SEM_EPOCH + 1)
                ecnt[o.eng] += 1
                sem_names.add(k)
        sems = {}
        for k in sorted(sem_names, key=str):
            sems[k] = nc.alloc_semaphore(name=("s_" + "_".join(str(x) for x in k)).replace(" ", "").replace("'", "")
                                         .replace("(", "_").replace(")", "_").replace(",", "_"))
        clock = {e: {} for e in ENGINES}
        plan = {e: [] for e in ENGINES}
        for o in ops:
            ck = clock[o.eng]
            wd = {}
            for d in sorted(o.deps):
                od = ops[d]
                s, v = od.ev
                if ck.get(s, 0) >= v:
                    continue
                wd[s] = max(wd.get(s, 0), v)
                ck[s] = v
                for s2, v2 in od.vc.items():
                    if ck.get(s2, 0) < v2:
                        ck[s2] = v2
            o.vc = dict(ck)
            if o.ev is not None:
                o.vc[o.ev[0]] = max(o.vc.get(o.ev[0], 0), o.ev[1])
            plan[o.eng].append((o, list(wd.items())))
        finals = [(("d", k), v) for k, v in dcnt.items()]
        self.n_waits = sum(len(w) for e in ENGINES for _, w in plan[e])
        self.n_sems = len(sems)

        def run_engine(eng_name, eng):
            for o, waits in plan[eng_name]:
                for s, v in waits:
                    eng.wait_ge(sems[s], v)
                ins = o.fn(eng)
                if o.ev is not None:
                    s, v = o.ev
                    ins.then_inc(sems[s], 16 if o.dsem is not None else 1)
            if eng_name == final_wait_eng:
                for s, v in finals:
                    eng.wait_ge(sems[s], v)

        with nc.Block() as block:
            @block.tensor
            def _(e):
                run_engine("pe", e)

            @block.scalar
            def _(e):
                run_engine("act", e)

            @block.vector
            def _(e):
                run_engine("dve", e)

            @block.gpsimd
            def _(e):
                run_engine("pool", e)

            @block.sync
            def _(e):
                run_engine("sp", e)


class TG:
    def __init__(self, T, NS, SL):
        self.T, self.NS, self.SL = T, NS, SL


def build_program(NT=4, do_sample=True, NL=L, NPB=PB, NBUF=4):
    nc = bass.Bass("TRN2", target_bir_lowering=False)
    P = Prog()

    def din(name, shape, dt=F32):
        return nc.dram_tensor(name, list(shape), dt, kind="ExternalInput").ap()

    def dout(name, shape, dt=F32):
        return nc.dram_tensor(name, list(shape), dt, kind="ExternalOutput").ap()

    def dint(name, shape, dt=BF16):
        return nc.dram_tensor(name, list(shape), dt, kind="Internal").ap()

    xp = din("xp", [PB, SEQ, D])
    xs = din("xs", [SB * SSEQ, D])
    memp = din("memp", [PB, NMEM, D])
    ck = din("ck", [L, SB, NMEM, D])
    cv = din("cv", [L, SB, NMEM, D])
    spool = din("spool", [L, SB, 15, 512])
    sconv = din("sconv", [L, SB, 2, 512])
    sffn = din("sffn", [L, SB, 2, 2 * DFF])
    cst_d = din("cst", [128, NCONST])
    wshape = {"w_in": (D, 2048), "w_out": (D, D), "w_q": (D, D), "w_k": (D, D), "w_v": (D, D), "w_o": (D, D),
              "w_up": (D, 2 * DFF), "w_down": (DFF, D)}
    wf = {k: din(k, [L, v[0], v[1]]) for k, v in wshape.items()}
    wb = {k: dint(k + "_b", [L, v[0], v[1]]) for k, v in wshape.items()}
    w_pool_d = din("w_pool", [L, 4, 128, 128])

    yp = dout("yp", [PB, SEQ, D])
    ys = dout("ys", [SB * SSEQ, D])
    mk = dout("mk", [L, PB, NMEM, D])
    mv = dout("mv", [L, PB, NMEM, D])
    poolp = dout("poolp", [L, PB, 15, 512])
    convp = dout("convp", [L, PB, 2, 512])
    ffnp = dout("ffnp", [L, PB, 2, 2 * DFF])
    pools = dout("pools", [L, SB, 15, 512])
    convs = dout("convs", [L, SB, 2, 512])
    ffns = dout("ffns", [L, SB, 2, 2 * DFF])

    with contextlib.ExitStack() as st:
        def sb(name, shape, dt):
            return st.enter_context(nc.sbuf_tensor(name, shape, dt))

        ident = sb("ident", [128, 128], F32)
        onesb = sb("onesb", [128, 128], BF16)
        ones1 = sb("ones1", [128, 128], BF16)
        cst = sb("cst_sb", [128, NCONST], F32)
        wpool = sb("wpool", [128, L * 4, 128], BF16)
        poolh_p = [sb("poolh_p%d" % l, [128, 128], F32) for l in range(L)]
        convh_p = [sb("convh_p%d" % l, [128, 128], F32) for l in range(L)]
        ffnh_p = [sb("ffnh_p%d" % l, [128, 128], F32) for l in range(L)]
        poolh_s = [sb("poolh_s%d" % l, [128, 2, 128], F32) for l in range(L)]
        convh_s = [sb("convh_s%d" % l, [128, 128], F32) for l in range(L)]
        ffnh_s = [sb("ffnh_s%d" % l, [128, 3, 128], F32) for l in range(L)]
        xT = sb("xT", [128, NCH, 512], F32)
        hT = sb("hT", [128, NCH, 512], BF16)
        sq = sb("sq", [128, NCH, 512], BF16)
        rstd = sb("rstd", [128, 2, 512], F32)
        yv = sb("yv", [128, NCH, 512], F32)
        catq = sb("catq", [128, NCH, 512], BF16)
        PT = sb("PT", [128, 2, 2, 512], BF16)
        rden = sb("rden", [128, 2, 512], F32)
        KT = [sb("KT%d" % i, [128, NCH, NMEM], BF16) for i in range(2)]
        VS = [sb("VS%d" % i, [128, 2, D], BF16) for i in range(2)]
        xstage = sb("xstage", [128, 4, D], F32)
        wr = [sb("wr%d" % i, [128, 8, 512], BF16) for i in range(NBUF)]
        rowt = [sb("rowt%d" % i, [128, 512], F32) for i in range(4)]
        AW = 13568
        arena = sb("arena", [128, AW], F32)
        PS = [st.enter_context(nc.psum_tensor("ps%d" % i, [128, 512], F32)) for i in range(8)]

        def AV(off, n, dt=F32):
            if dt == F32:
                return arena[:, off:off + n]
            assert n % 2 == 0
            return arena[:, off:off + n // 2].bitcast(BF16)

        bank_ctr = [0]

        def nb():
            b = bank_ctr[0]
            bank_ctr[0] = (b + 1) % 8
            return b

        wslot = [0]
        rowc = [0]

        def next_row():
            r = rowc[0]
            rowc[0] = (r + 1) % 4
            return r

        evc = [0]

        def ev_eng():
            evc[0] ^= 1
            return "act" if evc[0] else "dve"

        scratch = sb("scratch", [128, 8], F32)

        def arena_phase():
            P.extra = ()
            P.op("pool", lambda e: e.memset(scratch[:, 0:1], 0.0), writes=["AP"])
            P.extra = ("AP",)

        def cc(l, off, n=1):
            return cst[:, l * CL + off: l * CL + off + n]

        def copy_op(eng, out, in_, reads, writes):
            if eng == "act":
                P.op("act", lambda e: e.activation(out=out, in_=in_, func=AF.Copy), reads=reads, writes=writes)
            else:
                P.op(eng, lambda e: e.tensor_copy(out=out, in_=in_), reads=reads, writes=writes)

        def mm_fn(bank_ap, pairs, start=True, stop=True):
            def f(e):
                last = None
                n = len(pairs)
                for i, (a, b) in enumerate(pairs):
                    last = e.matmul(bank_ap, lhsT=a, rhs=b, start=(start and i == 0), stop=(stop and i == n - 1))
                return last
            return f

        def tr_fn(items):
            def f(e):
                last = None
                for (o, i) in items:
                    last = e.transpose(out=o, in_=i, identity=ident[:])
                return last
            return f

        def mm_block_kc_outer(slot, banks, T, rhs_of, key_of, kn=NCH, start=True, stop=True, k0=0):
            for kk in range(kn):
                def f(e, kk=kk):
                    last = None
                    for mo, b in enumerate(banks):
                        last = e.matmul(PS[b][:, 0:T], lhsT=wr[slot][:, kk, mo * 128:(mo + 1) * 128], rhs=rhs_of(k0 + kk),
                                        start=(start and kk == 0), stop=(stop and kk == kn - 1))
                    return last
                P.op("pe", f, reads=[("wr", slot), key_of(k0 + kk)], writes=[("ps", b) for b in banks])

        P.op("pool", lambda e: e.memset(ident[:], 0.0), writes=["ident"])
        P.op("pool", lambda e: e.affine_select(out=ident[:], in_=ident[:], pattern=[[-1, 128]],
                                               compare_op=ALU.not_equal, fill=1.0, base=0, channel_multiplier=1),
             reads=["ident"], writes=["ident"])
        P.op("pool", lambda e: e.memset(onesb[:], 1.0 / D), writes=["onesb"])
        P.op("pool", lambda e: e.memset(ones1[:], 1.0), writes=["ones1"])
        for i in range(4):
            P.op("pool", lambda e, i=i: e.memset(rowt[i][:], 0.0), writes=[("rowt", i)])
        P.op("pool", lambda e: e.memset(scratch[:, 2:4], 1.0), writes=["scr2", "scr3"])
        P.op("sp", lambda e: e.dma_start(out=cst[:], in_=cst_d), writes=["cst"], dsem="cst")
        P.op("pool", lambda e: e.dma_start(out=wpool[:], in_=w_pool_d.rearrange("l g c d -> c (l g) d")),
             writes=["wpool"], dsem="wpool")

        def precast(name, l):
            rows = wshape[name][0]
            for rc in range(rows // 128):
                P.op("pool", lambda e, name=name, l=l, rc=rc: e.dma_start(
                    out=wb[name][l, rc * 128:(rc + 1) * 128, :], in_=wf[name][l, rc * 128:(rc + 1) * 128, :]),
                    writes=[("wb", name, l, rc)], dsem=("pc", name, l, rc // 8), chain=False)

        def wb_keys(name, l, k0=0, kn=None):
            rows = wshape[name][0] // 128
            if kn is None:
                kn = rows
            g0, g1 = k0 // 8, (k0 + kn - 1) // 8
            return [("wb", name, l, rc) for rc in range(g0 * 8, min(rows, (g1 + 1) * 8))]

        precasted = set()

        def ensure_precast(name, l):
            if (name, l) not in precasted:
                precasted.add((name, l))
                precast(name, l)

        def layer_block_seq(l):
            seq = []
            for blk in range(4):
                seq.append(("w_in", l, 0, 8, ((blk * 512, 512),)))
            for name in ("w_out", "w_q", "w_o"):
                for blk in range(2):
                    seq.append((name, l, 0, 8, ((blk * 512, 512),)))
            for q in range(5):
                seq.append(("w_up", l, 0, 8, ((q * 512, 512),)))
                seq.append(("w_up", l, 0, 8, ((DFF + q * 512, 512),)))
            seq.append(("w_up", l, 0, 8, ((20 * 128, 256), (DFF + 20 * 128, 256))))
            for half in range(2):
                for (k0, kn) in ((0, 8), (8, 8), (16, 6)):
                    seq.append(("w_down", l, k0, kn, ((half * 512, 512),)))
            return seq

        first_pass = [bool(do_sample)]
        fp_seq = [b for l in range(NL) for b in layer_block_seq(l)]
        fp_cur = [0]
        fp_emitted = [0]
        PD = 2

        def emit_block_dma(eng, name, l, k0, kn, runs, s):
            src = wb[name]
            off = 0
            first = None
            for ri, (c0, wd) in enumerate(runs):
                o = P.op(eng, lambda e, s=s, off=off, c0=c0, wd=wd: e.dma_start(
                    out=wr[s][:, 0:kn, off:off + wd],
                    in_=src[l, k0 * 128:(k0 + kn) * 128, c0:c0 + wd].rearrange("(k p) m -> p k m", p=128)),
                    reads=wb_keys(name, l, k0, kn), writes=[("wr", s)] if ri == 0 else [],
                    dsem=(("wrp" if eng == "pool" else "wr"), s), noextra=True, chain=(ri == 0))
                if ri == 0:
                    first = o
                else:
                    o.deps |= first.deps
                    P.last_write[("wr", s)] = o.id
                off += wd

        def load_block(name, l, k0, kn, runs):
            runs = tuple(runs)
            if first_pass[0]:
                idx = fp_cur[0]
                assert fp_seq[idx] == (name, l, k0, kn, runs), (idx, fp_seq[idx], (name, l, k0, kn, runs))
                while fp_emitted[0] <= min(idx + PD, len(fp_seq) - 1):
                    j = fp_emitted[0]
                    n2, l2, k02, kn2, runs2 = fp_seq[j]
                    ensure_precast(n2, l2)
                    emit_block_dma("pool", n2, l2, k02, kn2, runs2, j % NBUF)
                    fp_emitted[0] += 1
                    if fp_emitted[0] == len(fp_seq):
                        for ll in range(NL):
                            ensure_precast("w_k", ll)
                            ensure_precast("w_v", ll)
                fp_cur[0] += 1
                wslot[0] = fp_emitted[0] % NBUF
                return idx % NBUF
            ensure_precast(name, l)
            s = wslot[0]
            wslot[0] = (s + 1) % NBUF
            emit_block_dma("sp", name, l, k0, kn, runs, s)
            return s

        def sumsq_rstd(tg, sq_keys, ri):
            T = tg.T
            b = nb()
            for c in range(NCH):
                P.op("pe", mm_fn(PS[b][:, 0:T], [(onesb[:], sq[:, c, 0:T])], start=(c == 0), stop=(c == NCH - 1)),
                     reads=[("sq", c), "onesb"], writes=[("ps", b)])
            P.op("act", lambda e: e.activation(out=rstd[:, ri, 0:T], in_=PS[b][:, 0:T], func=AF.Ln,
                                               bias=cst[:, EPSC:EPSC + 1], scale=1.0),
                 reads=["cst"], writes=[("ps", b), ("rstd", ri)])
            P.op("act", lambda e: e.activation(out=rstd[:, ri, 0:T], in_=rstd[:, ri, 0:T], func=AF.Exp, scale=-0.5),
                 reads=[("rstd", ri)], writes=[("rstd", ri)])

        def pre_norm(tg, l, goff):
            T = tg.T
            for c in range(NCH):
                P.op("act", lambda e, c=c: e.activation(out=sq[:, c, 0:T], in_=xT[:, c, 0:T], func=AF.Square),
                     reads=[("xT", c)], writes=[("sq", c)])
            sumsq_rstd(tg, None, 0)
            for c in range(NCH):
                P.op("dve", lambda e, c=c: e.scalar_tensor_tensor(out=hT[:, c, 0:T], in0=xT[:, c, 0:T],
                                                                   scalar=cc(l, goff + c), in1=rstd[:, 0, 0:T],
                                                                   op0=ALU.mult, op1=ALU.mult),
                     reads=[("xT", c), ("rstd", 0), "cst"], writes=[("hT", c)])

        def pre_norm_deferred(tg, l, goff, need_rr=False):
            T = tg.T
            for c in range(NCH):
                P.op("act", lambda e, c=c: e.activation(out=hT[:, c, 0:T], in_=xT[:, c, 0:T], func=AF.Identity,
                                                        scale=cc(l, goff + c)),
                     reads=[("xT", c), "cst"], writes=[("hT", c)])
            for c in range(NCH):
                P.op("act", lambda e, c=c: e.activation(out=sq[:, c, 0:T], in_=xT[:, c, 0:T], func=AF.Square),
                     reads=[("xT", c)], writes=[("sq", c)])

        def pre_norm_deferred2(tg, need_rr=False):
            T = tg.T
            sumsq_rstd(tg, None, 0)
            if need_rr:
                P.op("dve", lambda e: e.tensor_tensor(out=rden[:, 1, 0:T], in0=rstd[:, 0, 0:T], in1=rstd[:, 0, 0:T],
                                                      op=ALU.mult),
                     reads=[("rstd", 0)], writes=[("rden", 1)])

        def evac_y(tg, b, c, l, goff):
            T = tg.T
            P.op("act", lambda e: e.activation(out=yv[:, c, 0:T], in_=PS[b][:, 0:T], func=AF.Identity,
                                               scale=cc(l, goff + c)),
                 reads=["cst"], writes=[("ps", b), ("yv", c)])
            P.op("act", lambda e: e.activation(out=sq[:, c, 0:T], in_=PS[b][:, 0:T], func=AF.Square),
                 writes=[("ps", b), ("sq", c)])

        def post_norm(tg, l, goff):
            T = tg.T
            sumsq_rstd(tg, None, 1)
            rb = rstd[:, 1:2, 0:T].broadcast_to([128, 2, T])
            for c in range(0, NCH, 2):
                P.op("dve", lambda e, c=c: e.tensor_tensor(out=yv[:, c:c + 2, 0:T], in0=yv[:, c:c + 2, 0:T], in1=rb,
                                                           op=ALU.mult),
                     reads=[("yv", c), ("yv", c + 1), ("rstd", 1)], writes=[("yv", c), ("yv", c + 1)])
                P.op("dve", lambda e, c=c: e.tensor_tensor(out=xT[:, c:c + 2, 0:T], in0=yv[:, c:c + 2, 0:T],
                                                           in1=xT[:, c:c + 2, 0:T], op=ALU.add),
                     reads=[("yv", c), ("yv", c + 1), ("xT", c), ("xT", c + 1)], writes=[("xT", c), ("xT", c + 1)])

        def dense_1024(tg, name, l, rhs_buf, rhs_key, evac, hook=None):
            T = tg.T
            for blk in range(2):
                s = load_block(name, l, 0, 8, [(blk * 512, 512)])
                banks = []
                if blk == 0:
                    banks = [nb() for _ in range(4)]
                    mm_block_kc_outer(s, banks, T, lambda kc: rhs_buf[:, kc, 0:T], lambda kc: (rhs_key, kc))
                else:
                    for mo in range(4):
                        b = nb()
                        banks.append(b)
                        P.op("pe", mm_fn(PS[b][:, 0:T], [(wr[s][:, kc, mo * 128:(mo + 1) * 128], rhs_buf[:, kc, 0:T])
                                                           for kc in range(NCH)]),
                             reads=[("wr", s)] + [(rhs_key, kc) for kc in range(NCH)], writes=[("ps", b)])
                if blk == 0 and hook is not None:
                    hook()
                for mo in range(4):
                    evac(banks[mo], blk * 4 + mo)

        def mix(tg, l, kind, first):
            T, NS, SL = tg.T, tg.NS, tg.SL
            EL = 15 + SL
            uA = AV(0, 4 * NS * EL).rearrange("p (g s e) -> p g s e", g=4, s=NS)
            Bg = AV(2112, 4 * T).rearrange("p (j t) -> p j t", j=4)
            CVb = AV(4160, 4 * NS * (2 + SL)).rearrange("p (j s e) -> p j s e", j=4, s=NS)
            pa = [AV(6224 + g * 528, NS * EL).rearrange("p (s e) -> p s e", s=NS) for g in range(4)]
            pb = [AV(6224 + (4 + g) * 528, NS * EL).rearrange("p (s e) -> p s e", s=NS) for g in range(4)]
            dd = AV(10448, 4 * T, BF16).rearrange("p (g t) -> p g t", g=4)
            ta = AV(11472, 4 * T).rearrange("p (j t) -> p j t", j=4)

            def v3(ap2d):
                return ap2d.rearrange("p (s l) -> p s l", s=NS)

            pre_norm_deferred(tg, l, G_MIXPRE)
            r3 = v3(rstd[:, 0, 0:T])
            rr3 = v3(rden[:, 1, 0:T])
            s0 = load_block("w_in", l, 0, 8, [(0, 512)])
            banks0 = [nb() for _ in range(4)]
            mm_block_kc_outer(s0, banks0, T, lambda kc: hT[:, kc, 0:T], lambda kc: ("hT", kc))
            pre_norm_deferred2(tg, need_rr=True)
            arena_phase()
            if kind == "p":
                P.op("pool", lambda e: e.tensor_copy(out=uA[:, :, 0, 0:15],
                                                     in_=poolh_p[l][:, 0:60].rearrange("p (g r) -> p g r", g=4)),
                     reads=[("poolh", l)], writes=[("uA", g) for g in range(4)])
                P.op("pool", lambda e: e.tensor_copy(out=CVb[:, :, 0, 0:2],
                                                     in_=convh_p[l][:, 0:8].rearrange("p (j r) -> p j r", j=4)),
                     reads=[("convh", l)], writes=[("CV", j) for j in range(4)])
            else:
                r = next_row()
                P.op("sp", lambda e, r=r: e.dma_start(out=rowt[r][0:60, :], in_=spool[l].rearrange("s r d -> (s r) d")),
                     reads=[("rowt", r)], writes=[("rowt", r)], dsem=("rowt", r))
                b = nb()
                P.op("pe", tr_fn([(PS[b][:, g * 128:(g + 1) * 128], rowt[r][:, g * 128:(g + 1) * 128]) for g in range(4)]),
                     reads=[("rowt", r), "ident"], writes=[("ps", b)])
                P.op("dve", lambda e, b=b: e.tensor_copy(
                    out=uA[:, :, :, 0:15],
                    in_=PS[b][:, :].rearrange("p (g q) -> p g q", q=128)[:, :, 0:60].rearrange("p g (s r) -> p g s r", s=4)),
                    writes=[("ps", b)] + [("uA", g) for g in range(4)])
                r2 = next_row()
                P.op("sp", lambda e, r2=r2: e.dma_start(out=rowt[r2][0:8, :], in_=sconv[l].rearrange("s r d -> (s r) d")),
                     reads=[("rowt", r2)], writes=[("rowt", r2)], dsem=("rowt", r2))
                b2 = nb()
                P.op("pe", tr_fn([(PS[b2][:, j * 128:(j + 1) * 128], rowt[r2][:, j * 128:(j + 1) * 128]) for j in range(4)]),
                     reads=[("rowt", r2), "ident"], writes=[("ps", b2)])
                P.op("dve", lambda e, b2=b2: e.tensor_copy(
                    out=CVb[:, :, :, 0:2],
                    in_=PS[b2][:, :].rearrange("p (j q) -> p j q", q=128)[:, :, 0:8].rearrange("p j (s r) -> p j s r", s=4)),
                    writes=[("ps", b2)] + [("CV", j) for j in range(4)])
            def pool_section():
                lo1 = [15, 13, 9, 1]
                for g in range(4):
                    lo = lo1[g]
                    P.op("pool", lambda e, g=g, lo=lo: e.tensor_tensor(out=pa[g][:, :, lo:EL], in0=uA[:, g, :, lo:EL],
                                                                       in1=uA[:, g, :, lo - 1:EL - 1], op=ALU.add),
                         reads=[("uA", g)], writes=[("pa", g)])
                lo2 = [None, 15, 11, 3]
                for g in range(1, 4):
                    lo = lo2[g]
                    P.op("pool", lambda e, g=g, lo=lo: e.tensor_tensor(out=pb[g][:, :, lo:EL], in0=pa[g][:, :, lo:EL],
                                                                       in1=pa[g][:, :, lo - 2:EL - 2], op=ALU.add),
                         reads=[("pa", g)], writes=[("pb", g)])
                lo3 = [None, None, 15, 7]
                for g in range(2, 4):
                    lo = lo3[g]
                    P.op("pool", lambda e, g=g, lo=lo: e.tensor_tensor(out=pa[g][:, :, lo:EL], in0=pb[g][:, :, lo:EL],
                                                                       in1=pb[g][:, :, lo - 4:EL - 4], op=ALU.add),
                         reads=[("pb", g), ("pa", g)], writes=[("pa", g)])
                P.op("pool", lambda e: e.tensor_tensor(out=pb[3][:, :, 15:EL], in0=pa[3][:, :, 15:EL],
                                                       in1=pa[3][:, :, 7:EL - 8], op=ALU.add),
                     reads=[("pa", 3), ("pb", 3)], writes=[("pb", 3)])
                fin = [pa[0], pb[1], pa[2], pb[3]]
                fkey = [("pa", 0), ("pb", 1), ("pa", 2), ("pb", 3)]
                for g in range(4):
                    w = 2 << g
                    P.op("dve", lambda e, g=g, w=w: e.scalar_tensor_tensor(
                        out=v3(dd[:, g, 0:T]), in0=fin[g][:, :, 15:EL], scalar=1.0 / w, in1=uA[:, g, :, 15:EL],
                        op0=ALU.mult, op1=ALU.subtract),
                        reads=[fkey[g], ("uA", g)], writes=[("dd", g)])
                    if first:
                        n = w - 1
                        P.op("dve", lambda e, g=g, n=n: e.tensor_tensor(out=fin[g][:, 0, 15:15 + n], in0=fin[g][:, 0, 15:15 + n],
                                                                        in1=cst[:, INVCNT:INVCNT + n], op=ALU.mult),
                             reads=[fkey[g], "cst", ("dd", g)], writes=[fkey[g]])
                        P.op("dve", lambda e, g=g, n=n: e.tensor_tensor(out=dd[:, g, 0:n], in0=fin[g][:, 0, 15:15 + n],
                                                                        in1=uA[:, g, 0, 15:15 + n], op=ALU.subtract),
                             reads=[fkey[g], ("uA", g)], writes=[("dd", g)])
                if kind == "p":
                    P.op("pool", lambda e: e.tensor_copy(out=poolh_p[l][:, 0:60].rearrange("p (g r) -> p g r", g=4),
                                                         in_=uA[:, :, 0, SL:SL + 15]),
                         reads=[("uA", g) for g in range(4)], writes=[("poolh", l)])
                else:
                    for w in range(2):
                        P.op("pool", lambda e, w=w: e.tensor_copy(
                            out=poolh_s[l][:, w, 0:120].rearrange("p (gl s r) -> p gl s r", gl=2, s=4),
                            in_=uA[:, 2 * w:2 * w + 2, :, SL:SL + 15]),
                            reads=[("uA", 2 * w), ("uA", 2 * w + 1)], writes=[("poolhs", l, w)])

            for blk in range(4):
                if blk > 0:
                    s = load_block("w_in", l, 0, 8, [(blk * 512, 512)])
                for mo in range(4):
                    if blk == 0:
                        b = banks0[mo]
                    else:
                        b = nb()
                        P.op("pe", mm_fn(PS[b][:, 0:T], [(wr[s][:, kc, mo * 128:(mo + 1) * 128], hT[:, kc, 0:T])
                                                           for kc in range(NCH)]),
                             reads=[("wr", s)] + [("hT", kc) for kc in range(NCH)], writes=[("ps", b)])
                    if blk == 0:
                        P.op("dve", lambda e, b=b, mo=mo: e.tensor_tensor(out=uA[:, mo, :, 15:15 + SL], in0=v3(PS[b][:, 0:T]),
                                                                          in1=r3, op=ALU.mult),
                             reads=[("rstd", 0)], writes=[("ps", b), ("uA", mo)])
                    elif blk == 1:
                        P.op("dve", lambda e, b=b, mo=mo: e.tensor_tensor(out=Bg[:, mo, :], in0=PS[b][:, 0:T],
                                                                          in1=rstd[:, 0, 0:T], op=ALU.mult),
                             reads=[("rstd", 0)], writes=[("ps", b), ("Bg", mo)])
                    elif blk == 2:
                        P.op("dve", lambda e, b=b, mo=mo: e.tensor_tensor(out=CVb[:, mo, :, 2:2 + SL], in0=v3(PS[b][:, 0:T]),
                                                                          in1=rr3, op=ALU.mult),
                             reads=[("rden", 1)], writes=[("ps", b), ("CV", mo)])
                    else:
                        P.op("dve", lambda e, b=b, mo=mo: e.tensor_tensor(out=CVb[:, mo, :, 2:2 + SL], in0=v3(PS[b][:, 0:T]),
                                                                          in1=CVb[:, mo, :, 2:2 + SL], op=ALU.mult),
                             writes=[("ps", b), ("CV", mo)])
                if blk == 0:
                    pool_section()
            for g in range(4):
                b = nb()
                P.op("pe", mm_fn(PS[b][:, 0:T], [(wpool[:, l * 4 + g, :], dd[:, g, 0:T])]),
                     reads=["wpool", ("dd", g)], writes=[("ps", b)])
                P.op("act", lambda e, b=b, g=g: e.activation(out=catq[:, g, 0:T], in_=PS[b][:, 0:T], func=AF.Identity,
                                                             scale=cc(l, POOLSC + g)),
                     reads=["cst"], writes=[("ps", b), ("catq", g)])
            for j in range(4):
                P.op("pool", lambda e, j=j: e.tensor_scalar(out=v3(ta[:, j, 0:T]), in0=CVb[:, j, :, 2:2 + SL],
                                                            scalar1=cc(l, CONVW + 2 * 4 + j), scalar2=cc(l, CONVB + j),
                                                            op0=ALU.mult, op1=ALU.add),
                     reads=[("CV", j), "cst"], writes=[("ta", j)])
            for j in range(4):
                P.op("dve", lambda e, j=j: e.scalar_tensor_tensor(out=v3(ta[:, j, 0:T]), in0=CVb[:, j, :, 1:1 + SL],
                                                                   scalar=cc(l, CONVW + 1 * 4 + j), in1=v3(ta[:, j, 0:T]),
                                                                   op0=ALU.mult, op1=ALU.add),
                     reads=[("CV", j), ("ta", j), "cst"], writes=[("ta", j)])
            for j in range(4):
                P.op("dve", lambda e, j=j: e.scalar_tensor_tensor(out=v3(ta[:, j, 0:T]), in0=CVb[:, j, :, 0:SL],
                                                                   scalar=cc(l, CONVW + 0 * 4 + j), in1=v3(ta[:, j, 0:T]),
                                                                   op0=ALU.mult, op1=ALU.add),
                     reads=[("CV", j), ("ta", j), "cst"], writes=[("ta", j)])
            for j in range(4):
                P.op("pool", lambda e, j=j: e.tensor_tensor(out=catq[:, 4 + j, 0:T], in0=ta[:, j, 0:T], in1=Bg[:, j, :],
                                                            op=ALU.mult),
                     reads=[("ta", j), ("Bg", j)], writes=[("catq", 4 + j)])
            if kind == "p":
                P.op("pool", lambda e: e.tensor_copy(out=convh_p[l][:, 0:8].rearrange("p (j r) -> p j r", j=4),
                                                     in_=CVb[:, :, 0, SL:SL + 2]),
                     reads=[("CV", j) for j in range(4)], writes=[("convh", l)])
            else:
                P.op("pool", lambda e: e.tensor_copy(out=convh_s[l][:, 0:32].rearrange("p (j s r) -> p j s r", j=4, s=4),
                                                     in_=CVb[:, :, :, SL:SL + 2]),
                     reads=[("CV", j) for j in range(4)], writes=[("convhs", l)])
            dense_1024(tg, "w_out", l, catq, "catq", lambda b, c: evac_y(tg, b, c, l, G_MIXPOST))
            post_norm(tg, l, G_MIXPOST)

        def load_sample_kv(l, s):
            slot = s % 2
            P.op("sp", lambda e: e.dma_start(out=xstage[:, 0:2, :], in_=ck[l, s].rearrange("(m p) d -> p m d", p=128)),
                 writes=["xstage"], dsem="xst")
            P.op("sp", lambda e: e.dma_start(out=xstage[:, 2:4, :], in_=cv[l, s].rearrange("(m p) d -> p m d", p=128)),
                 writes=["xstageV"], dsem="xstv")
            P.op("pool", lambda e: e.tensor_copy(out=VS[slot][:], in_=xstage[:, 2:4, :]),
                 reads=["xstageV"], writes=[("VS", slot)])
            for hc2 in range(4):
                b = nb()
                items = []
                for i in range(2):
                    hc = hc2 * 2 + i
                    for mc in range(2):
                        items.append((PS[b][:, i * 256 + mc * 128: i * 256 + (mc + 1) * 128],
                                      xstage[:, mc, hc * 128:(hc + 1) * 128]))
                P.op("pe", tr_fn(items), reads=["xstage", "ident"], writes=[("ps", b)])
                copy_op(ev_eng(), KT[slot][:, hc2 * 2:hc2 * 2 + 2, :], PS[b][:, :].rearrange("p (i m) -> p i m", i=2),
                        [], [("ps", b), ("KT", slot)])

        def attn(tg, l, kind):
            T, NS, SL = tg.T, tg.NS, tg.SL
            pre_norm_deferred(tg, l, G_ATTNPRE)

            def evq(b, c):
                P.op("dve", lambda e: e.tensor_tensor(out=catq[:, c, 0:T], in0=PS[b][:, 0:T], in1=rstd[:, 0, 0:T],
                                                      op=ALU.mult),
                     reads=[("rstd", 0)], writes=[("ps", b), ("catq", c)])
            dense_1024(tg, "w_q", l, hT, "hT", evq, hook=lambda: pre_norm_deferred2(tg))
            units = []
            for s in range(NS):
                for hd in range(4):
                    units.append((s, hd))

            def geom(s):
                if kind == "p":
                    return l, 0, T
                return s % 2, s * SL, SL

            def stage_a(u, s, hd):
                slot, c0, cn = geom(s)
                if kind != "p" and hd == 0:
                    load_sample_kv(l, s)
                pti = u % 2
                pt = PT[:, pti]
                ptk = ("PT", pti)
                for mc in range(2):
                    b = nb()
                    P.op("pe", mm_fn(PS[b][:, 0:cn], [(KT[slot][:, 2 * hd + ec, mc * 128:(mc + 1) * 128],
                                                        catq[:, 2 * hd + ec, c0:c0 + cn]) for ec in range(2)]),
                         reads=[("KT", slot), ("catq", 2 * hd), ("catq", 2 * hd + 1)], writes=[("ps", b)])
                    P.op("act", lambda e, b=b, mc=mc, pt=pt, cn=cn: e.activation(out=pt[:, mc, 0:cn], in_=PS[b][:, 0:cn],
                                                                                 func=AF.Exp, scale=1.0 / 16.0),
                         writes=[("ps", b), ptk])

            def stage_b(u, s, hd):
                slot, c0, cn = geom(s)
                pti = u % 2
                pt = PT[:, pti]
                ptk = ("PT", pti)
                rk = ("rden", pti)
                rd = rden[:, pti, 0:cn]
                b = nb()
                P.op("pe", mm_fn(PS[b][:, 0:cn], [(ones1[:], pt[:, mc, 0:cn]) for mc in range(2)]),
                     reads=[ptk, "ones1"], writes=[("ps", b)])
                P.op("act", lambda e, b=b, rd=rd, cn=cn: e.activation(out=rd, in_=PS[b][:, 0:cn], func=AF.Ln),
                     writes=[("ps", b), rk])
                P.op("act", lambda e, rd=rd: e.activation(out=rd, in_=rd, func=AF.Exp, scale=-1.0),
                     reads=[rk], writes=[rk])
                for ec in range(2):
                    b = nb()
                    P.op("pe", mm_fn(PS[b][:, 0:cn], [(VS[slot][:, mc, hd * 256 + ec * 128: hd * 256 + (ec + 1) * 128],
                                                        pt[:, mc, 0:cn]) for mc in range(2)]),
                         reads=[("VS", slot), ptk], writes=[("ps", b)])
                    P.op("dve", lambda e, b=b, ec=ec, rd=rd, hd=hd, c0=c0, cn=cn: e.tensor_tensor(
                        out=hT[:, 2 * hd + ec, c0:c0 + cn], in0=PS[b][:, 0:cn], in1=rd, op=ALU.mult),
                        reads=[rk], writes=[("ps", b), ("hT", 2 * hd + ec)])

            for u, (s, hd) in enumerate(units):
                stage_a(u, s, hd)
                if u >= 1:
                    stage_b(u - 1, *units[u - 1])
            stage_b(len(units) - 1, *units[-1])
            dense_1024(tg, "w_o", l, hT, "hT", lambda b, c: evac_y(tg, b, c, l, G_ATTNPOST))
            post_norm(tg, l, G_ATTNPOST)

        def ffn(tg, l, kind, first):
            T, NS, SL = tg.T, tg.NS, tg.SL
            act = AV(0, GCH * T, BF16).rearrange("p (j t) -> p j t", j=GCH)
            U = [AV(5632 + i * 516, NS * (2 + SL)).rearrange("p (s e) -> p s e", s=NS) for i in range(4)]
            tb = [AV(7700 + i * 512, T) for i in range(4)]
            sg = [AV(9760 + i * 512, T) for i in range(4)]
            uc = [0]

            def v3(ap2d):
                return ap2d.rearrange("p (s l) -> p s l", s=NS)

            pre_norm(tg, l, G_FFNPRE)
            arena_phase()

            def conv_chunk(b, c):
                i = uc[0]
                uc[0] = (i + 1) % 4
                if kind == "p":
                    hin = ffnh_p[l][:, c * 2:c * 2 + 2]
                    hk = ("ffnh", l, c)
                    P.op("pool", lambda e: e.tensor_copy(out=U[i][:, 0, 0:2], in_=hin), reads=[hk, ("U", i)], writes=[("Uh", i)])
                else:
                    hin = ffnh_s[l][:, c // 16, (c % 16) * 8:(c % 16) * 8 + 8].rearrange("p (s r) -> p s r", s=4)
                    hk = ("ffnhs", l, c // 16)
                    P.op("pool", lambda e: e.tensor_copy(out=U[i][:, :, 0:2], in_=hin), reads=[hk, ("U", i)], writes=[("Uh", i)])
                P.op("act", lambda e: e.activation(out=U[i][:, :, 2:2 + SL], in_=v3(PS[b][:, 0:T]), func=AF.Copy),
                     reads=[("Uh", i)], writes=[("ps", b), ("U", i)])
                if c % 2 == 0:
                    P.op("act", lambda e: e.activation(out=tb[i], in_=PS[b][:, 0:T], func=AF.Identity,
                                                       scale=cc(l, FFNW + 2 * FCH + c), bias=cc(l, FFNB + c)),
                         reads=["cst"], writes=[("ps", b), ("tb", i)])
                else:
                    P.op("dve", lambda e: e.tensor_scalar(out=v3(tb[i]), in0=U[i][:, :, 2:2 + SL],
                                                          scalar1=cc(l, FFNW + 2 * FCH + c), scalar2=cc(l, FFNB + c),
                                                          op0=ALU.mult, op1=ALU.add),
                         reads=["cst", ("U", i)], writes=[("tb", i)])
                P.op("dve", lambda e: e.scalar_tensor_tensor(out=v3(tb[i]), in0=U[i][:, :, 1:1 + SL],
                                                              scalar=cc(l, FFNW + 1 * FCH + c), in1=v3(tb[i]),
                                                              op0=ALU.mult, op1=ALU.add),
                     reads=[("U", i), ("Uh", i), ("tb", i), "cst"], writes=[("tb", i)])
                P.op("dve", lambda e: e.scalar_tensor_tensor(out=v3(tb[i]), in0=U[i][:, :, 0:SL],
                                                              scalar=cc(l, FFNW + 0 * FCH + c), in1=v3(tb[i]),
                                                              op0=ALU.mult, op1=ALU.add),
                     reads=[("U", i), ("Uh", i), ("tb", i), "cst"], writes=[("tb", i)])
                if kind == "p":
                    P.op("pool", lambda e: e.tensor_copy(out=ffnh_p[l][:, c * 2:c * 2 + 2], in_=U[i][:, 0, SL:SL + 2]),
                         reads=[("U", i)], writes=[hk])
                else:
                    P.op("pool", lambda e: e.tensor_copy(
                        out=ffnh_s[l][:, c // 16, (c % 16) * 8:(c % 16) * 8 + 8].rearrange("p (s r) -> p s r", s=4),
                        in_=U[i][:, :, SL:SL + 2]),
                        reads=[("U", i)], writes=[hk])
                return i

            pending = []
            DELAY = 2

            def flush(n):
                while len(pending) > n:
                    pending.pop(0)()

            def gate_chunk(b, j, gi):
                i = conv_chunk(b, j)
                pending.append(lambda: P.op("act", lambda e: e.activation(out=sg[gi], in_=tb[i], func=AF.Silu),
                                            reads=[("tb", i)], writes=[("sg", gi)]))
                flush(DELAY)

            def val_chunk(b, j, gi):
                i = conv_chunk(b, GCH + j)
                pending.append(lambda: P.op("pool", lambda e: e.tensor_tensor(out=act[:, j, 0:T], in0=tb[i], in1=sg[gi],
                                                                              op=ALU.mult),
                                            reads=[("tb", i), ("sg", gi)], writes=[("act", j)]))
                flush(DELAY)

            def up_block(runs, chunks, first_blk=False):
                s = load_block("w_up", l, 0, 8, runs)
                if first_blk:
                    banks = [nb() for _ in range(4)]
                    mm_block_kc_outer(s, banks, T, lambda kc: hT[:, kc, 0:T], lambda kc: ("hT", kc))
                for mo, (kd, j, gi) in enumerate(chunks):
                    if first_blk:
                        b = banks[mo]
                    else:
                        b = nb()
                        P.op("pe", mm_fn(PS[b][:, 0:T], [(wr[s][:, kc, mo * 128:(mo + 1) * 128], hT[:, kc, 0:T])
                                                           for kc in range(NCH)]),
                             reads=[("wr", s)] + [("hT", kc) for kc in range(NCH)], writes=[("ps", b)])
                    if kd == "g":
                        gate_chunk(b, j, gi)
                    else:
                        val_chunk(b, j, gi)

            for q in range(5):
                up_block([(q * 512, 512)], [("g", q * 4 + i, i) for i in range(4)], first_blk=(q == 0))
                up_block([(DFF + q * 512, 512)], [("v", q * 4 + i, i) for i in range(4)])
            up_block([(20 * 128, 256), (DFF + 20 * 128, 256)], [("g", 20, 0), ("g", 21, 1), ("v", 20, 0), ("v", 21, 1)])
            flush(0)
            P.op("act", lambda e: e.activation(out=scratch[:, 3:4], in_=scratch[:, 2:3], func=AF.Ln),
                 reads=["scr2"], writes=["scr3"])
            for half in range(2):
                banks = [nb() for _ in range(4)]
                kgroups = [(0, 8), (8, 8), (16, 6)]
                for gi, (k0, kn) in enumerate(kgroups):
                    s = load_block("w_down", l, k0, kn, [(half * 512, 512)])
                    if half == 0:
                        mm_block_kc_outer(s, banks, T, lambda kc: act[:, kc, 0:T], lambda kc: ("act", kc), kn=kn,
                                          start=(gi == 0), stop=(gi == 2), k0=k0)
                        continue
                    for mo in range(4):
                        b = banks[mo]
                        P.op("pe", mm_fn(PS[b][:, 0:T], [(wr[s][:, kk, mo * 128:(mo + 1) * 128], act[:, k0 + kk, 0:T])
                                                           for kk in range(kn)], start=(gi == 0), stop=(gi == 2)),
                             reads=[("wr", s)] + [("act", k0 + kk) for kk in range(kn)], writes=[("ps", b)])
                for mo in range(4):
                    evac_y(tg, banks[mo], half * 4 + mo, l, G_FFNPOST)
            post_norm(tg, l, G_FFNPOST)

        def store_states(l, kind, b_idx):
            def tr_store(src_ap, key, nvalid, dst):
                b = nb()
                P.op("pe", tr_fn([(PS[b][:, 0:128], src_ap)]), reads=[key, "ident"], writes=[("ps", b)])
                r = next_row()
                P.op("act", lambda e: e.activation(out=rowt[r][:, 0:128], in_=PS[b][:, 0:128], func=AF.Copy),
                     writes=[("ps", b), ("rowt", r)])
                P.op("act", lambda e: e.dma_start(out=dst, in_=rowt[r][0:nvalid, 0:128]), reads=[("rowt", r)],
                     dsem=("rowst", r))
                P.op("pool", lambda e: e.memset(rowt[r][:, 0:128], 0.0), reads=[], writes=[("rowt", r)])
            if kind == "p":
                tr_store(poolh_p[l][:, :], ("poolh", l), 60,
                         poolp[l, b_idx].rearrange("r (g p) -> g r p", p=128))
                tr_store(convh_p[l][:, :], ("convh", l), 8,
                         convp[l, b_idx].rearrange("r (j p) -> j r p", p=128))
                b = nb()
                P.op("pe", tr_fn([(PS[b][:, 0:128], ffnh_p[l][:, :])]),
                     reads=[("ffnh", l, c) for c in range(FCH)] + ["ident"], writes=[("ps", b)])
                r = next_row()
                P.op("act", lambda e: e.activation(out=rowt[r][:, 0:128], in_=PS[b][:, 0:128], func=AF.Copy),
                     writes=[("ps", b), ("rowt", r)])
                P.op("act", lambda e: e.dma_start(out=ffnp[l, b_idx].rearrange("r (c p) -> c r p", p=128),
                                                  in_=rowt[r][0:88, 0:128]), reads=[("rowt", r)], dsem=("rowst", r))
                P.op("pool", lambda e: e.memset(rowt[r][:, 0:128], 0.0), reads=[], writes=[("rowt", r)])
            else:
                for w in range(2):
                    tr_store(poolh_s[l][:, w, :], ("poolhs", l, w), 120,
                             pools[l].rearrange("s r (g p) -> g s r p", p=128)[2 * w:2 * w + 2])
                tr_store(convh_s[l][:, :], ("convhs", l), 32,
                         convs[l].rearrange("s r (j p) -> j s r p", p=128))
                for w in range(3):
                    ncl = 16 if w < 2 else 12
                    tr_store(ffnh_s[l][:, w, :], ("ffnhs", l, w), ncl * 8,
                             ffns[l].rearrange("s r (c p) -> c s r p", p=128)[16 * w:16 * w + ncl])

        def load_sample_ffn_state(l):
            for grp in range(11):
                r = next_row()
                P.op("sp", lambda e, r=r, grp=grp: e.dma_start(
                    out=rowt[r][0:8, :], in_=sffn[l].rearrange("s r d -> (s r) d")[:, grp * 512:(grp + 1) * 512]),
                    reads=[("rowt", r)], writes=[("rowt", r)], dsem=("rowt", r))
                b = nb()
                P.op("pe", tr_fn([(PS[b][:, i * 128:(i + 1) * 128], rowt[r][:, i * 128:(i + 1) * 128]) for i in range(4)]),
                     reads=[("rowt", r), "ident"], writes=[("ps", b)])
                c0 = grp * 4
                w = c0 // 16
                cl = c0 % 16
                P.op("dve", lambda e, b=b, w=w, cl=cl: e.tensor_copy(
                    out=ffnh_s[l][:, w, cl * 8:(cl + 4) * 8].rearrange("p (c q) -> p c q", c=4),
                    in_=PS[b][:, :].rearrange("p (c q) -> p c q", q=128)[:, :, 0:8]),
                    writes=[("ps", b), ("ffnhs", l, w)])

        def kv_prep(bi):
            mstage = AV(0, 2 * D).rearrange("p (m d) -> p m d", m=2)
            mT = AV(2048, NCH * NMEM).rearrange("p (c m) -> p c m", c=NCH)
            mh = AV(5120, L * NCH * NMEM, BF16).rearrange("p (l c m) -> p l c m", l=L, c=NCH)
            rm = AV(7168, NMEM)
            stg = [AV(7424 + i * 512, 512) for i in range(4)]
            sc = [0]
            arena_phase()
            P.op("sp", lambda e: e.dma_start(out=mstage, in_=memp[bi].rearrange("(m p) d -> p m d", p=128)),
                 writes=["mstage"], dsem="mst")
            for c2 in range(4):
                b = nb()
                items = []
                for i in range(2):
                    c = c2 * 2 + i
                    for mc in range(2):
                        items.append((PS[b][:, i * 256 + mc * 128: i * 256 + (mc + 1) * 128],
                                      mstage[:, mc, c * 128:(c + 1) * 128]))
                P.op("pe", tr_fn(items), reads=["mstage", "ident"], writes=[("ps", b)])
                copy_op(ev_eng(), mT[:, c2 * 2:c2 * 2 + 2, :], PS[b][:, :].rearrange("p (i m) -> p i m", i=2),
                        [], [("ps", b), ("mT", c2)])
            P.op("act", lambda e: e.activation(out=sq[:, :, 0:NMEM], in_=mT, func=AF.Square),
                 reads=[("mT", i) for i in range(4)], writes=[("sq", c) for c in range(NCH)])
            b = nb()
            P.op("pe", mm_fn(PS[b][:, 0:NMEM], [(onesb[:], sq[:, c, 0:NMEM]) for c in range(NCH)]),
                 reads=[("sq", c) for c in range(NCH)] + ["onesb"], writes=[("ps", b)])
            P.op("act", lambda e, b=b: e.activation(out=rm, in_=PS[b][:, 0:NMEM], func=AF.Ln, bias=cst[:, EPSC:EPSC + 1], scale=1.0),
                 reads=["cst"], writes=[("ps", b), "rm"])
            P.op("act", lambda e: e.activation(out=rm, in_=rm, func=AF.Exp, scale=-0.5), reads=["rm"], writes=["rm"])
            for l in range(NL):
                for c in range(NCH):
                    P.op("dve", lambda e, l=l, c=c: e.scalar_tensor_tensor(out=mh[:, l, c, :], in0=mT[:, c, :],
                                                                           scalar=cc(l, G_MEM + c), in1=rm,
                                                                           op0=ALU.mult, op1=ALU.mult),
                         reads=[("mT", c // 2), "rm", "cst"], writes=[("mh", l, c)])
            for l in range(NL):
                mhk = [("mh", l, c) for c in range(NCH)]
                for name, dst in (("w_k", mk), ("w_v", mv)):
                    for blk in range(2):
                        s = load_block(name, l, 0, 8, [(blk * 512, 512)])
                        if name == "w_k":
                            for mo in range(4):
                                b = nb()
                                P.op("pe", mm_fn(PS[b][:, 0:NMEM], [(wr[s][:, kc, mo * 128:(mo + 1) * 128], mh[:, l, kc, :])
                                                                     for kc in range(NCH)]),
                                     reads=[("wr", s)] + mhk, writes=[("ps", b)])
                                copy_op(ev_eng(), KT[l][:, blk * 4 + mo, :], PS[b][:, 0:NMEM], [], [("ps", b), ("KT", l)])
                        for mc in range(2):
                            b = nb()
                            P.op("pe", mm_fn(PS[b][:, :], [(mh[:, l, kc, mc * 128:(mc + 1) * 128], wr[s][:, kc, :])
                                                             for kc in range(NCH)]),
                                 reads=[("wr", s)] + mhk, writes=[("ps", b)])
                            si = sc[0]
                            sc[0] = (si + 1) % 4
                            P.op("act", lambda e, b=b, si=si: e.activation(out=stg[si], in_=PS[b][:, :], func=AF.Copy),
                                 writes=[("ps", b), ("stg", si)])
                            if name == "w_v":
                                P.op("pool", lambda e, si=si, l=l, mc=mc, blk=blk: e.tensor_copy(
                                    out=VS[l][:, mc, blk * 512:(blk + 1) * 512], in_=stg[si]),
                                    reads=[("stg", si)], writes=[("VS", l)])
                            P.op("act", lambda e, si=si, l=l, mc=mc, blk=blk, dst=dst: e.dma_start(
                                out=dst[l, bi, mc * 128:(mc + 1) * 128, blk * 512:(blk + 1) * 512], in_=stg[si]),
                                reads=[("stg", si)], dsem=("stg", si))

        def run_tile(kind, bi, ti, last):
            if kind == "p":
                tg = TG(512, 1, 512)
                src = xp[bi, ti * 512:(ti + 1) * 512, :].rearrange("(tb p) d -> p tb d", p=128)
                ntb = 4
            else:
                tg = TG(128, 4, 32)
                src = xs.rearrange("(tb p) d -> p tb d", p=128)
                ntb = 1
            T = tg.T
            P.op("sp", lambda e: e.dma_start(out=xstage[:, 0:ntb, :], in_=src), writes=["xstage", "xstageV"], dsem="xst")
            for c in range(NCH):
                b = nb()
                P.op("pe", tr_fn([(PS[b][:, tb * 128:(tb + 1) * 128], xstage[:, tb, c * 128:(c + 1) * 128])
                                  for tb in range(ntb)]),
                     reads=["xstage", "ident"], writes=[("ps", b)])
                copy_op(ev_eng(), xT[:, c, 0:T], PS[b][:, 0:T], [], [("ps", b), ("xT", c)])
            for l in range(NL):
                first = (kind == "p" and ti == 0)
                if kind == "s":
                    load_sample_ffn_state(l)
                mix(tg, l, kind, first)
                attn(tg, l, kind)
                ffn(tg, l, kind, first)
                if last:
                    store_states(l, kind, bi)
            ystage = yv[:, :, :].rearrange("p c t -> p (c t)").rearrange("p (tb d) -> p tb d", d=D)
            for tb in range(ntb):
                for half in range(2):
                    b = nb()
                    P.op("pe", tr_fn([(PS[b][:, k * 128:(k + 1) * 128], xT[:, half * 4 + k, tb * 128:(tb + 1) * 128])
                                      for k in range(4)]),
                         reads=[("xT", half * 4 + k) for k in range(4)] + ["ident"], writes=[("ps", b)])
                    copy_op(ev_eng(), ystage[:, tb, half * 512:(half + 1) * 512], PS[b][:, :], [],
                            [("ps", b), ("ystg", tb, half)] + ([("yv", c) for c in range(NCH)] if (tb == 0 and half == 0) else []))
                if kind == "p":
                    dst = yp[bi, ti * 512 + tb * 128: ti * 512 + (tb + 1) * 128, :]
                else:
                    dst = ys[:, :]
                P.op("act", lambda e, tb=tb, dst=dst: e.dma_start(out=dst, in_=ystage[:, tb, :]),
                     reads=[("ystg", tb, 0), ("ystg", tb, 1)] + [("yv", c) for c in range(NCH)], dsem=("yst", tb))

        if do_sample:
            for l in range(NL):
                P.op("pool", lambda e, l=l: e.memset(poolh_s[l][:], 0.0), writes=[("poolhs", l, 0), ("poolhs", l, 1)])
                P.op("pool", lambda e, l=l: e.memset(convh_s[l][:], 0.0), writes=[("convhs", l)])
                P.op("pool", lambda e, l=l: e.memset(ffnh_s[l][:], 0.0), writes=[("ffnhs", l, w) for w in range(3)])
            run_tile("s", 0, 0, True)
            assert fp_cur[0] == len(fp_seq) and fp_emitted[0] == len(fp_seq)
            first_pass[0] = False
        for bi in range(NPB):
            kv_prep(bi)
            for l in range(NL):
                P.op("pool", lambda e, l=l: e.memset(poolh_p[l][:], 0.0), writes=[("poolh", l)])
                P.op("pool", lambda e, l=l: e.memset(convh_p[l][:], 0.0), writes=[("convh", l)])
                P.op("pool", lambda e, l=l: e.memset(ffnh_p[l][:], 0.0), writes=[("ffnh", l, c) for c in range(FCH)])
            for ti in range(NT):
                run_tile("p", bi, ti, ti == NT - 1)
        P.emit(nc)
    return nc, P


def _pack_consts(inp):
    cst = np.zeros((128, NCONST), np.float32)

    def put(col, vec):
        n = vec.shape[0] // 128
        cst[:, col:col + n] = vec.reshape(n, 128).T

    for l in range(L):
        base = l * CL
        put(base + G_MIXPRE, inp["g_mix_pre"][l])
        put(base + G_MIXPOST, inp["g_mix_post"][l])
        put(base + G_ATTNPRE, inp["g_attn_pre"][l])
        put(base + G_ATTNPOST, inp["g_attn_post"][l])
        put(base + G_MEM, inp["g_mem"][l])
        put(base + G_FFNPRE, inp["g_ffn_pre"][l])
        put(base + G_FFNPOST, inp["g_ffn_post"][l])
        put(base + POOLSC, inp["pool_scale"][l])
        for k in range(3):
            put(base + CONVW + k * 4, inp["conv_w"][l, k])
            put(base + FFNW + k * FCH, inp["ffn_conv_w"][l, k])
        put(base + CONVB, inp["conv_b"][l])
        put(base + FFNB, inp["ffn_conv_b"][l])
    cst[:, INVCNT:INVCNT + 15] = (1.0 / np.arange(1, 16, dtype=np.float64)).astype(np.float32)[None, :]
    cst[:, EPSC] = EPS
    return cst


_CACHE = {}


def kernel(**inputs):
    inp = {k: np.asarray(v) for k, v in inputs.items()}
    if "nc" not in _CACHE:
        _CACHE["nc"] = build_program()[0]
    nc = _CACHE["nc"]
    cst = _pack_consts(inp)
    shared = {
        "cst": cst,
        "w_in": np.ascontiguousarray(inp["w_in"]),
        "w_pool": np.ascontiguousarray(inp["w_pool"]),
        "w_out": np.ascontiguousarray(inp["w_out"]),
        "w_q": np.ascontiguousarray(inp["w_q"].reshape(L, D, D)),
        "w_k": np.ascontiguousarray(inp["w_k"].reshape(L, D, D)),
        "w_v": np.ascontiguousarray(inp["w_v"].reshape(L, D, D)),
        "w_o": np.ascontiguousarray(inp["w_o"].reshape(L, D, D)),
        "w_up": np.ascontiguousarray(inp["w_up"]),
        "w_down": np.ascontiguousarray(inp["w_down"]),
    }
    in_maps = []
    for i in range(NCORES):
        m = dict(shared)
        m["xp"] = np.ascontiguousarray(inp["x_prompt"][PB * i:PB * (i + 1)])
        m["xs"] = np.ascontiguousarray(inp["x_sample"][SB * i:SB * (i + 1)].reshape(SB * SSEQ, D))
        m["memp"] = np.ascontiguousarray(inp["mem_prompt"][PB * i:PB * (i + 1)])
        m["ck"] = np.ascontiguousarray(inp["cache_mem_k"][:, SB * i:SB * (i + 1)].reshape(L, SB, NMEM, D))
        m["cv"] = np.ascontiguousarray(inp["cache_mem_v"][:, SB * i:SB * (i + 1)].reshape(L, SB, NMEM, D))
        m["spool"] = np.ascontiguousarray(inp["state_pool"][:, SB * i:SB * (i + 1)])
        m["sconv"] = np.ascontiguousarray(inp["state_conv"][:, SB * i:SB * (i + 1)])
        m["sffn"] = np.ascontiguousarray(inp["state_ffn_conv"][:, SB * i:SB * (i + 1)])
        in_maps.append(m)
    res = run_bass_kernel_spmd(nc, in_maps, core_ids=list(range(NCORES)))
    R = res.results

    def cat(name, axis):
        return np.concatenate([np.asarray(r[name]) for r in R], axis=axis)

    y_prompt = cat("yp", 0)
    y_sample = cat("ys", 0).reshape(NCORES * SB, SSEQ, D)
    mem_k = cat("mk", 1).reshape(L, NCORES * PB, NMEM, 4, 256)
    mem_v = cat("mv", 1).reshape(L, NCORES * PB, NMEM, 4, 256)
    pool_p = cat("poolp", 1)
    conv_p = cat("convp", 1)
    ffn_p = cat("ffnp", 1)
    pool_s = cat("pools", 1)
    conv_s = cat("convs", 1)
    ffn_s = cat("ffns", 1)
    return (y_prompt, y_sample, mem_k, mem_v, pool_p, conv_p, ffn_p, pool_s, conv_s, ffn_s)
```

```python
import contextlib
import numpy as np
import concourse.bass as bass
import concourse.mybir as mybir
from concourse.bass_utils import run_bass_kernel_spmd

F32 = mybir.dt.float32
BF16 = mybir.dt.bfloat16
AF = mybir.ActivationFunctionType
ALU = mybir.AluOpType

ENGINES = ("pe", "act", "dve", "pool", "sp")

L = 2
D = 1024
SEQ = 2048
NMEM = 256
DFF = 2816
NCH = 8
FCH = 44
GCH = 22
NCORES = 8
PB = 2
SB = 4
SSEQ = 32
EPS = 1e-6

CL = 252
G_MIXPRE, G_MIXPOST, G_ATTNPRE, G_ATTNPOST, G_MEM, G_FFNPRE, G_FFNPOST = 0, 8, 16, 24, 32, 40, 48
POOLSC, CONVW, CONVB, FFNW, FFNB = 56, 60, 72, 76, 208
INVCNT = L * CL
EPSC = INVCNT + 15
NCONST = EPSC + 1
SEM_EPOCH = 1500


class Op:
    __slots__ = ("id", "eng", "fn", "deps", "signal", "dsem", "ev", "pos", "vc")

    def __init__(self, id, eng, fn, dsem):
        self.id = id
        self.eng = eng
        self.fn = fn
        self.deps = set()
        self.signal = False
        self.dsem = dsem
        self.ev = None
        self.pos = None
        self.vc = None


class Prog:
    def __init__(self):
        self.ops = []
        self.last_write = {}
        self.readers = {}
        self.last_dma = {}
        self.extra = ()

    def op(self, eng, fn, reads=(), writes=(), dsem=None, chain=True, noextra=False):
        o = Op(len(self.ops), eng, fn, dsem)
        if self.extra and not noextra:
            reads = list(reads) + list(self.extra)
        deps = o.deps
        lw = self.last_write
        rd = self.readers
        for k in reads:
            w = lw.get(k)
            if w is not None:
                deps.add(w)
        for k in writes:
            w = lw.get(k)
            if w is not None:
                deps.add(w)
            r = rd.get(k)
            if r:
                deps.update(r)
        for k in reads:
            rd.setdefault(k, []).append(o.id)
        for k in writes:
            lw[k] = o.id
            rd[k] = []
        if dsem is not None:
            p = self.last_dma.get(dsem)
            if p is not None and chain:
                deps.add(p)
            self.last_dma[dsem] = o.id
        deps.discard(o.id)
        self.ops.append(o)
        return o

    def emit(self, nc, final_wait_eng="sp", same_eng_skip=10 ** 9):
        ops = self.ops
        cnt = {e: 0 for e in ENGINES}
        for o in ops:
            o.pos = cnt[o.eng]
            cnt[o.eng] += 1
        for o in ops:
            if o.dsem is not None:
                continue
            drop = []
            for d in o.deps:
                od = ops[d]
                if od.eng == o.eng and od.dsem is None:
                    if o.eng == "pe" or (o.pos - od.pos) >= same_eng_skip:
                        drop.append(d)
            for d in drop:
                o.deps.discard(d)
        for o in ops:
            for d in o.deps:
                ops[d].signal = True
        ecnt = {e: 0 for e in ENGINES}
        dcnt = {}
        sem_names = set()
        for o in ops:
            if o.dsem is not None:
                dcnt[o.dsem] = dcnt.get(o.dsem, 0) + 16
                o.ev = (("d", o.dsem), dcnt[o.dsem])
                sem_names.add(("d", o.dsem))
            elif o.signal:
                k = ("e", o.eng, ecnt[o.eng] // SEM_EPOCH)
                o.ev = (k, ecnt[o.eng] % SEM_EPOCH + 1)
                ecnt[o.eng] += 1
                sem_names.add(k)
        sems = {}
        for k in sorted(sem_names, key=str):
            sems[k] = nc.alloc_semaphore(name=("s_" + "_".join(str(x) for x in k)).replace(" ", "").replace("'", "")
                                         .replace("(", "_").replace(")", "_").replace(",", "_"))
        clock = {e: {} for e in ENGINES}
        plan = {e: [] for e in ENGINES}
        for o in ops:
            ck = clock[o.eng]
            wd = {}
            for d in sorted(o.deps):
                od = ops[d]
                s, v = od.ev
                if ck.get(s, 0) >= v:
                    continue
                wd[s] = max(wd.get(s, 0), v)
                ck[s] = v
                for s2, v2 in od.vc.items():
                    if ck.get(s2, 0) < v2:
                        ck[s2] = v2
            o.vc = dict(ck)
            if o.ev is not None:
                o.vc[o.ev[0]] = max(o.vc.get(o.ev[0], 0), o.ev[1])
            plan[o.eng].append((o, list(wd.items())))
        finals = [(("d", k), v) for k, v in dcnt.items()]
        self.n_waits = sum(len(w) for e in ENGINES for _, w in plan[e])
        self.n_sems = len(sems)

        def run_engine(eng_name, eng):
            for o, waits in plan[eng_name]:
                for s, v in waits:
                    eng.wait_ge(sems[s], v)
                ins = o.fn(eng)
                if o.ev is not None:
                    s, v = o.ev
                    ins.then_inc(sems[s], 16 if o.dsem is not None else 1)
            if eng_name == final_wait_eng:
                for s, v in finals:
                    eng.wait_ge(sems[s], v)

        with nc.Block() as block:
            @block.tensor
            def _(e):
                run_engine("pe", e)

            @block.scalar
            def _(e):
                run_engine("act", e)

            @block.vector
            def _(e):
                run_engine("dve", e)

            @block.gpsimd
            def _(e):
                run_engine("pool", e)

            @block.sync
            def _(e):
                run_engine("sp", e)


class TG:
    def __init__(self, T, NS, SL):
        self.T, self.NS, self.SL = T, NS, SL


def build_program(NT=4, do_sample=True, NL=L, NPB=PB, NBUF=4):
    nc = bass.Bass("TRN2", target_bir_lowering=False)
    P = Prog()

    def din(name, shape, dt=F32):
        return nc.dram_tensor(name, list(shape), dt, kind="ExternalInput").ap()

    def dout(name, shape, dt=F32):
        return nc.dram_tensor(name, list(shape), dt, kind="ExternalOutput").ap()

    def dint(name, shape, dt=BF16):
        return nc.dram_tensor(name, list(shape), dt, kind="Internal").ap()

    xp = din("xp", [PB, SEQ, D])
    xs = din("xs", [SB * SSEQ, D])
    memp = din("memp", [PB, NMEM, D])
    ck = din("ck", [L, SB, NMEM, D])
    cv = din("cv", [L, SB, NMEM, D])
    spool = din("spool", [L, SB, 15, 512])
    sconv = din("sconv", [L, SB, 2, 512])
    sffn = din("sffn", [L, SB, 2, 2 * DFF])
    cst_d = din("cst", [128, NCONST])
    wshape = {"w_in": (D, 2048), "w_out": (D, D), "w_q": (D, D), "w_k": (D, D), "w_v": (D, D), "w_o": (D, D),
              "w_up": (D, 2 * DFF), "w_down": (DFF, D)}
    wf = {k: din(k, [L, v[0], v[1]]) for k, v in wshape.items()}
    wb = {k: dint(k + "_b", [L, v[0], v[1]]) for k, v in wshape.items()}
    w_pool_d = din("w_pool", [L, 4, 128, 128])

    yp = dout("yp", [PB, SEQ, D])
    ys = dout("ys", [SB * SSEQ, D])
    mk = dout("mk", [L, PB, NMEM, D])
    mv = dout("mv", [L, PB, NMEM, D])
    poolp = dout("poolp", [L, PB, 15, 512])
    convp = dout("convp", [L, PB, 2, 512])
    ffnp = dout("ffnp", [L, PB, 2, 2 * DFF])
    pools = dout("pools", [L, SB, 15, 512])
    convs = dout("convs", [L, SB, 2, 512])
    ffns = dout("ffns", [L, SB, 2, 2 * DFF])

    with contextlib.ExitStack() as st:
        def sb(name, shape, dt):
            return st.enter_context(nc.sbuf_tensor(name, shape, dt))

        ident = sb("ident", [128, 128], F32)
        onesb = sb("onesb", [128, 128], BF16)
        ones1 = sb("ones1", [128, 128], BF16)
        cst = sb("cst_sb", [128, NCONST], F32)
        wpool = sb("wpool", [128, L * 4, 128], BF16)
        poolh_p = [sb("poolh_p%d" % l, [128, 128], F32) for l in range(L)]
        convh_p = [sb("convh_p%d" % l, [128, 128], F32) for l in range(L)]
        ffnh_p = [sb("ffnh_p%d" % l, [128, 128], F32) for l in range(L)]
        poolh_s = [sb("poolh_s%d" % l, [128, 2, 128], F32) for l in range(L)]
        convh_s = [sb("convh_s%d" % l, [128, 128], F32) for l in range(L)]
        ffnh_s = [sb("ffnh_s%d" % l, [128, 3, 128], F32) for l in range(L)]
        xT = sb("xT", [128, NCH, 512], F32)
        hT = sb("hT", [128, NCH, 512], BF16)
        sq = sb("sq", [128, NCH, 512], BF16)
        rstd = sb("rstd", [128, 2, 512], F32)
        yv = sb("yv", [128, NCH, 512], F32)
        catq = sb("catq", [128, NCH, 512], BF16)
        PT = sb("PT", [128, 2, 2, 512], BF16)
        rden = sb("rden", [128, 2, 512], F32)
        KT = [sb("KT%d" % i, [128, NCH, NMEM], BF16) for i in range(2)]
        VS = [sb("VS%d" % i, [128, 2, D], BF16) for i in range(2)]
        xstage = sb("xstage", [128, 4, D], F32)
        wr = [sb("wr%d" % i, [128, 8, 512], BF16) for i in range(NBUF)]
        rowt = [sb("rowt%d" % i, [128, 512], F32) for i in range(4)]
        AW = 13568
        arena = sb("arena", [128, AW], F32)
        PS = [st.enter_context(nc.psum_tensor("ps%d" % i, [128, 512], F32)) for i in range(8)]

        def AV(off, n, dt=F32):
            if dt == F32:
                return arena[:, off:off + n]
            assert n % 2 == 0
            return arena[:, off:off + n // 2].bitcast(BF16)

        bank_ctr = [0]

        def nb():
            b = bank_ctr[0]
            bank_ctr[0] = (b + 1) % 8
            return b

        wslot = [0]
        rowc = [0]

        def next_row():
            r = rowc[0]
            rowc[0] = (r + 1) % 4
            return r

        evc = [0]

        def ev_eng():
            evc[0] ^= 1
            return "act" if evc[0] else "dve"

        scratch = sb("scratch", [128, 8], F32)

        def arena_phase():
            P.extra = ()
            P.op("pool", lambda e: e.memset(scratch[:, 0:1], 0.0), writes=["AP"])
            P.extra = ("AP",)

        def cc(l, off, n=1):
            return cst[:, l * CL + off: l * CL + off + n]

        def copy_op(eng, out, in_, reads, writes):
            if eng == "act":
                P.op("act", lambda e: e.activation(out=out, in_=in_, func=AF.Copy), reads=reads, writes=writes)
            else:
                P.op(eng, lambda e: e.tensor_copy(out=out, in_=in_), reads=reads, writes=writes)

        def mm_fn(bank_ap, pairs, start=True, stop=True):
            def f(e):
                last = None
                n = len(pairs)
                for i, (a, b) in enumerate(pairs):
                    last = e.matmul(bank_ap, lhsT=a, rhs=b, start=(start and i == 0), stop=(stop and i == n - 1))
                return last
            return f

        def tr_fn(items):
            def f(e):
                last = None
                for (o, i) in items:
                    last = e.transpose(out=o, in_=i, identity=ident[:])
                return last
            return f

        def mm_block_kc_outer(slot, banks, T, rhs_of, key_of, kn=NCH, start=True, stop=True, k0=0):
            for kk in range(kn):
                def f(e, kk=kk):
                    last = None
                    for mo, b in enumerate(banks):
                        last = e.matmul(PS[b][:, 0:T], lhsT=wr[slot][:, kk, mo * 128:(mo + 1) * 128], rhs=rhs_of(k0 + kk),
                                        start=(start and kk == 0), stop=(stop and kk == kn - 1))
                    return last
                P.op("pe", f, reads=[("wr", slot), key_of(k0 + kk)], writes=[("ps", b) for b in banks])

        P.op("pool", lambda e: e.memset(ident[:], 0.0), writes=["ident"])
        P.op("pool", lambda e: e.affine_select(out=ident[:], in_=ident[:], pattern=[[-1, 128]],
                                               compare_op=ALU.not_equal, fill=1.0, base=0, channel_multiplier=1),
             reads=["ident"], writes=["ident"])
        P.op("pool", lambda e: e.memset(onesb[:], 1.0 / D), writes=["onesb"])
        P.op("pool", lambda e: e.memset(ones1[:], 1.0), writes=["ones1"])
        for i in range(4):
            P.op("pool", lambda e, i=i: e.memset(rowt[i][:], 0.0), writes=[("rowt", i)])
        P.op("pool", lambda e: e.memset(scratch[:, 2:4], 1.0), writes=["scr2", "scr3"])
        P.op("sp", lambda e: e.dma_start(out=cst[:], in_=cst_d), writes=["cst"], dsem="cst")
        P.op("pool", lambda e: e.dma_start(out=wpool[:], in_=w_pool_d.rearrange("l g c d -> c (l g) d")),
             writes=["wpool"], dsem="wpool")

        def precast(name, l):
            rows = wshape[name][0]
            for rc in range(rows // 128):
                P.op("pool", lambda e, name=name, l=l, rc=rc: e.dma_start(
                    out=wb[name][l, rc * 128:(rc + 1) * 128, :], in_=wf[name][l, rc * 128:(rc + 1) * 128, :]),
                    writes=[("wb", name, l, rc)], dsem=("pc", name, l, rc // 8), chain=False)

        def wb_keys(name, l, k0=0, kn=None):
            rows = wshape[name][0] // 128
            if kn is None:
                kn = rows
            g0, g1 = k0 // 8, (k0 + kn - 1) // 8
            return [("wb", name, l, rc) for rc in range(g0 * 8, min(rows, (g1 + 1) * 8))]

        precasted = set()

        def ensure_precast(name, l):
            if (name, l) not in precasted:
                precasted.add((name, l))
                precast(name, l)

        def layer_block_seq(l):
            seq = []
            for blk in range(4):
                seq.append(("w_in", l, 0, 8, ((blk * 512, 512),)))
            for name in ("w_out", "w_q", "w_o"):
                for blk in range(2):
                    seq.append((name, l, 0, 8, ((blk * 512, 512),)))
            for q in range(5):
                seq.append(("w_up", l, 0, 8, ((q * 512, 512),)))
                seq.append(("w_up", l, 0, 8, ((DFF + q * 512, 512),)))
            seq.append(("w_up", l, 0, 8, ((20 * 128, 256), (DFF + 20 * 128, 256))))
            for half in range(2):
                for (k0, kn) in ((0, 8), (8, 8), (16, 6)):
                    seq.append(("w_down", l, k0, kn, ((half * 512, 512),)))
            return seq

        first_pass = [bool(do_sample)]
        fp_seq = [b for l in range(NL) for b in layer_block_seq(l)]
        fp_cur = [0]
        fp_emitted = [0]
        PD = 2

        def emit_block_dma(eng, name, l, k0, kn, runs, s):
            src = wb[name]
            off = 0
            first = None
            for ri, (c0, wd) in enumerate(runs):
                o = P.op(eng, lambda e, s=s, off=off, c0=c0, wd=wd: e.dma_start(
                    out=wr[s][:, 0:kn, off:off + wd],
                    in_=src[l, k0 * 128:(k0 + kn) * 128, c0:c0 + wd].rearrange("(k p) m -> p k m", p=128)),
                    reads=wb_keys(name, l, k0, kn), writes=[("wr", s)] if ri == 0 else [],
                    dsem=(("wrp" if eng == "pool" else "wr"), s), noextra=True, chain=(ri == 0))
                if ri == 0:
                    first = o
                else:
                    o.deps |= first.deps
                    P.last_write[("wr", s)] = o.id
                off += wd

        def load_block(name, l, k0, kn, runs):
            runs = tuple(runs)
            if first_pass[0]:
                idx = fp_cur[0]
                assert fp_seq[idx] == (name, l, k0, kn, runs), (idx, fp_seq[idx], (name, l, k0, kn, runs))
                while fp_emitted[0] <= min(idx + PD, len(fp_seq) - 1):
                    j = fp_emitted[0]
                    n2, l2, k02, kn2, runs2 = fp_seq[j]
                    ensure_precast(n2, l2)
                    emit_block_dma("sp", n2, l2, k02, kn2, runs2, j % NBUF)
                    for j2 in range(j + 1, len(fp_seq)):
                        if (fp_seq[j2][0], fp_seq[j2][1]) != (n2, l2):
                            ensure_precast(fp_seq[j2][0], fp_seq[j2][1])
                            break
                    fp_emitted[0] += 1
                    if fp_emitted[0] == len(fp_seq):
                        for ll in range(NL):
                            ensure_precast("w_k", ll)
                            ensure_precast("w_v", ll)
                fp_cur[0] += 1
                wslot[0] = fp_emitted[0] % NBUF
                return idx % NBUF
            ensure_precast(name, l)
            s = wslot[0]
            wslot[0] = (s + 1) % NBUF
            emit_block_dma("sp", name, l, k0, kn, runs, s)
            return s

        def sumsq_rstd(tg, sq_keys, ri):
            T = tg.T
            b = nb()
            for c in range(NCH):
                P.op("pe", mm_fn(PS[b][:, 0:T], [(onesb[:], sq[:, c, 0:T])], start=(c == 0), stop=(c == NCH - 1)),
                     reads=[("sq", c), "onesb"], writes=[("ps", b)])
            P.op("act", lambda e: e.activation(out=rstd[:, ri, 0:T], in_=PS[b][:, 0:T], func=AF.Ln,
                                               bias=cst[:, EPSC:EPSC + 1], scale=1.0),
                 reads=["cst"], writes=[("ps", b), ("rstd", ri)])
            P.op("act", lambda e: e.activation(out=rstd[:, ri, 0:T], in_=rstd[:, ri, 0:T], func=AF.Exp, scale=-0.5),
                 reads=[("rstd", ri)], writes=[("rstd", ri)])

        def pre_norm(tg, l, goff):
            T = tg.T
            for c in range(NCH):
                P.op("act", lambda e, c=c: e.activation(out=sq[:, c, 0:T], in_=xT[:, c, 0:T], func=AF.Square),
                     reads=[("xT", c)], writes=[("sq", c)])
            sumsq_rstd(tg, None, 0)
            for c in range(NCH):
                P.op("dve", lambda e, c=c: e.scalar_tensor_tensor(out=hT[:, c, 0:T], in0=xT[:, c, 0:T],
                                                                   scalar=cc(l, goff + c), in1=rstd[:, 0, 0:T],
                                                                   op0=ALU.mult, op1=ALU.mult),
                     reads=[("xT", c), ("rstd", 0), "cst"], writes=[("hT", c)])

        def pre_norm_deferred(tg, l, goff, need_rr=False):
            T = tg.T
            for c in range(NCH):
                P.op("act", lambda e, c=c: e.activation(out=hT[:, c, 0:T], in_=xT[:, c, 0:T], func=AF.Identity,
                                                        scale=cc(l, goff + c)),
                     reads=[("xT", c), "cst"], writes=[("hT", c)])
            for c in range(NCH):
                P.op("act", lambda e, c=c: e.activation(out=sq[:, c, 0:T], in_=xT[:, c, 0:T], func=AF.Square),
                     reads=[("xT", c)], writes=[("sq", c)])

        def pre_norm_deferred2(tg, need_rr=False):
            T = tg.T
            sumsq_rstd(tg, None, 0)
            if need_rr:
                P.op("dve", lambda e: e.tensor_tensor(out=rden[:, 1, 0:T], in0=rstd[:, 0, 0:T], in1=rstd[:, 0, 0:T],
                                                      op=ALU.mult),
                     reads=[("rstd", 0)], writes=[("rden", 1)])

        def evac_y(tg, b, c, l, goff):
            T = tg.T
            P.op("act", lambda e: e.activation(out=yv[:, c, 0:T], in_=PS[b][:, 0:T], func=AF.Identity,
                                               scale=cc(l, goff + c)),
                 reads=["cst"], writes=[("ps", b), ("yv", c)])
            P.op("act", lambda e: e.activation(out=sq[:, c, 0:T], in_=PS[b][:, 0:T], func=AF.Square),
                 writes=[("ps", b), ("sq", c)])

        def post_norm(tg, l, goff):
            T = tg.T
            sumsq_rstd(tg, None, 1)
            rb = rstd[:, 1:2, 0:T].broadcast_to([128, 2, T])
            for c in range(0, NCH, 2):
                P.op("dve", lambda e, c=c: e.tensor_tensor(out=yv[:, c:c + 2, 0:T], in0=yv[:, c:c + 2, 0:T], in1=rb,
                                                           op=ALU.mult),
                     reads=[("yv", c), ("yv", c + 1), ("rstd", 1)], writes=[("yv", c), ("yv", c + 1)])
                P.op("dve", lambda e, c=c: e.tensor_tensor(out=xT[:, c:c + 2, 0:T], in0=yv[:, c:c + 2, 0:T],
                                                           in1=xT[:, c:c + 2, 0:T], op=ALU.add),
                     reads=[("yv", c), ("yv", c + 1), ("xT", c), ("xT", c + 1)], writes=[("xT", c), ("xT", c + 1)])

        def dense_1024(tg, name, l, rhs_buf, rhs_key, evac, hook=None):
            T = tg.T
            for blk in range(2):
                s = load_block(name, l, 0, 8, [(blk * 512, 512)])
                banks = []
                if blk == 0:
                    banks = [nb() for _ in range(4)]
                    mm_block_kc_outer(s, banks, T, lambda kc: rhs_buf[:, kc, 0:T], lambda kc: (rhs_key, kc))
                else:
                    for mo in range(4):
                        b = nb()
                        banks.append(b)
                        P.op("pe", mm_fn(PS[b][:, 0:T], [(wr[s][:, kc, mo * 128:(mo + 1) * 128], rhs_buf[:, kc, 0:T])
                                                           for kc in range(NCH)]),
                             reads=[("wr", s)] + [(rhs_key, kc) for kc in range(NCH)], writes=[("ps", b)])
                if blk == 0 and hook is not None:
                    hook()
                for mo in range(4):
                    evac(banks[mo], blk * 4 + mo)

        def mix(tg, l, kind, first):
            T, NS, SL = tg.T, tg.NS, tg.SL
            EL = 15 + SL
            uA = AV(0, 4 * NS * EL).rearrange("p (g s e) -> p g s e", g=4, s=NS)
            Bg = AV(2112, 4 * T).rearrange("p (j t) -> p j t", j=4)
            CVb = AV(4160, 4 * NS * (2 + SL)).rearrange("p (j s e) -> p j s e", j=4, s=NS)
            pa = [AV(6224 + g * 528, NS * EL).rearrange("p (s e) -> p s e", s=NS) for g in range(4)]
            pb = [AV(6224 + (4 + g) * 528, NS * EL).rearrange("p (s e) -> p s e", s=NS) for g in range(4)]
            dd = AV(10448, 4 * T, BF16).rearrange("p (g t) -> p g t", g=4)
            ta = AV(11472, 4 * T).rearrange("p (j t) -> p j t", j=4)

            def v3(ap2d):
                return ap2d.rearrange("p (s l) -> p s l", s=NS)

            pre_norm_deferred(tg, l, G_MIXPRE)
            r3 = v3(rstd[:, 0, 0:T])
            rr3 = v3(rden[:, 1, 0:T])
            s0 = load_block("w_in", l, 0, 8, [(0, 512)])
            banks0 = [nb() for _ in range(4)]
            mm_block_kc_outer(s0, banks0, T, lambda kc: hT[:, kc, 0:T], lambda kc: ("hT", kc))
            pre_norm_deferred2(tg, need_rr=True)
            arena_phase()
            if kind == "p":
                P.op("pool", lambda e: e.tensor_copy(out=uA[:, :, 0, 0:15],
                                                     in_=poolh_p[l][:, 0:60].rearrange("p (g r) -> p g r", g=4)),
                     reads=[("poolh", l)], writes=[("uA", g) for g in range(4)])
                P.op("pool", lambda e: e.tensor_copy(out=CVb[:, :, 0, 0:2],
                                                     in_=convh_p[l][:, 0:8].rearrange("p (j r) -> p j r", j=4)),
                     reads=[("convh", l)], writes=[("CV", j) for j in range(4)])
            else:
                r = next_row()
                P.op("sp", lambda e, r=r: e.dma_start(out=rowt[r][0:60, :], in_=spool[l].rearrange("s r d -> (s r) d")),
                     reads=[("rowt", r)], writes=[("rowt", r)], dsem=("rowt", r))
                b = nb()
                P.op("pe", tr_fn([(PS[b][:, g * 128:(g + 1) * 128], rowt[r][:, g * 128:(g + 1) * 128]) for g in range(4)]),
                     reads=[("rowt", r), "ident"], writes=[("ps", b)])
                P.op("dve", lambda e, b=b: e.tensor_copy(
                    out=uA[:, :, :, 0:15],
                    in_=PS[b][:, :].rearrange("p (g q) -> p g q", q=128)[:, :, 0:60].rearrange("p g (s r) -> p g s r", s=4)),
                    writes=[("ps", b)] + [("uA", g) for g in range(4)])
                r2 = next_row()
                P.op("sp", lambda e, r2=r2: e.dma_start(out=rowt[r2][0:8, :], in_=sconv[l].rearrange("s r d -> (s r) d")),
                     reads=[("rowt", r2)], writes=[("rowt", r2)], dsem=("rowt", r2))
                b2 = nb()
                P.op("pe", tr_fn([(PS[b2][:, j * 128:(j + 1) * 128], rowt[r2][:, j * 128:(j + 1) * 128]) for j in range(4)]),
                     reads=[("rowt", r2), "ident"], writes=[("ps", b2)])
                P.op("dve", lambda e, b2=b2: e.tensor_copy(
                    out=CVb[:, :, :, 0:2],
                    in_=PS[b2][:, :].rearrange("p (j q) -> p j q", q=128)[:, :, 0:8].rearrange("p j (s r) -> p j s r", s=4)),
                    writes=[("ps", b2)] + [("CV", j) for j in range(4)])
            def pool_section():
                lo1 = [15, 13, 9, 1]
                for g in range(4):
                    lo = lo1[g]
                    P.op("pool", lambda e, g=g, lo=lo: e.tensor_tensor(out=pa[g][:, :, lo:EL], in0=uA[:, g, :, lo:EL],
                                                                       in1=uA[:, g, :, lo - 1:EL - 1], op=ALU.add),
                         reads=[("uA", g)], writes=[("pa", g)])
                lo2 = [None, 15, 11, 3]
                for g in range(1, 4):
                    lo = lo2[g]
                    P.op("pool", lambda e, g=g, lo=lo: e.tensor_tensor(out=pb[g][:, :, lo:EL], in0=pa[g][:, :, lo:EL],
                                                                       in1=pa[g][:, :, lo - 2:EL - 2], op=ALU.add),
                         reads=[("pa", g)], writes=[("pb", g)])
                lo3 = [None, None, 15, 7]
                for g in range(2, 4):
                    lo = lo3[g]
                    P.op("pool", lambda e, g=g, lo=lo: e.tensor_tensor(out=pa[g][:, :, lo:EL], in0=pb[g][:, :, lo:EL],
                                                                       in1=pb[g][:, :, lo - 4:EL - 4], op=ALU.add),
                         reads=[("pb", g), ("pa", g)], writes=[("pa", g)])
                P.op("pool", lambda e: e.tensor_tensor(out=pb[3][:, :, 15:EL], in0=pa[3][:, :, 15:EL],
                                                       in1=pa[3][:, :, 7:EL - 8], op=ALU.add),
                     reads=[("pa", 3), ("pb", 3)], writes=[("pb", 3)])
                fin = [pa[0], pb[1], pa[2], pb[3]]
                fkey = [("pa", 0), ("pb", 1), ("pa", 2), ("pb", 3)]
                for g in range(4):
                    w = 2 << g
                    P.op("dve", lambda e, g=g, w=w: e.scalar_tensor_tensor(
                        out=v3(dd[:, g, 0:T]), in0=fin[g][:, :, 15:EL], scalar=1.0 / w, in1=uA[:, g, :, 15:EL],
                        op0=ALU.mult, op1=ALU.subtract),
                        reads=[fkey[g], ("uA", g)], writes=[("dd", g)])
                    if first:
                        n = w - 1
                        P.op("dve", lambda e, g=g, n=n: e.tensor_tensor(out=fin[g][:, 0, 15:15 + n], in0=fin[g][:, 0, 15:15 + n],
                                                                        in1=cst[:, INVCNT:INVCNT + n], op=ALU.mult),
                             reads=[fkey[g], "cst", ("dd", g)], writes=[fkey[g]])
                        P.op("dve", lambda e, g=g, n=n: e.tensor_tensor(out=dd[:, g, 0:n], in0=fin[g][:, 0, 15:15 + n],
                                                                        in1=uA[:, g, 0, 15:15 + n], op=ALU.subtract),
                             reads=[fkey[g], ("uA", g)], writes=[("dd", g)])
                if kind == "p":
                    P.op("pool", lambda e: e.tensor_copy(out=poolh_p[l][:, 0:60].rearrange("p (g r) -> p g r", g=4),
                                                         in_=uA[:, :, 0, SL:SL + 15]),
                         reads=[("uA", g) for g in range(4)], writes=[("poolh", l)])
                else:
                    for w in range(2):
                        P.op("pool", lambda e, w=w: e.tensor_copy(
                            out=poolh_s[l][:, w, 0:120].rearrange("p (gl s r) -> p gl s r", gl=2, s=4),
                            in_=uA[:, 2 * w:2 * w + 2, :, SL:SL + 15]),
                            reads=[("uA", 2 * w), ("uA", 2 * w + 1)], writes=[("poolhs", l, w)])

            for blk in range(4):
                if blk > 0:
                    s = load_block("w_in", l, 0, 8, [(blk * 512, 512)])
                for mo in range(4):
                    if blk == 0:
                        b = banks0[mo]
                    else:
                        b = nb()
                        P.op("pe", mm_fn(PS[b][:, 0:T], [(wr[s][:, kc, mo * 128:(mo + 1) * 128], hT[:, kc, 0:T])
                                                           for kc in range(NCH)]),
                             reads=[("wr", s)] + [("hT", kc) for kc in range(NCH)], writes=[("ps", b)])
                    if blk == 0:
                        P.op("dve", lambda e, b=b, mo=mo: e.tensor_tensor(out=uA[:, mo, :, 15:15 + SL], in0=v3(PS[b][:, 0:T]),
                                                                          in1=r3, op=ALU.mult),
                             reads=[("rstd", 0)], writes=[("ps", b), ("uA", mo)])
                    elif blk == 1:
                        P.op("dve", lambda e, b=b, mo=mo: e.tensor_tensor(out=Bg[:, mo, :], in0=PS[b][:, 0:T],
                                                                          in1=rstd[:, 0, 0:T], op=ALU.mult),
                             reads=[("rstd", 0)], writes=[("ps", b), ("Bg", mo)])
                    elif blk == 2:
                        P.op("dve", lambda e, b=b, mo=mo: e.tensor_tensor(out=CVb[:, mo, :, 2:2 + SL], in0=v3(PS[b][:, 0:T]),
                                                                          in1=rr3, op=ALU.mult),
                             reads=[("rden", 1)], writes=[("ps", b), ("CV", mo)])
                    else:
                        P.op("dve", lambda e, b=b, mo=mo: e.tensor_tensor(out=CVb[:, mo, :, 2:2 + SL], in0=v3(PS[b][:, 0:T]),
                                                                          in1=CVb[:, mo, :, 2:2 + SL], op=ALU.mult),
                             writes=[("ps", b), ("CV", mo)])
                if blk == 0:
                    pool_section()
            for g in range(4):
                b = nb()
                P.op("pe", mm_fn(PS[b][:, 0:T], [(wpool[:, l * 4 + g, :], dd[:, g, 0:T])]),
                     reads=["wpool", ("dd", g)], writes=[("ps", b)])
                P.op("act", lambda e, b=b, g=g: e.activation(out=catq[:, g, 0:T], in_=PS[b][:, 0:T], func=AF.Identity,
                                                             scale=cc(l, POOLSC + g)),
                     reads=["cst"], writes=[("ps", b), ("catq", g)])
            for j in range(4):
                P.op("pool", lambda e, j=j: e.tensor_scalar(out=v3(ta[:, j, 0:T]), in0=CVb[:, j, :, 2:2 + SL],
                                                            scalar1=cc(l, CONVW + 2 * 4 + j), scalar2=cc(l, CONVB + j),
                                                            op0=ALU.mult, op1=ALU.add),
                     reads=[("CV", j), "cst"], writes=[("ta", j)])
            for j in range(4):
                P.op("dve", lambda e, j=j: e.scalar_tensor_tensor(out=v3(ta[:, j, 0:T]), in0=CVb[:, j, :, 1:1 + SL],
                                                                   scalar=cc(l, CONVW + 1 * 4 + j), in1=v3(ta[:, j, 0:T]),
                                                                   op0=ALU.mult, op1=ALU.add),
                     reads=[("CV", j), ("ta", j), "cst"], writes=[("ta", j)])
            for j in range(4):
                P.op("dve", lambda e, j=j: e.scalar_tensor_tensor(out=v3(ta[:, j, 0:T]), in0=CVb[:, j, :, 0:SL],
                                                                   scalar=cc(l, CONVW + 0 * 4 + j), in1=v3(ta[:, j, 0:T]),
                                                                   op0=ALU.mult, op1=ALU.add),
                     reads=[("CV", j), ("ta", j), "cst"], writes=[("ta", j)])
            for j in range(4):
                P.op("pool", lambda e, j=j: e.tensor_tensor(out=catq[:, 4 + j, 0:T], in0=ta[:, j, 0:T], in1=Bg[:, j, :],
                                                            op=ALU.mult),
                     reads=[("ta", j), ("Bg", j)], writes=[("catq", 4 + j)])
            if kind == "p":
                P.op("pool", lambda e: e.tensor_copy(out=convh_p[l][:, 0:8].rearrange("p (j r) -> p j r", j=4),
                                                     in_=CVb[:, :, 0, SL:SL + 2]),
                     reads=[("CV", j) for j in range(4)], writes=[("convh", l)])
            else:
                P.op("pool", lambda e: e.tensor_copy(out=convh_s[l][:, 0:32].rearrange("p (j s r) -> p j s r", j=4, s=4),
                                                     in_=CVb[:, :, :, SL:SL + 2]),
                     reads=[("CV", j) for j in range(4)], writes=[("convhs", l)])
            dense_1024(tg, "w_out", l, catq, "catq", lambda b, c: evac_y(tg, b, c, l, G_MIXPOST))
            post_norm(tg, l, G_MIXPOST)

        def load_sample_kv(l, s):
            slot = s % 2
            P.op("sp", lambda e: e.dma_start(out=xstage[:, 0:2, :], in_=ck[l, s].rearrange("(m p) d -> p m d", p=128)),
                 writes=["xstage"], dsem="xst")
            P.op("sp", lambda e: e.dma_start(out=xstage[:, 2:4, :], in_=cv[l, s].rearrange("(m p) d -> p m d", p=128)),
                 writes=["xstageV"], dsem="xstv")
            P.op("pool", lambda e: e.tensor_copy(out=VS[slot][:], in_=xstage[:, 2:4, :]),
                 reads=["xstageV"], writes=[("VS", slot)])
            for hc2 in range(4):
                b = nb()
                items = []
                for i in range(2):
                    hc = hc2 * 2 + i
                    for mc in range(2):
                        items.append((PS[b][:, i * 256 + mc * 128: i * 256 + (mc + 1) * 128],
                                      xstage[:, mc, hc * 128:(hc + 1) * 128]))
                P.op("pe", tr_fn(items), reads=["xstage", "ident"], writes=[("ps", b)])
                copy_op(ev_eng(), KT[slot][:, hc2 * 2:hc2 * 2 + 2, :], PS[b][:, :].rearrange("p (i m) -> p i m", i=2),
                        [], [("ps", b), ("KT", slot)])

        def attn(tg, l, kind):
            T, NS, SL = tg.T, tg.NS, tg.SL
            pre_norm_deferred(tg, l, G_ATTNPRE)

            def evq(b, c):
                P.op("dve", lambda e: e.tensor_tensor(out=catq[:, c, 0:T], in0=PS[b][:, 0:T], in1=rstd[:, 0, 0:T],
                                                      op=ALU.mult),
                     reads=[("rstd", 0)], writes=[("ps", b), ("catq", c)])
            dense_1024(tg, "w_q", l, hT, "hT", evq, hook=lambda: pre_norm_deferred2(tg))
            units = []
            for s in range(NS):
                for hd in range(4):
                    units.append((s, hd))

            def geom(s):
                if kind == "p":
                    return l, 0, T
                return s % 2, s * SL, SL

            def stage_a(u, s, hd):
                slot, c0, cn = geom(s)
                if kind != "p" and hd == 0:
                    load_sample_kv(l, s)
                pti = u % 2
                pt = PT[:, pti]
                ptk = ("PT", pti)
                for mc in range(2):
                    b = nb()
                    P.op("pe", mm_fn(PS[b][:, 0:cn], [(KT[slot][:, 2 * hd + ec, mc * 128:(mc + 1) * 128],
                                                        catq[:, 2 * hd + ec, c0:c0 + cn]) for ec in range(2)]),
                         reads=[("KT", slot), ("catq", 2 * hd), ("catq", 2 * hd + 1)], writes=[("ps", b)])
                    P.op("act", lambda e, b=b, mc=mc, pt=pt, cn=cn: e.activation(out=pt[:, mc, 0:cn], in_=PS[b][:, 0:cn],
                                                                                 func=AF.Exp, scale=1.0 / 16.0),
                         writes=[("ps", b), ptk])

            def stage_b(u, s, hd):
                slot, c0, cn = geom(s)
                pti = u % 2
                pt = PT[:, pti]
                ptk = ("PT", pti)
                rk = ("rden", pti)
                rd = rden[:, pti, 0:cn]
                b = nb()
                P.op("pe", mm_fn(PS[b][:, 0:cn], [(ones1[:], pt[:, mc, 0:cn]) for mc in range(2)]),
                     reads=[ptk, "ones1"], writes=[("ps", b)])
                P.op("act", lambda e, b=b, rd=rd, cn=cn: e.activation(out=rd, in_=PS[b][:, 0:cn], func=AF.Ln),
                     writes=[("ps", b), rk])
                P.op("act", lambda e, rd=rd: e.activation(out=rd, in_=rd, func=AF.Exp, scale=-1.0),
                     reads=[rk], writes=[rk])
                for ec in range(2):
                    b = nb()
                    P.op("pe", mm_fn(PS[b][:, 0:cn], [(VS[slot][:, mc, hd * 256 + ec * 128: hd * 256 + (ec + 1) * 128],
                                                        pt[:, mc, 0:cn]) for mc in range(2)]),
                         reads=[("VS", slot), ptk], writes=[("ps", b)])
                    P.op("dve", lambda e, b=b, ec=ec, rd=rd, hd=hd, c0=c0, cn=cn: e.tensor_tensor(
                        out=hT[:, 2 * hd + ec, c0:c0 + cn], in0=PS[b][:, 0:cn], in1=rd, op=ALU.mult),
                        reads=[rk], writes=[("ps", b), ("hT", 2 * hd + ec)])

            for u, (s, hd) in enumerate(units):
                stage_a(u, s, hd)
                if u >= 1:
                    stage_b(u - 1, *units[u - 1])
            stage_b(len(units) - 1, *units[-1])
            dense_1024(tg, "w_o", l, hT, "hT", lambda b, c: evac_y(tg, b, c, l, G_ATTNPOST))
            post_norm(tg, l, G_ATTNPOST)

        def ffn(tg, l, kind, first):
            T, NS, SL = tg.T, tg.NS, tg.SL
            act = AV(0, GCH * T, BF16).rearrange("p (j t) -> p j t", j=GCH)
            U = [AV(5632 + i * 516, NS * (2 + SL)).rearrange("p (s e) -> p s e", s=NS) for i in range(4)]
            tb = [AV(7700 + i * 512, T) for i in range(4)]
            sg = [AV(9760 + i * 512, T) for i in range(4)]
            uc = [0]

            def v3(ap2d):
                return ap2d.rearrange("p (s l) -> p s l", s=NS)

            pre_norm(tg, l, G_FFNPRE)
            arena_phase()

            def conv_chunk(b, c):
                i = uc[0]
                uc[0] = (i + 1) % 4
                if kind == "p":
                    hin = ffnh_p[l][:, c * 2:c * 2 + 2]
                    hk = ("ffnh", l, c)
                    P.op("pool", lambda e: e.tensor_copy(out=U[i][:, 0, 0:2], in_=hin), reads=[hk, ("U", i)], writes=[("Uh", i)])
                else:
                    hin = ffnh_s[l][:, c // 16, (c % 16) * 8:(c % 16) * 8 + 8].rearrange("p (s r) -> p s r", s=4)
                    hk = ("ffnhs", l, c // 16)
                    P.op("pool", lambda e: e.tensor_copy(out=U[i][:, :, 0:2], in_=hin), reads=[hk, ("U", i)], writes=[("Uh", i)])
                P.op("act", lambda e: e.activation(out=U[i][:, :, 2:2 + SL], in_=v3(PS[b][:, 0:T]), func=AF.Copy),
                     reads=[("Uh", i)], writes=[("ps", b), ("U", i)])
                if c % 2 == 0:
                    P.op("act", lambda e: e.activation(out=tb[i], in_=PS[b][:, 0:T], func=AF.Identity,
                                                       scale=cc(l, FFNW + 2 * FCH + c), bias=cc(l, FFNB + c)),
                         reads=["cst"], writes=[("ps", b), ("tb", i)])
                else:
                    P.op("dve", lambda e: e.tensor_scalar(out=v3(tb[i]), in0=U[i][:, :, 2:2 + SL],
                                                          scalar1=cc(l, FFNW + 2 * FCH + c), scalar2=cc(l, FFNB + c),
                                                          op0=ALU.mult, op1=ALU.add),
                         reads=["cst", ("U", i)], writes=[("tb", i)])
                P.op("dve", lambda e: e.scalar_tensor_tensor(out=v3(tb[i]), in0=U[i][:, :, 1:1 + SL],
                                                              scalar=cc(l, FFNW + 1 * FCH + c), in1=v3(tb[i]),
                                                              op0=ALU.mult, op1=ALU.add),
                     reads=[("U", i), ("Uh", i), ("tb", i), "cst"], writes=[("tb", i)])
                P.op("dve", lambda e: e.scalar_tensor_tensor(out=v3(tb[i]), in0=U[i][:, :, 0:SL],
                                                              scalar=cc(l, FFNW + 0 * FCH + c), in1=v3(tb[i]),
                                                              op0=ALU.mult, op1=ALU.add),
                     reads=[("U", i), ("Uh", i), ("tb", i), "cst"], writes=[("tb", i)])
                if kind == "p":
                    P.op("pool", lambda e: e.tensor_copy(out=ffnh_p[l][:, c * 2:c * 2 + 2], in_=U[i][:, 0, SL:SL + 2]),
                         reads=[("U", i)], writes=[hk])
                else:
                    P.op("pool", lambda e: e.tensor_copy(
                        out=ffnh_s[l][:, c // 16, (c % 16) * 8:(c % 16) * 8 + 8].rearrange("p (s r) -> p s r", s=4),
                        in_=U[i][:, :, SL:SL + 2]),
                        reads=[("U", i)], writes=[hk])
                return i

            pending = []
            DELAY = 2

            def flush(n):
                while len(pending) > n:
                    pending.pop(0)()

            def gate_chunk(b, j, gi):
                i = conv_chunk(b, j)
                pending.append(lambda: P.op("act", lambda e: e.activation(out=sg[gi], in_=tb[i], func=AF.Silu),
                                            reads=[("tb", i)], writes=[("sg", gi)]))
                flush(DELAY)

            def val_chunk(b, j, gi):
                i = conv_chunk(b, GCH + j)
                pending.append(lambda: P.op("pool", lambda e: e.tensor_tensor(out=act[:, j, 0:T], in0=tb[i], in1=sg[gi],
                                                                              op=ALU.mult),
                                            reads=[("tb", i), ("sg", gi)], writes=[("act", j)]))
                flush(DELAY)

            def up_block(runs, chunks, first_blk=False):
                s = load_block("w_up", l, 0, 8, runs)
                if first_blk:
                    banks = [nb() for _ in range(4)]
                    mm_block_kc_outer(s, banks, T, lambda kc: hT[:, kc, 0:T], lambda kc: ("hT", kc))
                for mo, (kd, j, gi) in enumerate(chunks):
                    if first_blk:
                        b = banks[mo]
                    else:
                        b = nb()
                        P.op("pe", mm_fn(PS[b][:, 0:T], [(wr[s][:, kc, mo * 128:(mo + 1) * 128], hT[:, kc, 0:T])
                                                           for kc in range(NCH)]),
                             reads=[("wr", s)] + [("hT", kc) for kc in range(NCH)], writes=[("ps", b)])
                    if kd == "g":
                        gate_chunk(b, j, gi)
                    else:
                        val_chunk(b, j, gi)

            for q in range(5):
                up_block([(q * 512, 512)], [("g", q * 4 + i, i) for i in range(4)], first_blk=(q == 0))
                up_block([(DFF + q * 512, 512)], [("v", q * 4 + i, i) for i in range(4)])
            up_block([(20 * 128, 256), (DFF + 20 * 128, 256)], [("g", 20, 0), ("g", 21, 1), ("v", 20, 0), ("v", 21, 1)])
            flush(0)
            P.op("act", lambda e: e.activation(out=scratch[:, 3:4], in_=scratch[:, 2:3], func=AF.Ln),
                 reads=["scr2"], writes=["scr3"])
            for half in range(2):
                banks = [nb() for _ in range(4)]
                kgroups = [(0, 8), (8, 8), (16, 6)]
                for gi, (k0, kn) in enumerate(kgroups):
                    s = load_block("w_down", l, k0, kn, [(half * 512, 512)])
                    if half == 0:
                        mm_block_kc_outer(s, banks, T, lambda kc: act[:, kc, 0:T], lambda kc: ("act", kc), kn=kn,
                                          start=(gi == 0), stop=(gi == 2), k0=k0)
                        continue
                    for mo in range(4):
                        b = banks[mo]
                        P.op("pe", mm_fn(PS[b][:, 0:T], [(wr[s][:, kk, mo * 128:(mo + 1) * 128], act[:, k0 + kk, 0:T])
                                                           for kk in range(kn)], start=(gi == 0), stop=(gi == 2)),
                             reads=[("wr", s)] + [("act", k0 + kk) for kk in range(kn)], writes=[("ps", b)])
                for mo in range(4):
                    evac_y(tg, banks[mo], half * 4 + mo, l, G_FFNPOST)
            post_norm(tg, l, G_FFNPOST)

        def store_states(l, kind, b_idx):
            def tr_store(src_ap, key, nvalid, dst):
                b = nb()
                P.op("pe", tr_fn([(PS[b][:, 0:128], src_ap)]), reads=[key, "ident"], writes=[("ps", b)])
                r = next_row()
                P.op("act", lambda e: e.activation(out=rowt[r][:, 0:128], in_=PS[b][:, 0:128], func=AF.Copy),
                     writes=[("ps", b), ("rowt", r)])
                P.op("act", lambda e: e.dma_start(out=dst, in_=rowt[r][0:nvalid, 0:128]), reads=[("rowt", r)],
                     dsem=("rowst", r))
                P.op("pool", lambda e: e.memset(rowt[r][:, 0:128], 0.0), reads=[], writes=[("rowt", r)])
            if kind == "p":
                tr_store(poolh_p[l][:, :], ("poolh", l), 60,
                         poolp[l, b_idx].rearrange("r (g p) -> g r p", p=128))
                tr_store(convh_p[l][:, :], ("convh", l), 8,
                         convp[l, b_idx].rearrange("r (j p) -> j r p", p=128))
                b = nb()
                P.op("pe", tr_fn([(PS[b][:, 0:128], ffnh_p[l][:, :])]),
                     reads=[("ffnh", l, c) for c in range(FCH)] + ["ident"], writes=[("ps", b)])
                r = next_row()
                P.op("act", lambda e: e.activation(out=rowt[r][:, 0:128], in_=PS[b][:, 0:128], func=AF.Copy),
                     writes=[("ps", b), ("rowt", r)])
                P.op("act", lambda e: e.dma_start(out=ffnp[l, b_idx].rearrange("r (c p) -> c r p", p=128),
                                                  in_=rowt[r][0:88, 0:128]), reads=[("rowt", r)], dsem=("rowst", r))
                P.op("pool", lambda e: e.memset(rowt[r][:, 0:128], 0.0), reads=[], writes=[("rowt", r)])
            else:
                for w in range(2):
                    tr_store(poolh_s[l][:, w, :], ("poolhs", l, w), 120,
                             pools[l].rearrange("s r (g p) -> g s r p", p=128)[2 * w:2 * w + 2])
                tr_store(convh_s[l][:, :], ("convhs", l), 32,
                         convs[l].rearrange("s r (j p) -> j s r p", p=128))
                for w in range(3):
                    ncl = 16 if w < 2 else 12
                    tr_store(ffnh_s[l][:, w, :], ("ffnhs", l, w), ncl * 8,
                             ffns[l].rearrange("s r (c p) -> c s r p", p=128)[16 * w:16 * w + ncl])

        def load_sample_ffn_state(l):
            for grp in range(11):
                r = next_row()
                P.op("sp", lambda e, r=r, grp=grp: e.dma_start(
                    out=rowt[r][0:8, :], in_=sffn[l].rearrange("s r d -> (s r) d")[:, grp * 512:(grp + 1) * 512]),
                    reads=[("rowt", r)], writes=[("rowt", r)], dsem=("rowt", r))
                b = nb()
                P.op("pe", tr_fn([(PS[b][:, i * 128:(i + 1) * 128], rowt[r][:, i * 128:(i + 1) * 128]) for i in range(4)]),
                     reads=[("rowt", r), "ident"], writes=[("ps", b)])
                c0 = grp * 4
                w = c0 // 16
                cl = c0 % 16
                P.op("dve", lambda e, b=b, w=w, cl=cl: e.tensor_copy(
                    out=ffnh_s[l][:, w, cl * 8:(cl + 4) * 8].rearrange("p (c q) -> p c q", c=4),
                    in_=PS[b][:, :].rearrange("p (c q) -> p c q", q=128)[:, :, 0:8]),
                    writes=[("ps", b), ("ffnhs", l, w)])

        def kv_prep(bi):
            mstage = AV(0, 2 * D).rearrange("p (m d) -> p m d", m=2)
            mT = AV(2048, NCH * NMEM).rearrange("p (c m) -> p c m", c=NCH)
            mh = AV(5120, L * NCH * NMEM, BF16).rearrange("p (l c m) -> p l c m", l=L, c=NCH)
            rm = AV(7168, NMEM)
            stg = [AV(7424 + i * 512, 512) for i in range(4)]
            sc = [0]
            arena_phase()
            P.op("sp", lambda e: e.dma_start(out=mstage, in_=memp[bi].rearrange("(m p) d -> p m d", p=128)),
                 writes=["mstage"], dsem="mst")
            for c2 in range(4):
                b = nb()
                items = []
                for i in range(2):
                    c = c2 * 2 + i
                    for mc in range(2):
                        items.append((PS[b][:, i * 256 + mc * 128: i * 256 + (mc + 1) * 128],
                                      mstage[:, mc, c * 128:(c + 1) * 128]))
                P.op("pe", tr_fn(items), reads=["mstage", "ident"], writes=[("ps", b)])
                copy_op(ev_eng(), mT[:, c2 * 2:c2 * 2 + 2, :], PS[b][:, :].rearrange("p (i m) -> p i m", i=2),
                        [], [("ps", b), ("mT", c2)])
            P.op("act", lambda e: e.activation(out=sq[:, :, 0:NMEM], in_=mT, func=AF.Square),
                 reads=[("mT", i) for i in range(4)], writes=[("sq", c) for c in range(NCH)])
            b = nb()
            P.op("pe", mm_fn(PS[b][:, 0:NMEM], [(onesb[:], sq[:, c, 0:NMEM]) for c in range(NCH)]),
                 reads=[("sq", c) for c in range(NCH)] + ["onesb"], writes=[("ps", b)])
            P.op("act", lambda e, b=b: e.activation(out=rm, in_=PS[b][:, 0:NMEM], func=AF.Ln, bias=cst[:, EPSC:EPSC + 1], scale=1.0),
                 reads=["cst"], writes=[("ps", b), "rm"])
            P.op("act", lambda e: e.activation(out=rm, in_=rm, func=AF.Exp, scale=-0.5), reads=["rm"], writes=["rm"])
            for l in range(NL):
                for c in range(NCH):
                    P.op("dve", lambda e, l=l, c=c: e.scalar_tensor_tensor(out=mh[:, l, c, :], in0=mT[:, c, :],
                                                                           scalar=cc(l, G_MEM + c), in1=rm,
                                                                           op0=ALU.mult, op1=ALU.mult),
                         reads=[("mT", c // 2), "rm", "cst"], writes=[("mh", l, c)])
            for l in range(NL):
                mhk = [("mh", l, c) for c in range(NCH)]
                for name, dst in (("w_k", mk), ("w_v", mv)):
                    for blk in range(2):
                        s = load_block(name, l, 0, 8, [(blk * 512, 512)])
                        if name == "w_k":
                            for mo in range(4):
                                b = nb()
                                P.op("pe", mm_fn(PS[b][:, 0:NMEM], [(wr[s][:, kc, mo * 128:(mo + 1) * 128], mh[:, l, kc, :])
                                                                     for kc in range(NCH)]),
                                     reads=[("wr", s)] + mhk, writes=[("ps", b)])
                                copy_op(ev_eng(), KT[l][:, blk * 4 + mo, :], PS[b][:, 0:NMEM], [], [("ps", b), ("KT", l)])
                        for mc in range(2):
                            b = nb()
                            P.op("pe", mm_fn(PS[b][:, :], [(mh[:, l, kc, mc * 128:(mc + 1) * 128], wr[s][:, kc, :])
                                                             for kc in range(NCH)]),
                                 reads=[("wr", s)] + mhk, writes=[("ps", b)])
                            si = sc[0]
                            sc[0] = (si + 1) % 4
                            P.op("act", lambda e, b=b, si=si: e.activation(out=stg[si], in_=PS[b][:, :], func=AF.Copy),
                                 writes=[("ps", b), ("stg", si)])
                            if name == "w_v":
                                P.op("pool", lambda e, si=si, l=l, mc=mc, blk=blk: e.tensor_copy(
                                    out=VS[l][:, mc, blk * 512:(blk + 1) * 512], in_=stg[si]),
                                    reads=[("stg", si)], writes=[("VS", l)])
                            P.op("act", lambda e, si=si, l=l, mc=mc, blk=blk, dst=dst: e.dma_start(
                                out=dst[l, bi, mc * 128:(mc + 1) * 128, blk * 512:(blk + 1) * 512], in_=stg[si]),
                                reads=[("stg", si)], dsem=("stg", si))

        def run_tile(kind, bi, ti, last):
            if kind == "p":
                tg = TG(512, 1, 512)
                src = xp[bi, ti * 512:(ti + 1) * 512, :].rearrange("(tb p) d -> p tb d", p=128)
                ntb = 4
            else:
                tg = TG(128, 4, 32)
                src = xs.rearrange("(tb p) d -> p tb d", p=128)
                ntb = 1
            T = tg.T
            P.op("sp", lambda e: e.dma_start(out=xstage[:, 0:ntb, :], in_=src), writes=["xstage", "xstageV"], dsem="xst")
            for c in range(NCH):
                b = nb()
                P.op("pe", tr_fn([(PS[b][:, tb * 128:(tb + 1) * 128], xstage[:, tb, c * 128:(c + 1) * 128])
                                  for tb in range(ntb)]),
                     reads=["xstage", "ident"], writes=[("ps", b)])
                copy_op(ev_eng(), xT[:, c, 0:T], PS[b][:, 0:T], [], [("ps", b), ("xT", c)])
            for l in range(NL):
                first = (kind == "p" and ti == 0)
                if kind == "s":
                    load_sample_ffn_state(l)
                mix(tg, l, kind, first)
                attn(tg, l, kind)
                ffn(tg, l, kind, first)
                if last:
                    store_states(l, kind, bi)
            ystage = yv[:, :, :].rearrange("p c t -> p (c t)").rearrange("p (tb d) -> p tb d", d=D)
            for tb in range(ntb):
                for half in range(2):
                    b = nb()
                    P.op("pe", tr_fn([(PS[b][:, k * 128:(k + 1) * 128], xT[:, half * 4 + k, tb * 128:(tb + 1) * 128])
                                      for k in range(4)]),
                         reads=[("xT", half * 4 + k) for k in range(4)] + ["ident"], writes=[("ps", b)])
                    copy_op(ev_eng(), ystage[:, tb, half * 512:(half + 1) * 512], PS[b][:, :], [],
                            [("ps", b), ("ystg", tb, half)] + ([("yv", c) for c in range(NCH)] if (tb == 0 and half == 0) else []))
                if kind == "p":
                    dst = yp[bi, ti * 512 + tb * 128: ti * 512 + (tb + 1) * 128, :]
                else:
                    dst = ys[:, :]
                P.op("act", lambda e, tb=tb, dst=dst: e.dma_start(out=dst, in_=ystage[:, tb, :]),
                     reads=[("ystg", tb, 0), ("ystg", tb, 1)] + [("yv", c) for c in range(NCH)], dsem=("yst", tb))

        if do_sample:
            for l in range(NL):
                P.op("pool", lambda e, l=l: e.memset(poolh_s[l][:], 0.0), writes=[("poolhs", l, 0), ("poolhs", l, 1)])
                P.op("pool", lambda e, l=l: e.memset(convh_s[l][:], 0.0), writes=[("convhs", l)])
                P.op("pool", lambda e, l=l: e.memset(ffnh_s[l][:], 0.0), writes=[("ffnhs", l, w) for w in range(3)])
            run_tile("s", 0, 0, True)
            assert fp_cur[0] == len(fp_seq) and fp_emitted[0] == len(fp_seq)
            first_pass[0] = False
        for bi in range(NPB):
            kv_prep(bi)
            for l in range(NL):
                P.op("pool", lambda e, l=l: e.memset(poolh_p[l][:], 0.0), writes=[("poolh", l)])
                P.op("pool", lambda e, l=l: e.memset(convh_p[l][:], 0.0), writes=[("convh", l)])
                P.op("pool", lambda e, l=l: e.memset(ffnh_p[l][:], 0.0), writes=[("ffnh", l, c) for c in range(FCH)])
            for ti in range(NT):
                run_tile("p", bi, ti, ti == NT - 1)
        P.emit(nc)
    return nc, P


def _pack_consts(inp):
    cst = np.zeros((128, NCONST), np.float32)

    def put(col, vec):
        n = vec.shape[0] // 128
        cst[:, col:col + n] = vec.reshape(n, 128).T

    for l in range(L):
        base = l * CL
        put(base + G_MIXPRE, inp["g_mix_pre"][l])
        put(base + G_MIXPOST, inp["g_mix_post"][l])
        put(base + G_ATTNPRE, inp["g_attn_pre"][l])
        put(base + G_ATTNPOST, inp["g_attn_post"][l])
        put(base + G_MEM, inp["g_mem"][l])
        put(base + G_FFNPRE, inp["g_ffn_pre"][l])
        put(base + G_FFNPOST, inp["g_ffn_post"][l])
        put(base + POOLSC, inp["pool_scale"][l])
        for k in range(3):
            put(base + CONVW + k * 4, inp["conv_w"][l, k])
            put(base + FFNW + k * FCH, inp["ffn_conv_w"][l, k])
        put(base + CONVB, inp["conv_b"][l])
        put(base + FFNB, inp["ffn_conv_b"][l])
    cst[:, INVCNT:INVCNT + 15] = (1.0 / np.arange(1, 16, dtype=np.float64)).astype(np.float32)[None, :]
    cst[:, EPSC] = EPS
    return cst


_CACHE = {}


def kernel(**inputs):
    inp = {k: np.asarray(v) for k, v in inputs.items()}
    if "nc" not in _CACHE:
        _CACHE["nc"] = build_program()[0]
    nc = _CACHE["nc"]
    cst = _pack_consts(inp)
    shared = {
        "cst": cst,
        "w_in": np.ascontiguousarray(inp["w_in"]),
        "w_pool": np.ascontiguousarray(inp["w_pool"]),
        "w_out": np.ascontiguousarray(inp["w_out"]),
        "w_q": np.ascontiguousarray(inp["w_q"].reshape(L, D, D)),
        "w_k": np.ascontiguousarray(inp["w_k"].reshape(L, D, D)),
        "w_v": np.ascontiguousarray(inp["w_v"].reshape(L, D, D)),
        "w_o": np.ascontiguousarray(inp["w_o"].reshape(L, D, D)),
        "w_up": np.ascontiguousarray(inp["w_up"]),
        "w_down": np.ascontiguousarray(inp["w_down"]),
    }
    in_maps = []
    for i in range(NCORES):
        m = dict(shared)
        m["xp"] = np.ascontiguousarray(inp["x_prompt"][PB * i:PB * (i + 1)])
        m["xs"] = np.ascontiguousarray(inp["x_sample"][SB * i:SB * (i + 1)].reshape(SB * SSEQ, D))
        m["memp"] = np.ascontiguousarray(inp["mem_prompt"][PB * i:PB * (i + 1)])
        m["ck"] = np.ascontiguousarray(inp["cache_mem_k"][:, SB * i:SB * (i + 1)].reshape(L, SB, NMEM, D))
        m["cv"] = np.ascontiguousarray(inp["cache_mem_v"][:, SB * i:SB * (i + 1)].reshape(L, SB, NMEM, D))
        m["spool"] = np.ascontiguousarray(inp["state_pool"][:, SB * i:SB * (i + 1)])
        m["sconv"] = np.ascontiguousarray(inp["state_conv"][:, SB * i:SB * (i + 1)])
        m["sffn"] = np.ascontiguousarray(inp["state_ffn_conv"][:, SB * i:SB * (i + 1)])
        in_maps.append(m)
    res = run_bass_kernel_spmd(nc, in_maps, core_ids=list(range(NCORES)))
    R = res.results

    def cat(name, axis):
        return np.concatenate([np.asarray(r[name]) for r in R], axis=axis)

    y_prompt = cat("yp", 0)
    y_sample = cat("ys", 0).reshape(NCORES * SB, SSEQ, D)
    mem_k = cat("mk", 1).reshape(L, NCORES * PB, NMEM, 4, 256)
    mem_v = cat("mv", 1).reshape(L, NCORES * PB, NMEM, 4, 256)
    pool_p = cat("poolp", 1)
    conv_p = cat("convp", 1)
    ffn_p = cat("ffnp", 1)
    pool_s = cat("pools", 1)
    conv_s = cat("convs", 1)
    ffn_s = cat("ffns", 1)
    return (y_prompt, y_sample, mem_k, mem_v, pool_p, conv_p, ffn_p, pool_s, conv_s, ffn_s)
```
